# Optimizing a Trainium2 kernel written in Bass

```python
import math
import jax, jax.numpy as jnp
from jax import lax
import numpy as np

D_MODEL = 1024
BATCH = 32
SEQ = 256
DEPTH = 2
DEC_BATCH = 4
DEC_SEQ = 4096
PAST_LEN = 256

GRID_W = 64
HEAD_DIM = 64
A_HEADS = 8
A_KV_HEADS = 2
A_GROUP = A_HEADS // A_KV_HEADS
B_HEADS = 4
WINDOW = 128
BLOCK = 128
N_BAND = -(-WINDOW // BLOCK)
ROPE_BASE = 10000.0
ATT_SIZES = [A_HEADS * HEAD_DIM, A_KV_HEADS * HEAD_DIM, A_KV_HEADS * HEAD_DIM,
             B_HEADS * 2 * HEAD_DIM, B_HEADS * 2 * HEAD_DIM, B_HEADS * 2 * HEAD_DIM]
ATT_IN = sum(ATT_SIZES)
ATT_OUT = A_HEADS * HEAD_DIM + B_HEADS * 2 * HEAD_DIM
D_RNN = 1280
RNN_BLOCKS = 10
RNN_BW = D_RNN // RNN_BLOCKS
CONV_W = 4
CONV_LEFT = (CONV_W - 1) // 2
RGLRU_C = 8.0
D_FF = 4 * D_MODEL
N_ATT = (DEPTH + 1) // 2
N_REC = DEPTH // 2
EPS = 1e-6
SCALE = HEAD_DIM ** -0.5
NEG = -1e30

kernel_name = 'hybrid_diffusion_prefix_step'

F32 = jnp.float32


def rmsnorm(x, g):
    xf = x.astype(F32)
    y = xf * lax.rsqrt(jnp.mean(xf * xf, axis=-1, keepdims=True) + EPS)
    return (y * g.astype(F32)).astype(x.dtype)


def ada_mod(cvec, w, b):
    m = jax.nn.silu(cvec) @ w + b
    return jnp.split(m[:, None, :], 6, axis=-1)


def modulate(h, shift, scale):
    return h * (1.0 + scale) + shift


def axial_rope_tables(n_tokens):
    rows = n_tokens // GRID_W
    r, cl = jnp.meshgrid(jnp.arange(rows, dtype=F32), jnp.arange(GRID_W, dtype=F32), indexing='ij')
    quarter = HEAD_DIM // 4
    inv = ROPE_BASE ** (-jnp.arange(quarter, dtype=F32) / quarter)
    ang = jnp.stack([r.reshape(-1)[:, None] * inv, cl.reshape(-1)[:, None] * inv], axis=1)
    return jnp.cos(ang)[:, None], jnp.sin(ang)[:, None]


def apply_rope(x, cos, sin):
    B, T, H, _ = x.shape
    xf = x.astype(F32).reshape(B, T, H, 2, 2, HEAD_DIM // 4)
    x1, x2 = xf[..., 0, :], xf[..., 1, :]
    out = jnp.stack([x1 * cos - x2 * sin, x1 * sin + x2 * cos], axis=-2)
    return out.reshape(B, T, H, HEAD_DIM).astype(x.dtype)


def rope_pairs(x, cos, sin):
    B, T, H, E = x.shape
    return apply_rope(x.reshape(B, T, 2 * H, HEAD_DIM), cos, sin).reshape(B, T, H, E)


def att_project(h, w_in):
    B, T, _ = h.shape
    idx = np.cumsum(ATT_SIZES)[:-1].tolist()
    qa, ka, va, qb, kb, vb = jnp.split(h @ w_in, idx, axis=-1)
    return (qa.reshape(B, T, A_HEADS, HEAD_DIM), ka.reshape(B, T, A_KV_HEADS, HEAD_DIM),
            va.reshape(B, T, A_KV_HEADS, HEAD_DIM), qb.reshape(B, T, B_HEADS, 2 * HEAD_DIM),
            kb.reshape(B, T, B_HEADS, 2 * HEAD_DIM), vb.reshape(B, T, B_HEADS, 2 * HEAD_DIM))


def sink_attention_dense(q, k, v, sink):
    B, Q, H, d = q.shape
    nq = Q // BLOCK
    qb = jnp.moveaxis(q.reshape(B, nq, BLOCK, A_KV_HEADS, A_GROUP, d), 1, 0)
    sink_l = sink.reshape(A_KV_HEADS, A_GROUP).astype(F32)

    def one(qi):
        s = jnp.einsum('bqkgd,bskd->bkgqs', qi, k).astype(F32) * SCALE
        sk = jnp.broadcast_to(sink_l[None, :, :, None, None], s.shape[:-1] + (1,))
        p = jax.nn.softmax(jnp.concatenate([s, sk], axis=-1), axis=-1)[..., :-1]
        return jnp.einsum('bkgqs,bskd->bqkgd', p.astype(v.dtype), v)

    o = lax.map(one, qb)
    return jnp.moveaxis(o, 0, 1).reshape(B, Q, H, d)


def banded_sink_attention(q, k, v, kc, vc, sink):
    B, T, H, d = q.shape
    nb = T // BLOCK
    L = kc.shape[1]
    qb = q.reshape(B, nb, BLOCK, A_KV_HEADS, A_GROUP, d)

    def band(t):
        tp = jnp.pad(t, ((0, 0), (N_BAND * BLOCK, N_BAND * BLOCK), (0, 0), (0, 0)))
        tp = tp.reshape(B, nb + 2 * N_BAND, BLOCK, A_KV_HEADS, d)
        return jnp.concatenate([tp[:, j:j + nb] for j in range(2 * N_BAND + 1)], axis=2)

    kb, vb = band(k), band(v)
    S = (2 * N_BAND + 1) * BLOCK
    start = jnp.arange(nb)[:, None] * BLOCK
    qpos = start + jnp.arange(BLOCK)[None]
    kpos = start + jnp.arange(S)[None] - N_BAND * BLOCK
    mask = ((jnp.abs(qpos[:, :, None] - kpos[:, None, :]) <= WINDOW)
            & (kpos[:, None, :] >= 0) & (kpos[:, None, :] < T))
    s_band = jnp.einsum('bnqkgd,bnskd->bnkgqs', qb, kb).astype(F32) * SCALE
    s_band = jnp.where(mask[None, :, None, None], s_band, NEG)
    s_ctx = jnp.einsum('bnqkgd,bskd->bnkgqs', qb, kc).astype(F32) * SCALE
    sk = jnp.broadcast_to(sink.reshape(A_KV_HEADS, A_GROUP).astype(F32)[None, None, :, :, None, None],
                          s_ctx.shape[:-1] + (1,))
    p = jax.nn.softmax(jnp.concatenate([s_ctx, s_band, sk], axis=-1), axis=-1)
    p_ctx = p[..., :L].astype(v.dtype)
    p_band = p[..., L:L + S].astype(v.dtype)
    o = (jnp.einsum('bnkgqs,bskd->bnqkgd', p_ctx, vc)
         + jnp.einsum('bnkgqs,bnskd->bnqkgd', p_band, vb))
    return o.reshape(B, T, H, d)


def diff_lambda(lam_qk, lambda_init):
    lq = lam_qk.astype(F32)
    return jnp.exp(jnp.sum(lq[0] * lq[1])) - jnp.exp(jnp.sum(lq[2] * lq[3])) + lambda_init


def diff_attention(q, k, v, lam):
    B, Q, H, E = q.shape
    d = E // 2
    k1, k2 = k[..., :d], k[..., d:]
    nq = Q // BLOCK
    qb = jnp.moveaxis(q.reshape(B, nq, BLOCK, H, E), 1, 0)

    def one(qi):
        s1 = jnp.einsum('bqhd,bshd->bhqs', qi[..., :d], k1).astype(F32) * SCALE
        s2 = jnp.einsum('bqhd,bshd->bhqs', qi[..., d:], k2).astype(F32) * SCALE
        w = jax.nn.softmax(s1, axis=-1) - lam * jax.nn.softmax(s2, axis=-1)
        return jnp.einsum('bhqs,bshe->bqhe', w.astype(v.dtype), v)

    o = lax.map(one, qb)
    return jnp.moveaxis(o, 0, 1).reshape(B, Q, H, E)


def att_merge(oa, ob, subln, lambda_init, w_out):
    B, T = oa.shape[:2]
    ob = rmsnorm(ob, subln) * (1.0 - lambda_init)
    return jnp.concatenate([oa.reshape(B, T, -1), ob.reshape(B, T, -1)], axis=-1) @ w_out


def att_mixer_context(h, w_in, w_out, sink, lam_qk, subln, lambda_init):
    qa, ka, va, qb, kb, vb = att_project(h, w_in)
    oa = sink_attention_dense(qa, ka, va, sink)
    ob = diff_attention(qb, kb, vb, diff_lambda(lam_qk, lambda_init))
    return att_merge(oa, ob, subln, lambda_init, w_out), (ka, va, kb, vb)


def att_mixer_latent(h, ck_a, cv_a, ck_b, cv_b, w_in, w_out, sink, lam_qk, subln, lambda_init):
    cos, sin = axial_rope_tables(h.shape[1])
    qa, ka, va, qb, kb, vb = att_project(h, w_in)
    qa, ka = apply_rope(qa, cos, sin), apply_rope(ka, cos, sin)
    qb, kb = rope_pairs(qb, cos, sin), rope_pairs(kb, cos, sin)
    oa = banded_sink_attention(qa, ka, va, ck_a.astype(ka.dtype), cv_a.astype(va.dtype), sink)
    k_all = jnp.concatenate([ck_b.astype(kb.dtype), kb], axis=1)
    v_all = jnp.concatenate([cv_b.astype(vb.dtype), vb], axis=1)
    ob = diff_attention(qb, k_all, v_all, diff_lambda(lam_qk, lambda_init))
    return att_merge(oa, ob, subln, lambda_init, w_out)


def centred_dwconv(x, w, b):
    T = x.shape[1]
    xp = jnp.pad(x, ((0, 0), (CONV_LEFT, CONV_W - 1 - CONV_LEFT), (0, 0)))
    y = b + xp[:, 0:T] * w[0]
    for j in range(1, CONV_W):
        y = y + xp[:, j:j + T] * w[j]
    return y


def block_diag(x, w, b):
    B, T, _ = x.shape
    return jnp.einsum('btnc,ncd->btnd', x.reshape(B, T, RNN_BLOCKS, RNN_BW), w).reshape(B, T, D_RNN) + b


def rglru_scan(x, w_a, b_a, w_x, b_x, lam, h0, reverse):
    r = jax.nn.sigmoid(block_diag(x, w_a, b_a).astype(F32))
    i = jax.nn.sigmoid(block_diag(x, w_x, b_x).astype(F32))
    log_a = -RGLRU_C * r * jax.nn.softplus(-lam.astype(F32))
    a = jnp.exp(log_a)
    u = jnp.sqrt(-jnp.expm1(2.0 * log_a)) * (i * x.astype(F32))

    def step(hc, au):
        hc = au[0] * hc + au[1]
        return hc, hc

    hT, hs = lax.scan(step, h0.astype(F32), (jnp.moveaxis(a, 1, 0), jnp.moveaxis(u, 1, 0)), reverse=reverse)
    return jnp.moveaxis(hs, 0, 1), hT


def rec_mixer(h, h0_f, h0_b, w_in, conv_w, conv_b, w_a, b_a, w_x, b_x, lam, w_out):
    gate, xr = jnp.split(h @ w_in, 2, axis=-1)
    xr = centred_dwconv(xr, conv_w, conv_b)
    hf, sf = rglru_scan(xr, w_a[0], b_a[0], w_x[0], b_x[0], lam[0], h0_f, False)
    hb, sb = rglru_scan(xr, w_a[1], b_a[1], w_x[1], b_x[1], lam[1], h0_b, True)
    y = (hf + hb).astype(h.dtype) * jax.nn.gelu(gate)
    return y @ w_out, sf, sb


def sq_relu_mlp(h, w1, w2):
    return jnp.square(jax.nn.relu(h @ w1)) @ w2


def setup_inputs(seed: int = 0) -> dict:
    key = jax.random.key(seed)
    ks = jax.random.split(key, 32)

    def nrm(k, shape, scale=1.0):
        return jax.random.normal(k, shape, F32) * scale

    a0 = jax.random.uniform(ks[29], (N_REC, 2, D_RNN), F32, minval=0.9, maxval=0.999)
    a1 = a0 ** (1.0 / RGLRU_C)
    return {
        'x_prompt': nrm(ks[0], (BATCH, SEQ, D_MODEL)),
        'x_sample': nrm(ks[1], (DEC_BATCH, DEC_SEQ, D_MODEL)),
        'cache_a_k': nrm(ks[2], (DEC_BATCH, N_ATT, PAST_LEN, A_KV_HEADS, HEAD_DIM)),
        'cache_a_v': nrm(ks[3], (DEC_BATCH, N_ATT, PAST_LEN, A_KV_HEADS, HEAD_DIM)),
        'cache_b_k': nrm(ks[4], (DEC_BATCH, N_ATT, PAST_LEN, B_HEADS, 2 * HEAD_DIM)),
        'cache_b_v': nrm(ks[5], (DEC_BATCH, N_ATT, PAST_LEN, B_HEADS, 2 * HEAD_DIM)),
        'state_fwd': nrm(ks[6], (DEC_BATCH, N_REC, D_RNN), 0.5),
        'state_bwd': nrm(ks[7], (DEC_BATCH, N_REC, D_RNN), 0.5),
        'c': nrm(ks[8], (DEC_BATCH, D_MODEL)),
        'c_ctx': nrm(ks[9], (D_MODEL,)),
        'norm1': 1.0 + nrm(ks[10], (DEPTH, D_MODEL), 0.02),
        'norm2': 1.0 + nrm(ks[11], (DEPTH, D_MODEL), 0.02),
        'w_ada': nrm(ks[12], (DEPTH, D_MODEL, 6 * D_MODEL), 0.5 * D_MODEL ** -0.5),
        'b_ada': nrm(ks[13], (DEPTH, 6 * D_MODEL), 0.02),
        'w_mlp1': nrm(ks[14], (DEPTH, D_MODEL, D_FF), D_MODEL ** -0.5),
        'w_mlp2': nrm(ks[15], (DEPTH, D_FF, D_MODEL), D_FF ** -0.5),
        'att_w_in': nrm(ks[16], (N_ATT, D_MODEL, ATT_IN), D_MODEL ** -0.5),
        'att_w_out': nrm(ks[17], (N_ATT, ATT_OUT, D_MODEL), ATT_OUT ** -0.5),
        'att_sink': nrm(ks[18], (N_ATT, A_HEADS), 0.5),
        'att_lam_qk': nrm(ks[19], (N_ATT, 4, HEAD_DIM), 0.1),
        'att_subln': 1.0 + nrm(ks[20], (N_ATT, 2 * HEAD_DIM), 0.02),
        'rec_w_in': nrm(ks[21], (N_REC, D_MODEL, 2 * D_RNN), D_MODEL ** -0.5),
        'rec_conv_w': nrm(ks[22], (N_REC, CONV_W, D_RNN), CONV_W ** -0.5),
        'rec_conv_b': nrm(ks[23], (N_REC, D_RNN), 0.02),
        'rec_w_a': nrm(ks[24], (N_REC, 2, RNN_BLOCKS, RNN_BW, RNN_BW), RNN_BW ** -0.5),
        'rec_b_a': nrm(ks[25], (N_REC, 2, D_RNN), 0.02),
        'rec_w_x': nrm(ks[26], (N_REC, 2, RNN_BLOCKS, RNN_BW, RNN_BW), RNN_BW ** -0.5),
        'rec_b_x': nrm(ks[27], (N_REC, 2, D_RNN), 0.02),
        'rec_lam': jnp.log(a1) - jnp.log1p(-a1),
        'rec_w_out': nrm(ks[28], (N_REC, D_RNN, D_MODEL), D_RNN ** -0.5),
        'final_norm': 1.0 + nrm(ks[30], (D_MODEL,), 0.02),
    }


def reference(x_prompt, x_sample, cache_a_k, cache_a_v, cache_b_k, cache_b_v, state_fwd, state_bwd,
              c, c_ctx, norm1, norm2, w_ada, b_ada, w_mlp1, w_mlp2, att_w_in, att_w_out, att_sink,
              att_lam_qk, att_subln, rec_w_in, rec_conv_w, rec_conv_b, rec_w_a, rec_b_a, rec_w_x,
              rec_b_x, rec_lam, rec_w_out, final_norm):
    xp, xs = x_prompt, x_sample
    new_ak, new_av, new_bk, new_bv, new_sf, new_sb = [], [], [], [], [], []
    for layer in range(DEPTH):
        j = layer // 2
        mp = ada_mod(c_ctx[None], w_ada[layer], b_ada[layer])
        ms = ada_mod(c, w_ada[layer], b_ada[layer])
        hp = modulate(rmsnorm(xp, norm1[layer]), mp[0], mp[1])
        hs = modulate(rmsnorm(xs, norm1[layer]), ms[0], ms[1])
        if layer % 2 == 0:
            lam_init = 0.8 - 0.6 * math.exp(-0.3 * layer)
            op, (ka, va, kb, vb) = att_mixer_context(hp, att_w_in[j], att_w_out[j], att_sink[j],
                                                     att_lam_qk[j], att_subln[j], lam_init)
            os_ = att_mixer_latent(hs, cache_a_k[:, j], cache_a_v[:, j], cache_b_k[:, j], cache_b_v[:, j],
                                   att_w_in[j], att_w_out[j], att_sink[j], att_lam_qk[j], att_subln[j],
                                   lam_init)
            new_ak.append(ka)
            new_av.append(va)
            new_bk.append(kb)
            new_bv.append(vb)
        else:
            zeros = jnp.zeros((xp.shape[0], D_RNN), F32)
            op, sf, sb = rec_mixer(hp, zeros, zeros, rec_w_in[j], rec_conv_w[j], rec_conv_b[j], rec_w_a[j],
                                   rec_b_a[j], rec_w_x[j], rec_b_x[j], rec_lam[j], rec_w_out[j])
            os_, _, _ = rec_mixer(hs, state_fwd[:, j], state_bwd[:, j], rec_w_in[j], rec_conv_w[j],
                                  rec_conv_b[j], rec_w_a[j], rec_b_a[j], rec_w_x[j], rec_b_x[j], rec_lam[j],
                                  rec_w_out[j])
            new_sf.append(sf)
            new_sb.append(sb)
        xp = xp + mp[2] * op
        xs = xs + ms[2] * os_
        hp = modulate(rmsnorm(xp, norm2[layer]), mp[3], mp[4])
        hs = modulate(rmsnorm(xs, norm2[layer]), ms[3], ms[4])
        xp = xp + mp[5] * sq_relu_mlp(hp, w_mlp1[layer], w_mlp2[layer])
        xs = xs + ms[5] * sq_relu_mlp(hs, w_mlp1[layer], w_mlp2[layer])
    y_prompt = rmsnorm(xp, final_norm)
    y_sample = rmsnorm(xs, final_norm)
    new_a_k = jnp.stack(new_ak, axis=1)
    new_a_v = jnp.stack(new_av, axis=1)
    new_b_k = jnp.stack(new_bk, axis=1)
    new_b_v = jnp.stack(new_bv, axis=1)
    new_state_fwd = jnp.stack(new_sf, axis=1)
    new_state_bwd = jnp.stack(new_sb, axis=1)
    return (y_prompt, y_sample, new_a_k, new_a_v, new_b_k, new_b_v, new_state_fwd, new_state_bwd)
```

```python
import numpy as np
from contextlib import ExitStack
import concourse.bass as bass
import concourse.mybir as mybir
from concourse.bass_utils import run_bass_kernel_spmd

F32, BF16 = mybir.dt.float32, mybir.dt.bfloat16
AF = mybir.ActivationFunctionType
ALU = mybir.AluOpType
NDS = 6
import os
NO_CC = os.environ.get('KNO_CC') == '1'
STAGE = float(os.environ.get('KSTAGE', '99'))


class StopBuild(Exception):
    pass


DEAD = [False]


def stage_check(k):
    if STAGE < k:
        DEAD[0] = True
EPS = 1e-6
SCALE = 0.125
LAM_INIT = 0.8 - 0.6 * 1.0
TP, TS = 1024, 2048
QA, QAR, KA, KAR, QB, QBR, KB, KBR, VA, VB, WX = 0, 512, 1024, 1152, 1280, 1792, 2304, 2816, 3328, 3456, 3968


def sp_layout():
    ents = [('g1', 16), ('g2', 16), ('gf', 8), ('bada', 96), ('convw_p', 50), ('convw_s', 50), ('convb', 10),
            ('ba_p', 20), ('bx_p', 20), ('lam_p', 20), ('ba_s', 20), ('bx_s', 20), ('lam_s', 20),
            ('subln', 1), ('state_up', 10), ('sink', 8), ('lamqk', 256), ('sel', 2), ('cT', 16)]
    off, d = 0, {}
    for n, w in ents:
        d[n] = (off, w)
        off += w
    return d, off


class Dep:
    __slots__ = ('w', 'r')

    def __init__(s):
        s.w = None
        s.r = {}


class Tl:
    def __init__(s, t, name):
        s.t = t
        s.name = name
        s.subs = {}

    def __getitem__(s, idx):
        return s.t[idx]


class KB_:
    def __init__(s, nc, es):
        s.nc, s.es = nc, es
        s.eng = {}
        for n in ['pe', 'act', 'dve', 'pool', 'sp']:
            s.eng[n] = dict(ops=[], sem=es.enter_context(nc.semaphore('s_' + n)), cnt=0, known={})
        s.dq = {}
        for q in ['sp', 'pool', 'act']:
            s.dq[q] = dict(sems=[es.enter_context(nc.semaphore(f'd_{q}{i}')) for i in range(NDS)], n=0)
        s.out_toks = []
        s.tiles = []
        s.cc_sem = es.enter_context(nc.semaphore('s_cc'))
        s.cc_n = 0
        s.H = {'pe': nc.tensor, 'act': nc.scalar, 'dve': nc.vector, 'pool': nc.gpsimd, 'sp': nc.sync}
        es.enter_context(nc.Block())

    def _emit(s, eng, waits, fn, inc):
        h = s.H[eng]
        for sem, val in waits:
            h.wait_ge(sem, val)
        if fn is not None:
            ins = fn(h)
            if inc is not None:
                ins.then_inc(inc[0], inc[1])

    def tile(s, es, name, shape, dt):
        t = Tl(es.enter_context(s.nc.sbuf_tensor('t_' + name, list(shape), dt)), name)
        return t

    def _conf(s, tl, sub):
        if sub is None:
            return list(tl.subs.values())
        return [tl.subs[k] for k in (sub, None) if k in tl.subs]

    @staticmethod
    def _ks(key):
        return key if isinstance(key, tuple) else (key, None)

    def _deps(s, r, w, tok):
        need = []
        for key in r:
            tl, sub = s._ks(key)
            for d in s._conf(tl, sub):
                if d.w:
                    need.append(d.w)
                if getattr(tl, 'psum', False):
                    need.extend(t for t in d.r.values() if t[0] is not tok[0])
        for key in w:
            tl, sub = s._ks(key)
            for d in s._conf(tl, sub):
                if d.w:
                    need.append(d.w)
                need.extend(d.r.values())
        for key in r:
            tl, sub = s._ks(key)
            d = tl.subs.setdefault(sub, Dep())
            d.r[tok[0].num] = tok
        for key in w:
            tl, sub = s._ks(key)
            if sub is None:
                tl.subs = {}
            d = tl.subs.setdefault(sub, Dep())
            d.w = tok
            d.r = {}
        return need

    def _need(s, eng, toks):
        kn = s.eng[eng]['known']
        waits = []
        for sem, val in toks:
            if kn.get(sem.num, 0) < val:
                kn[sem.num] = val
                waits.append((sem, val))
        return waits

    def op(s, eng, fn, r=(), w=(), inc=True):
        if DEAD[0]:
            return None
        e = s.eng[eng]
        tok = (e['sem'], e['cnt'] + 1)
        need = s._deps(r, w, tok)
        need = [t for t in need if not (t[0] is e['sem'] and (eng == 'pe' or t[1] > e['cnt']))]
        waits = s._need(eng, need)
        s._emit(eng, waits, fn, (e['sem'], 1) if inc else None)
        if inc:
            e['cnt'] += 1
        return tok

    def dma(s, q, out, in_, r=(), w=(), is_out=False, fn=None, **kw):
        if DEAD[0]:
            return None
        dq = s.dq[q]
        n = dq['n']
        sem = dq['sems'][n % NDS]
        val = 16 * (n // NDS + 1)
        dq['n'] += 1
        tok = (sem, val)
        need = s._deps(r, w, tok)
        if n >= NDS:
            need.append((sem, val - 16))
        waits = s._need(q, need)
        if fn is None:
            fn = lambda e: e.dma_start(out=out, in_=in_, **kw)
        s._emit(q, waits, fn, (sem, 16))
        if is_out:
            s.out_toks.append(tok)
        return tok

    def cc(s, fn, r=(), w=()):
        if DEAD[0]:
            return None
        s.cc_n += 1
        tok = (s.cc_sem, s.cc_n)
        need = s._deps(r, w, tok)
        waits = s._need('pool', need)
        h = s.H['pool']
        for sem, val in waits:
            h.wait_ge(sem, val)
        fn(h).then_inc(s.cc_sem)
        return tok

    def all_toks(s):
        toks = [(e['sem'], e['cnt']) for e in s.eng.values() if e['cnt'] > 0]
        if s.cc_n > 0:
            toks.append((s.cc_sem, s.cc_n))
        for dq in s.dq.values():
            n = dq['n']
            for i in range(min(n, NDS)):
                last = ((n - 1 - i) // NDS) * NDS + i if False else None
            for i, sem in enumerate(dq['sems']):
                cnt = (n - i + NDS - 1) // NDS if n > i else 0
                if cnt > 0:
                    toks.append((sem, 16 * cnt))
        return toks

    def barrier(s):
        if DEAD[0]:
            return
        toks = s.all_toks()
        for n in s.eng:
            waits = s._need(n, [t for t in toks if not (t[0] is s.eng[n]['sem'])])
            if waits:
                s._emit(n, waits, None, None)

    def finish(s):
        waits = s._need('sp', s.all_toks())
        s._emit('sp', waits, None, None)

    def emit(s):
        with s.nc.Block() as block:
            def mk(name):
                def f(e):
                    for waits, fn, inc in s.eng[name]['ops']:
                        for sem, val in waits:
                            e.wait_ge(sem, val)
                        if fn is not None:
                            ins = fn(e)
                            if inc is not None:
                                ins.then_inc(inc[0], inc[1])
                return f
            block.tensor(mk('pe'))
            block.scalar(mk('act'))
            block.vector(mk('dve'))
            block.gpsimd(mk('pool'))
            block.sync(mk('sp'))


def build_program():
    DEAD[0] = False
    nc = bass.Bass("TRN2", target_bir_lowering=False)
    SPL, NS = sp_layout()

    def din(name, shape):
        return nc.dram_tensor(name, list(shape), F32, kind="ExternalInput").ap()

    def dout(name, shape):
        return nc.dram_tensor(name, list(shape), F32, kind="ExternalOutput").ap()

    xp_d = din('xp', [TP, 1024]); xso_d = din('xs_own', [TS, 1024]); xst_d = din('xs_oth', [TS, 1024])
    cos_d = din('cos', [128, 4096]); sin_d = din('sin', [128, 4096])
    cak_d = din('cak', [256, 128]); cav_d = din('cav', [256, 128]); cbk_d = din('cbk', [256, 512]); cbv_d = din('cbv', [256, 512])
    sp_d = din('sp', [128, NS]); ident_d = din('ident', [128, 128]); mask_d = din('masks', [128, 2, 512])
    wada_d = din('w_ada', [2, 1024, 6144]); w1_d = din('w_mlp1', [2, 1024, 4096]); w2_d = din('w_mlp2', [2, 4096, 1024])
    win_d = din('w_in', [1024, 2304]); winx_d = din('w_in_x', [1024, WX]); wout_d = din('w_out', [1024, 1024])
    rwin_d = din('rec_w_in', [1024, 2560]); rwout_d = din('rec_w_out', [1280, 1024])
    rwa_p_d = din('rwa_p', [2, 10, 128, 128]); rwx_p_d = din('rwx_p', [2, 10, 128, 128])
    rwa_s_d = din('rwa_s', [2, 10, 128, 128]); rwx_s_d = din('rwx_s', [2, 10, 128, 128])
    yp_d = dout('yp', [TP, 1024]); ys_d = dout('ys', [TS, 1024])
    nak_d = dout('nak', [TP, 128]); nav_d = dout('nav', [TP, 128]); nbk_d = dout('nbk', [TP, 512]); nbv_d = dout('nbv', [TP, 512])
    nsf_d = dout('nsf', [4, 1280]); nsb_d = dout('nsb', [4, 1280])
    xsrc1 = nc.dram_tensor('xsrc1', [128, 128], F32).ap()
    xdst1 = nc.dram_tensor('xdst1', [128, 128], F32).ap()
    xsrc2 = nc.dram_tensor('xsrc2', [128, 128], F32).ap()
    xdst2 = nc.dram_tensor('xdst2', [128, 128], F32).ap()
    PAIRS = [[0, 1], [2, 3], [4, 5], [6, 7]]

    with ExitStack() as es:
        K = KB_(nc, es)
        PS = [Tl(es.enter_context(nc.psum_tensor(f'ps{i}', [128, 512], F32)), f'ps{i}') for i in range(8)]
        for p_ in PS:
            p_.psum = True
        ring = {'a': 0, 'b': 0}

        def psA():
            ring['a'] = (ring['a'] + 1) % 4
            return PS[ring['a']]

        def psB():
            ring['b'] = (ring['b'] + 1) % 4
            return PS[4 + ring['b']]

        ident = K.tile(es, 'ident', [128, 128], F32)
        ones = K.tile(es, 'ones', [128, 128], BF16)
        masks = K.tile(es, 'masks', [128, 2, 512], F32)
        spt = K.tile(es, 'spt', [128, NS], F32)
        modv = K.tile(es, 'modv', [128, 2, 48, 2], F32)
        Amod = K.tile(es, 'Amod', [128, 2, 2, 2, 8], F32)
        cons = K.tile(es, 'cons', [128, 64], F32)
        K.dma('sp', ident[:], ident_d, w=[ident])
        K.dma('sp', masks[:], mask_d, w=[masks])
        K.dma('sp', spt[:], sp_d, w=[spt])
        K.op('dve', lambda e: e.memset(ones[:], 1.0), w=[ones])

        def SP(name, a=0, b=None):
            o, w = SPL[name]
            b = w if b is None else b
            return spt[:, o + a:o + b]

        flip = {'i': 0}

        def evac(out, in_, r, w, scale=1.0, bias=0.0, eng=None):
            if eng is None:
                flip['i'] ^= 1
                eng = 'act' if flip['i'] else 'dve'
            if eng == 'act':
                K.op('act', lambda e: e.activation(out=out, in_=in_, func=AF.Identity, bias=bias, scale=scale), r, w)
            else:
                K.op('dve', lambda e: e.tensor_copy(out=out, in_=in_), r, w)

        def mm(out, lhsT, rhs, start, stop, r, w, last):
            K.op('pe', lambda e: e.matmul(out, lhsT, rhs, start=start, stop=stop), r, w, inc=last)

        def tr(out, in_, r, w, last):
            K.op('pe', lambda e: e.transpose(out, in_, ident[:]), list(r) + [ident], w, inc=last)

        slabs = []
        slab_i = {'i': 0}

        def alloc_slabs(scope, n, size=4096):
            slabs.clear()
            slab_i['size'] = size
            for i in range(n):
                slabs.append(K.tile(scope, f'slab{i}_{len(K.tiles)}', [128, size], BF16))
                K.tiles.append(None)
            slab_i['i'] = 0

        def wload(parts):
            sl = slabs[slab_i['i'] % len(slabs)]
            slab_i['i'] += 1
            views, off = [], 0
            for j, src in enumerate(parts):
                kc, n = src.shape[1], src.shape[2]
                v = sl[:, off:off + kc * n].rearrange("p (k n) -> p k n", k=kc)
                K.dma('pool', v, src, w=[(sl, j)] if len(parts) > 1 else [sl])
                views.append(v)
                off += kc * n
            assert off <= slab_i['size']
            return sl, views

        def wview(w2d, c0, n, k0=0, kc=None):
            v = w2d.rearrange("(k p) n -> p k n", p=128)
            kc = v.shape[1] - k0 if kc is None else kc
            return v[:, k0:k0 + kc, c0:c0 + n]

        with ExitStack() as sc:
            alloc_slabs(sc, 4)
            csil = K.tile(sc, 'csil', [128, 8, 2], BF16)
            K.op('act', lambda e: e.activation(out=csil[:], in_=SP('cT').rearrange("p (k j) -> p k j", j=2), func=AF.Silu),
                 [spt], [csil])
            for l in range(2):
                ps = psA()
                for blk in range(12):
                    sl, (wv,) = wload([wview(wada_d[l], blk * 512, 512)])
                    for j in range(4):
                        ch = blk * 4 + j
                        for kc in range(8):
                            mm(ps[:, ch * 2:ch * 2 + 2], wv[:, kc, j * 128:(j + 1) * 128], csil[:, kc, :],
                               kc == 0, kc == 7, [sl, csil], [ps], kc == 7)
                for cv in range(2):
                    K.op('dve', lambda e, l=l, cv=cv, ps=ps: e.tensor_tensor(
                        out=modv[:, l, :, cv], in0=ps[:, 0:96].rearrange("p (c j) -> p c j", j=2)[:, :, cv],
                        in1=SP('bada', l * 48, l * 48 + 48), op=ALU.add), [ps, spt], [(modv, (l, cv))])
                for nrm in range(2):
                    for cv in range(2):
                        g = SP('g1' if nrm == 0 else 'g2', l * 8, l * 8 + 8)
                        K.op('dve', lambda e, l=l, nrm=nrm, cv=cv, g=g: e.scalar_tensor_tensor(
                            out=Amod[:, l, nrm, cv, :], in0=modv[:, l, nrm * 24 + 8:nrm * 24 + 16, cv], scalar=1.0,
                            in1=g, op0=ALU.add, op1=ALU.mult), [(modv, (l, cv)), spt], [(Amod, (l, nrm, cv))])
            K.barrier()

        def MA(l, nrm, cv, c):
            return Amod[:, l, nrm, cv, c:c + 1]

        def MB(l, nrm, cv, c):
            return modv[:, l, nrm * 24 + c, cv:cv + 1]

        def MG(l, nrm, cv, c):
            return modv[:, l, nrm * 24 + 16 + c, cv:cv + 1]

        def load_xT(scope_tiles, x_d, t0, ntt, dst, dst_t0, T_key):
            stg = scope_tiles['stg']
            for tt in range(ntt):
                st = stg[tt % 2]
                K.dma('sp', st[:], x_d[t0 + tt * 128:t0 + (tt + 1) * 128, :], w=[st])
                for hb in range(2):
                    ps = psA()
                    for j in range(4):
                        c = hb * 4 + j
                        tr(ps[:, j * 128:(j + 1) * 128], st[:, c * 128:(c + 1) * 128], [st], [ps], j == 3)
                    o = dst_t0 + tt * 128
                    evac(dst[:, hb * 4:hb * 4 + 4, o:o + 128], ps[:, :].rearrange("p (c t) -> p c t", c=4),
                         [ps], [(dst, (T_key, o // 512))])

        def norm_stats(tmp, xsrc, xkey, n=512):
            sq, rstd = tmp['sq'], tmp['rstd']
            K.op('act', lambda e: e.activation(out=sq[:, :, 0:n], in_=xsrc, func=AF.Square), [xkey], [sq])
            ps = psA()
            for c in range(8):
                mm(ps[:, 0:n], ones[:], sq[:, c, 0:n], c == 0, c == 7, [ones, sq], [ps], c == 7)
            K.op('act', lambda e: e.activation(out=rstd[:, 0:n], in_=ps[:, 0:n], func=AF.Ln, bias=cons[:, 20:21], scale=1.0 / 1024),
                 [ps, cons], [rstd])
            K.op('act', lambda e: e.activation(out=rstd[:, 0:n], in_=rstd[:, 0:n], func=AF.Exp, scale=-0.5), [rstd], [rstd])
            return rstd

        def norm_mod(tmp, xsrc, xkey, hdst, hkey, l, nrm, cv, n=512):
            rstd = norm_stats(tmp, xsrc, xkey, n)
            u = tmp['u']
            for c in range(8):
                uc = u[c % 2]
                K.op('dve', lambda e, c=c, uc=uc: e.scalar_tensor_tensor(
                    out=uc[:, 0:n], in0=xsrc[:, c, :], scalar=MA(l, nrm, cv, c), in1=rstd[:, 0:n],
                    op0=ALU.mult, op1=ALU.mult), [xkey, rstd, Amod], [uc])
                K.op('act', lambda e, c=c, uc=uc: e.activation(out=hdst(c), in_=uc[:, 0:n], func=AF.Identity,
                                                               bias=MB(l, nrm, cv, c), scale=1.0), [uc, modv], [hkey])

        def linear_fm(wparts, hsrc, hkeys, ntile, epi, tile0=0):
            ci = 0
            for src in wparts:
                sl, (wv,) = wload([src])
                kcn, n = src.shape[1], src.shape[2]
                for j in range(n // 128):
                    for t in range(tile0, tile0 + ntile):
                        ps = psA()
                        for kc in range(kcn):
                            mm(ps[:, :], wv[:, kc, j * 128:(j + 1) * 128], hsrc(kc, t), kc == 0, kc == kcn - 1,
                               [sl] + hkeys(t), [ps], kc == kcn - 1)
                        epi(ci, t, ps)
                    ci += 1

        def resid_epi(xT, xkey, l, nrm, cv):
            def epi(c, t, ps):
                K.op('dve', lambda e: e.scalar_tensor_tensor(
                    out=xT[:, c, t * 512:(t + 1) * 512], in0=ps[:, :], scalar=MG(l, nrm, cv, c),
                    in1=xT[:, c, t * 512:(t + 1) * 512], op0=ALU.mult, op1=ALU.add), [ps, modv, xkey(t)], [xkey(t)])
            return epi

        def mlp(scope, xT, xkeyf, T, l, cv, tmp, hT, nhalf):
            nt = T // 512
            for t in range(nt):
                norm_mod(tmp, xT[:, :, t * 512:(t + 1) * 512], xkeyf(t),
                         lambda c, t=t: hT[:, c, t * 512:(t + 1) * 512], (hT, t), l, 1, cv)
            HC = 32 // nhalf
            hid = K.tile(scope, f'hid{l}{cv}', [128, HC, T], BF16)
            rl = [K.tile(scope, f'rl{l}{cv}{i}', [128, 512], F32) for i in range(2)]
            for half in range(nhalf):
                def epi1(c, t, ps):
                    r_ = rl[(c + t) % 2]
                    K.op('act', lambda e: e.activation(out=r_[:], in_=ps[:, :], func=AF.Relu), [ps], [r_])
                    K.op('pool', lambda e: e.tensor_tensor(out=hid[:, c, t * 512:(t + 1) * 512], in0=r_[:], in1=r_[:],
                                                           op=ALU.mult), [r_], [(hid, (c, t))])
                linear_fm([wview(w1_d[l], half * HC * 128 + s * 512, 512) for s in range(HC // 4)],
                          lambda kc, t: hT[:, kc, t * 512:(t + 1) * 512], lambda t: [(hT, t)], nt, epi1)
                ncw = min(512, 4096 // HC)
                linear_fm([wview(w2_d[l], c * ncw, ncw, k0=half * HC, kc=HC) for c in range(1024 // ncw)],
                          lambda kc, t: hid[:, kc, t * 512:(t + 1) * 512], lambda t: [hid], nt,
                          resid_epi(xT, xkeyf, l, 1, cv))

        def final_out(tmp, xT, xkeyf, T, y_d, ostg):
            for t in range(T // 512):
                xs = xT[:, :, t * 512:(t + 1) * 512]
                rstd = norm_stats(tmp, xs, xkeyf(t))
                yn = tmp['yn']
                for c in range(8):
                    K.op('dve', lambda e, c=c: e.scalar_tensor_tensor(
                        out=yn[:, c, :], in0=xs[:, c, :], scalar=SP('gf', c, c + 1), in1=rstd[:, 0:512],
                        op0=ALU.mult, op1=ALU.mult), [xkeyf(t), rstd, spt], [(yn, c)])
                for tt in range(4):
                    og = ostg[tt % 2]
                    for hb in range(2):
                        ps = psA()
                        for j in range(4):
                            c = hb * 4 + j
                            tr(ps[:, j * 128:(j + 1) * 128], yn[:, c, tt * 128:(tt + 1) * 128], [(yn, c)], [ps], j == 3)
                        evac(og[:, hb * 512:(hb + 1) * 512], ps[:, :], [ps], [(og, hb)])
                    r0 = t * 512 + tt * 128
                    K.dma('sp', y_d[r0:r0 + 128, :], og[:], r=[og], is_out=True)

        def attn_A_unit(tmp, qrhs, qkey, kblocks, g, OT, otok0, esink):
            O, D = psB(), psB()
            Pts = tmp['P']
            nb = len(kblocks)
            Ss = [None] * nb

            def qk(i):
                S = psA()
                mm(S[:, :], kblocks[i][0], qrhs, True, True, [kblocks[i][1], qkey], [S], True)
                Ss[i] = S
            qk(0)
            if nb > 1:
                qk(1)
            for i in range(nb):
                if i + 2 < nb:
                    qk(i + 2)
                Pt = Pts[tmp['pi'] % len(Pts)]
                tmp['pi'] += 1
                S = Ss[i]
                K.op('act', lambda e, S=S, Pt=Pt: e.activation(out=Pt[:], in_=S[:, :], func=AF.Exp, scale=SCALE), [S], [Pt])
                mi = kblocks[i][4]
                if mi is not None:
                    K.op('dve', lambda e, Pt=Pt, mi=mi: e.tensor_tensor(out=Pt[:], in0=Pt[:], in1=masks[:, mi, :], op=ALU.mult),
                         [Pt, masks], [Pt])
                mm(O[:, :], kblocks[i][2], Pt[:], i == 0, i == nb - 1, [kblocks[i][3], Pt], [O], i == nb - 1)
                mm(D[:, :], ones[:], Pt[:], i == 0, i == nb - 1, [ones, Pt], [D], i == nb - 1)
            rd = tmp['rd']
            for j in range(4):
                h = g * 4 + j
                K.op('act', lambda e, j=j, h=h: e.activation(out=rd[:, j * 128:(j + 1) * 128], in_=D[:, j * 128:(j + 1) * 128],
                                                             func=AF.Ln, bias=esink[:, h:h + 1], scale=1.0),
                     [D, cons], [(rd, j)])
            K.op('act', lambda e: e.activation(out=rd[:], in_=rd[:], func=AF.Exp, scale=-1.0), [rd], [rd])
            for j in range(4):
                p0 = (j % 2) * 64
                ch = g * 2 + j // 2
                K.op('dve', lambda e, j=j, p0=p0, ch=ch: e.tensor_tensor(
                    out=OT[p0:p0 + 64, ch, otok0:otok0 + 128], in0=O[p0:p0 + 64, j * 128:(j + 1) * 128],
                    in1=rd[p0:p0 + 64, j * 128:(j + 1) * 128], op=ALU.mult), [O, rd], [(OT, (ch, p0, otok0))])

        pend_epi = []

        def attn_B_unit(tmp, q1, q2, qkey, kblocks, OT, och, otok0, neglam, subw):
            O, D = psB(), psB()
            Pts = tmp['P']
            nb = len(kblocks)
            Ss = [None] * nb

            def qk(i):
                S = psA()
                mm(S[:, 0:256], kblocks[i][0], q1, True, True, [kblocks[i][2], qkey], [S], False)
                mm(S[:, 256:512], kblocks[i][1], q2, True, True, [kblocks[i][2], qkey], [S], True)
                Ss[i] = S
            qk(0)
            if nb > 1:
                qk(1)
            for i in range(nb):
                if i + 2 < nb:
                    qk(i + 2)
                Pt = Pts[tmp['pi'] % len(Pts)]
                tmp['pi'] += 1
                S = Ss[i]
                K.op('act', lambda e, S=S, Pt=Pt: e.activation(out=Pt[:], in_=S[:, :], func=AF.Exp, scale=SCALE), [S], [Pt])
                mm(O[:, :], kblocks[i][3], Pt[:], i == 0, i == nb - 1, [kblocks[i][4], Pt], [O], i == nb - 1)
                mm(D[:, :], ones[:], Pt[:], i == 0, i == nb - 1, [ones, Pt], [D], i == nb - 1)
                if i == min(8, nb - 1) and pend_epi:
                    pend_epi.pop()()
            b_epilogue1(tmp, O, D, neglam)
            pend_epi.append(lambda: b_epilogue2(tmp, OT, och, otok0, subw))

        def flush_epi():
            while pend_epi:
                pend_epi.pop()()

        def b_epilogue1(tmp, O, D, neglam):
            rd, ob, sqb, rs = tmp['rd'], tmp['ob'], tmp['sqb'], tmp['rs']
            K.op('act', lambda e: e.activation(out=rd[:], in_=D[:, :], func=AF.Ln), [D], [rd])
            K.op('act', lambda e: e.activation(out=rd[:], in_=rd[:], func=AF.Exp, scale=-1.0), [rd], [rd])
            K.op('dve', lambda e: e.tensor_tensor(out=rd[:], in0=O[:, :], in1=rd[:], op=ALU.mult), [O, rd], [rd])
            K.op('dve', lambda e: e.scalar_tensor_tensor(out=ob[:], in0=rd[:, 256:512], scalar=neglam, in1=rd[:, 0:256],
                                                         op0=ALU.mult, op1=ALU.add), [rd, cons], [ob])
            K.op('act', lambda e: e.activation(out=sqb[:], in_=ob[:], func=AF.Square), [ob], [sqb])

        def b_epilogue2(tmp, OT, och, otok0, subw):
            rd, ob, sqb, rs = tmp['rd'], tmp['ob'], tmp['sqb'], tmp['rs']
            ps = psA()
            mm(ps[:, 0:256], ones[:], sqb[:], True, True, [ones, sqb], [ps], True)
            K.op('act', lambda e: e.activation(out=rs[:], in_=ps[:, 0:256], func=AF.Ln, bias=cons[:, 20:21], scale=1.0 / 128), [ps, cons], [rs])
            K.op('act', lambda e: e.activation(out=rs[:], in_=rs[:], func=AF.Exp, scale=-0.5), [rs], [rs])
            K.op('dve', lambda e: e.scalar_tensor_tensor(out=OT[:, och, otok0:otok0 + 256], in0=ob[:], scalar=subw, in1=rs[:],
                                                         op0=ALU.mult, op1=ALU.mult), [ob, rs, cons], [(OT, (och, 0, otok0))])

        lq = SP('lamqk')
        K.op('dve', lambda e: e.memset(cons[:], 0.0), [], [cons])
        K.op('dve', lambda e: e.memset(cons[:, 20:21], EPS), [cons], [cons])
        prod = K.tile(es, 'lprod', [128, 128], F32)
        K.op('dve', lambda e: e.tensor_tensor(out=prod[:].rearrange("p (a d) -> p a d", a=2),
                                              in0=lq.rearrange("p (a b d) -> p a b d", a=2, b=2)[:, :, 0, :],
                                              in1=lq.rearrange("p (a b d) -> p a b d", a=2, b=2)[:, :, 1, :], op=ALU.mult),
             [spt], [prod])
        K.op('dve', lambda e: e.reduce_sum(out=cons[:, 16:18], in_=prod[:].rearrange("p (a d) -> p a d", a=2),
                                           axis=mybir.AxisListType.X), [prod, cons], [cons])
        K.op('act', lambda e: e.activation(out=cons[:, 18:20], in_=cons[:, 16:18], func=AF.Exp), [cons], [cons])
        K.op('act', lambda e: e.activation(out=cons[:, 0:8], in_=SP('sink'), func=AF.Exp), [spt, cons], [cons])
        K.op('dve', lambda e: e.tensor_tensor(out=cons[:, 8:9], in0=cons[:, 19:20], in1=cons[:, 18:19], op=ALU.subtract), [cons], [cons])
        K.op('dve', lambda e: e.tensor_scalar(out=cons[:, 8:9], in0=cons[:, 8:9], scalar1=-LAM_INIT, scalar2=None, op0=ALU.add), [cons], [cons])
        K.op('dve', lambda e: e.tensor_scalar(out=cons[:, 9:10], in0=SP('subln'), scalar1=1.0 - LAM_INIT, scalar2=None, op0=ALU.mult),
             [spt, cons], [cons])
        esink, neglam, subw = cons, cons[:, 8:9], cons[:, 9:10]

        def exchange(scope, nm, data_ap, data_keys, W, xsrc, xdst, ksrc, kdst, out_ap, out_key):
            cb = K.tile(scope, 'cb' + nm, [128, 128], F32)
            hg = K.tile(scope, 'hg' + nm, [128, 128], F32)
            K.op('dve', lambda e: e.memset(cb[:], 0.0), [], [cb])
            K.op('dve', lambda e: e.tensor_scalar(out=cb[:, 0:W], in0=data_ap, scalar1=SP('sel', 1, 2), scalar2=None, op0=ALU.mult),
                 list(data_keys) + [spt, cb], [cb])
            K.op('dve', lambda e: e.tensor_scalar(out=cb[:, 64:64 + W], in0=data_ap, scalar1=SP('sel', 0, 1), scalar2=None, op0=ALU.mult),
                 list(data_keys) + [spt, cb], [cb])
            K.dma('sp', xsrc, cb[:], r=[cb], w=[ksrc])
            if NO_CC:
                K.dma('pool', xdst, xsrc, r=[ksrc], w=[kdst])
            else:
                K.cc(lambda e: e.collective_compute("AllReduce", ALU.add, replica_groups=PAIRS, ins=[xsrc.opt()], outs=[xdst.opt()]),
                     r=[ksrc], w=[kdst])
            K.dma('sp', hg[:], xdst, r=[kdst], w=[hg])
            K.op('dve', lambda e: e.tensor_scalar(out=out_ap, in0=hg[:, 0:W], scalar1=SP('sel', 0, 1), scalar2=None, op0=ALU.mult),
                 [hg, spt], [out_key])
            K.op('dve', lambda e: e.scalar_tensor_tensor(out=out_ap, in0=hg[:, 64:64 + W], scalar=SP('sel', 1, 2), in1=out_ap,
                                                         op0=ALU.mult, op1=ALU.add), [hg, spt, out_key], [out_key])

        def rec_layer(scope, xT, xkeyf, T, hT, cv, segL, phases, sfx, rwa_d, rwx_d, tmp=None, exch=None, state_out=None):
            nseg = T // segL
            nt = T // 512
            PL = 512
            npc = T // PL
            xrp2 = [K.tile(scope, f'xrp{i}' + sfx, [128, nseg, segL + 4], F32) for i in range(2)]
            xc2 = [K.tile(scope, f'xc{i}' + sfx, [128, 512], F32) for i in range(3)]
            xcb2 = [K.tile(scope, f'xcb{i}' + sfx, [128, 512], BF16) for i in range(3)]
            gb = K.tile(scope, 'gb' + sfx, [128, T], BF16)
            bA2 = [K.tile(scope, f'bA{i}' + sfx, [128, PL], F32) for i in range(2)]
            bR2 = [K.tile(scope, f'bR{i}' + sfx, [128, PL], F32) for i in range(2)]
            bI2 = [K.tile(scope, f'bI{i}' + sfx, [128, PL], F32) for i in range(2)]
            y = K.tile(scope, 'y' + sfx, [128, 10, T], BF16)
            wg2 = [K.tile(scope, f'wg{i}' + sfx, [128, 4, 128], BF16) for i in range(3)]
            carry = K.tile(scope, 'carry' + sfx, [128, 4], F32)
            cl = K.tile(scope, 'cl' + sfx, [128, 2, 2, 10], F32)
            pfx = '_p' if sfx == 'p' else '_s'
            lam = SP('lam' + pfx).rearrange("p (d c) -> p d c", d=2)
            K.op('act', lambda e: e.activation(out=cl[:, :, 0, :], in_=lam, func=AF.Exp, scale=-1.0), [spt], [cl])
            K.op('act', lambda e: e.activation(out=cl[:, :, 0, :], in_=cl[:, :, 0, :], func=AF.Ln, bias=1.0, scale=1.0), [cl], [cl])
            K.op('dve', lambda e: e.tensor_scalar(out=cl[:, :, 1, :], in0=cl[:, :, 0, :], scalar1=-16.0, scalar2=None, op0=ALU.mult), [cl], [cl])
            K.op('dve', lambda e: e.tensor_scalar(out=cl[:, :, 0, :], in0=cl[:, :, 0, :], scalar1=-8.0, scalar2=None, op0=ALU.mult), [cl], [cl])
            for xr_ in xrp2:
                K.op('pool', lambda e, xr_=xr_: e.memset(xr_[:], 0.0), [], [xr_])
            cw = SP('convw' + pfx)
            halo = None
            if exch is not None:
                hl = K.tile(scope, 'hl', [128, 10, 2], F32)
                halo = K.tile(scope, 'halo', [128, 10, 2], F32)
                for n in range(10):
                    sl, (wv,) = wload([wview(rwin_d, 1280 + n * 128, 128)])
                    ps = psA()
                    for kc in range(8):
                        mm(ps[:, 0:2], wv[:, kc, :], hT[:, kc, T - 2:T], kc == 0, kc == 7, [sl, (hT, nt - 1)], [ps], kc == 7)
                    evac(hl[:, n, :], ps[:, 0:2], [ps], [(hl, n)], eng='dve')
                exchange(scope, 'x1', hl[:].rearrange("p a b -> p (a b)"), [hl], 20, xsrc1, xdst1, exch['xsrc1'], exch['xdst1'],
                         halo[:].rearrange("p a b -> p (a b)"), halo)
            gcount = {'i': 0}
            for (dirs, final) in phases:
                if exch is not None and final:
                    exchange(scope, 'x2', exch['stS'][:], [exch['stS']], 10, xsrc2, xdst2, exch['xsrc2'], exch['xdst2'],
                             exch['sdn'][:], exch['sdn'])
                pend = {}

                wq = {}

                def wfetch(n, dirs=dirs, final=final):
                    parts = [wview(rwin_d, 1280 + n * 128, 128)]
                    if final:
                        parts.append(wview(rwin_d, n * 128, 128))
                    sl, wvs = wload(parts)
                    slk = [(sl, j) for j in range(len(parts))] if len(parts) > 1 else [sl]
                    wg = wg2[n % 3]
                    gparts = []
                    for d in dirs:
                        gparts += [rwa_d[d, n].rearrange("c (o d) -> c o d", o=1), rwx_d[d, n].rearrange("c (o d) -> c o d", o=1)]
                    for j, gp in enumerate(gparts):
                        K.dma('pool', wg[:, j:j + 1, :], gp, w=[(wg, j)])
                    wq[n] = (sl, wvs, slk)

                def inproj(n, dirs=dirs, final=final):
                    xrp = xrp2[n % 2]
                    sl, wvs, slk = wq.pop(n)
                    wg = wg2[n % 3]
                    for t in range(nt):
                        ps = psA()
                        for kc in range(8):
                            mm(ps[:, :], wvs[0][:, kc, :], hT[:, kc, t * 512:(t + 1) * 512], kc == 0, kc == 7,
                               [slk[0], (hT, t)], [ps], kc == 7)
                        if segL >= 512:
                            sgi, o = (t * 512) // segL, (t * 512) % segL
                            evac(xrp[:, sgi, 2 + o:2 + o + 512], ps[:, :], [ps], [(xrp, t)], eng='act')
                        else:
                            ns = 512 // segL
                            evac(xrp[:, t * ns:(t + 1) * ns, 2:2 + segL], ps[:, :].rearrange("p (s l) -> p s l", s=ns), [ps], [(xrp, t)], eng='act')
                    if halo is not None:
                        K.op('dve', lambda e, n=n: e.tensor_copy(out=xrp[:, 0, 2 + segL:4 + segL], in_=halo[:, n, :][:, ::-1]), [halo], [(xrp, 'halo')])
                    pend[n] = (sl, wvs, slk)

                def gateproj(n):
                    sl, wvs, slk = pend[n]
                    for t in range(nt):
                        ps2 = psA()
                        for kc in range(8):
                            mm(ps2[:, :], wvs[1][:, kc, :], hT[:, kc, t * 512:(t + 1) * 512], kc == 0, kc == 7,
                               [slk[1], (hT, t)], [ps2], kc == 7)
                        K.op('act', lambda e, t=t, ps2=ps2: e.activation(out=gb[:, t * 512:(t + 1) * 512], in_=ps2[:, :],
                                                                         func=AF.Gelu_apprx_tanh), [ps2], [(gb, t)])

                def stageA(n, pc, k):
                    xrp = xrp2[n % 2]
                    xcp, xbp = xc2[k % 3], xcb2[k % 3]
                    c0 = pc * PL
                    if segL >= PL:
                        segs = [(c0 // segL, c0 % segL, PL, 0)]
                    else:
                        segs = [(c0 // segL + s_, 0, segL, s_ * segL) for s_ in range(PL // segL)]
                    for (sgi, o, L_, bo) in segs:
                        o_ = xcp[:, bo:bo + L_]
                        K.op('dve', lambda e, sgi=sgi, o=o, L_=L_, o_=o_: e.tensor_scalar(
                            out=o_, in0=xrp[:, sgi, o:o + L_], scalar1=cw[:, n:n + 1], scalar2=SP('convb', n, n + 1),
                            op0=ALU.mult, op1=ALU.add), [xrp, spt], [xcp])
                        for k_ in range(1, 5):
                            K.op('dve', lambda e, sgi=sgi, o=o, L_=L_, o_=o_, k_=k_: e.scalar_tensor_tensor(
                                out=o_, in0=xrp[:, sgi, o + k_:o + k_ + L_], scalar=cw[:, k_ * 10 + n:k_ * 10 + n + 1], in1=o_,
                                op0=ALU.mult, op1=ALU.add), [xrp, spt, xcp], [xcp])
                    K.op('pool', lambda e: e.tensor_copy(out=xbp[:], in_=xcp[:]), [xcp], [xbp])

                def stageB(n, pc, k, dirs=dirs, final=final):
                    wg = wg2[n % 3]
                    xcp, xbp = xc2[k % 3], xcb2[k % 3]
                    c0 = pc * PL
                    for di, d in enumerate(dirs):
                        up = (d == 0)
                        gi = gcount['i'] % 2
                        gcount['i'] += 1
                        bA, bR, bI = bA2[gi], bR2[gi], bI2[gi]
                        psr, psi = psA(), psA()
                        mm(psr[:, :], wg[:, 2 * di, :], xbp[:], True, True, [(wg, 2 * di), xbp], [psr], True)
                        mm(psi[:, :], wg[:, 2 * di + 1, :], xbp[:], True, True, [(wg, 2 * di + 1), xbp], [psi], True)
                        K.op('act', lambda e, psr=psr, bR=bR, d=d: e.activation(
                            out=bR[:], in_=psr[:, :], func=AF.Sigmoid, bias=SP('ba' + pfx, d * 10 + n, d * 10 + n + 1), scale=1.0),
                            [psr, spt], [bR])
                        K.op('act', lambda e, psi=psi, bI=bI, d=d: e.activation(
                            out=bI[:], in_=psi[:, :], func=AF.Sigmoid, bias=SP('bx' + pfx, d * 10 + n, d * 10 + n + 1), scale=1.0),
                            [psi, spt], [bI])
                        K.op('act', lambda e, d=d, bA=bA, bR=bR: e.activation(out=bA[:], in_=bR[:], func=AF.Exp, scale=cl[:, d, 0, n:n + 1]), [bR, cl], [bA])
                        K.op('act', lambda e, d=d, bR=bR: e.activation(out=bR[:], in_=bR[:], func=AF.Exp, scale=cl[:, d, 1, n:n + 1]), [bR, cl], [bR])
                        K.op('pool', lambda e, bI=bI: e.tensor_tensor(out=bI[:], in0=bI[:], in1=xcp[:], op=ALU.mult), [bI, xcp], [bI])
                        K.op('act', lambda e, bR=bR: e.activation(out=bR[:], in_=bR[:], func=AF.Sqrt, bias=1.0, scale=-1.0), [bR], [bR])
                        K.op('pool', lambda e, bI=bI, bR=bR: e.tensor_tensor(out=bI[:], in0=bI[:], in1=bR[:], op=ALU.mult), [bI, bR], [bI])
                        nsc = PL // segL if segL < PL else 1
                        L = PL // nsc
                        seg_first = (c0 % segL == 0) if up else ((c0 + PL) % segL == 0)
                        for sc_ in range(nsc):
                            lo = sc_ * L
                            if segL <= PL or (seg_first and exch is None):
                                init, ik = 0.0, []
                            elif seg_first:
                                init = (SP('state_up', n, n + 1) if up else exch['sdn'][:, n:n + 1])
                                ik = [spt if up else exch['sdn']]
                            else:
                                init, ik = carry[:, 0:1], [carry]
                            if up:
                                K.op('dve', lambda e, lo=lo, L=L, init=init, bA=bA, bR=bR, bI=bI: e.tensor_tensor_scan(
                                    out=bR[:, lo:lo + L], data0=bA[:, lo:lo + L], data1=bI[:, lo:lo + L], initial=init,
                                    op0=ALU.mult, op1=ALU.add), [bA, bI] + ik, [bR])
                            else:
                                K.op('dve', lambda e, lo=lo, L=L, init=init, bA=bA, bR=bR, bI=bI: e.tensor_tensor_scan(
                                    out=bR[:, lo:lo + L][:, ::-1], data0=bA[:, lo:lo + L][:, ::-1], data1=bI[:, lo:lo + L][:, ::-1],
                                    initial=init, op0=ALU.mult, op1=ALU.add), [bA, bI] + ik, [bR])
                        last_col = PL - 1 if up else 0
                        if segL > PL:
                            K.op('dve', lambda e, last_col=last_col, bR=bR: e.tensor_copy(out=carry[:, 0:1], in_=bR[:, last_col:last_col + 1]), [bR], [carry])
                            if exch is not None and up and pc == npc - 1:
                                K.op('dve', lambda e, last_col=last_col, bR=bR: e.tensor_copy(out=exch['stS'][:, n:n + 1], in_=bR[:, last_col:last_col + 1]),
                                     [bR], [exch['stS']])
                        elif state_out is not None:
                            lc = segL - 1 if up else 0
                            ns_ = PL // segL
                            s0 = c0 // segL
                            K.op('dve', lambda e, d=d, lc=lc, bR=bR, s0=s0, ns_=ns_: e.tensor_copy(
                                out=state_out[:, d, s0:s0 + ns_, n], in_=bR[:].rearrange("p (s l) -> p s l", l=segL)[:, :, lc]), [bR], [(state_out, (d, n, s0))])
                        yv = y[:, n, c0:c0 + PL]
                        first = (di == 0 and not (final and len(dirs) == 1))
                        lastd = final and di == len(dirs) - 1
                        if first and not lastd:
                            K.op('pool', lambda e, yv=yv, bR=bR: e.tensor_copy(out=yv, in_=bR[:]), [bR], [(y, (n, pc))])
                        else:
                            if not first:
                                K.op('pool', lambda e, yv=yv, bR=bR: e.tensor_tensor(out=bR[:], in0=bR[:], in1=yv, op=ALU.add), [bR, (y, (n, pc))], [bR])
                            if lastd:
                                K.op('dve', lambda e, yv=yv, c0=c0, bR=bR: e.tensor_tensor(out=yv, in0=bR[:], in1=gb[:, c0:c0 + PL], op=ALU.mult),
                                     [bR, gb], [(y, (n, pc))])
                            else:
                                K.op('pool', lambda e, yv=yv, bR=bR: e.tensor_copy(out=yv, in_=bR[:]), [bR], [(y, (n, pc))])

                items = []
                for n in range(10):
                    porder = list(range(npc)) if dirs[0] == 0 else list(range(npc - 1, -1, -1))
                    items += [(n, pc) for pc in porder]
                wfetch(0)
                wfetch(1)
                inproj(0)
                stageA(items[0][0], items[0][1], 0)
                stageA(items[1][0], items[1][1], 1)
                for i, (n, pc) in enumerate(items):
                    if i % npc == 0:
                        if n + 1 < 10:
                            inproj(n + 1)
                        if n + 2 < 10:
                            wfetch(n + 2)
                        if final:
                            gateproj(n)
                    if i + 2 < len(items):
                        stageA(items[i + 2][0], items[i + 2][1], i + 2)
                    stageB(n, pc, i)
            l = 1
            linear_fm([wview(rwout_d, s * 128, 128) for s in range(8)], lambda kc, t: y[:, kc, t * 512:(t + 1) * 512],
                      lambda t: [y], nt, resid_epi(xT, xkeyf, l, 0, cv))

        try:
            stage_check(1)
            with ExitStack() as P:
                xT = K.tile(P, 'xTp', [128, 8, TP], F32)
                hT = K.tile(P, 'hTp', [128, 8, TP], BF16)
                xk = lambda t: (xT, ('x', t))
                with ExitStack() as sc:
                    alloc_slabs(sc, 4)
                    tmp = dict(sq=K.tile(sc, 'sq', [128, 8, 512], BF16), rstd=K.tile(sc, 'rstd', [128, 512], F32),
                               u=[K.tile(sc, f'u{i}', [128, 512], F32) for i in range(2)],
                               P=[K.tile(sc, f'P{i}', [128, 512], BF16) for i in range(4)], pi=0,
                               rd=K.tile(sc, 'rd', [128, 512], F32), ob=K.tile(sc, 'ob', [128, 256], F32),
                               sqb=K.tile(sc, 'sqb', [128, 256], BF16), rs=K.tile(sc, 'rs', [128, 256], F32))
                    stg = dict(stg=[K.tile(sc, f'stg{i}', [128, 1024], F32) for i in range(2)])
                    QK = K.tile(sc, 'QKp', [128, 9, TP], BF16)
                    VAt = K.tile(sc, 'VAp', [128, 8, 2, 128], BF16)
                    VBt = K.tile(sc, 'VBp', [128, 8, 512], BF16)
                    OT = K.tile(sc, 'OTp', [128, 8, TP], BF16)
                    kvs = [K.tile(sc, 'kvs0', [128, 8, 512], F32)] * 2
                    load_xT(stg, xp_d, 0, 8, xT, 0, 'x')
                    stage_check(1.1)
                    for t in range(2):
                        norm_mod(tmp, xT[:, :, t * 512:(t + 1) * 512], xk(t), lambda c, t=t: hT[:, c, t * 512:(t + 1) * 512], (hT, t), 0, 0, 0)
                    QBz = K.tile(sc, 'QBzp', [128, 4, 2, TP], BF16)
                    K.op('pool', lambda e: e.memset(QBz[:], 0.0), [], [QBz])

                    def epi_qk(c, t, ps):
                        if 5 <= c <= 8:
                            for hh in range(2):
                                evac(QBz[hh * 64:(hh + 1) * 64, c - 5, hh, t * 512:(t + 1) * 512], ps[hh * 64:(hh + 1) * 64, :], [ps], [(QBz, (c, t, hh))])
                        else:
                            cq = c if c < 5 else c - 4
                            evac(QK[:, cq, t * 512:(t + 1) * 512], ps[:, :], [ps], [(QK, (cq, t))])
                    stage_check(1.2)
                    linear_fm([wview(winx_d, QA, 512), wview(winx_d, KA, 128), wview(winx_d, QB, 512), wview(winx_d, KB, 512)],
                              lambda kc, t: hT[:, kc, t * 512:(t + 1) * 512], lambda t: [(hT, t)], 2, epi_qk)
                    stage_check(1.3)
                    for si, (c0, n, o0) in enumerate([(512, 256, 0), (1280, 512, 256), (1792, 512, 768)]):
                        if str(si) not in os.environ.get('KSI', '012'):
                            continue
                        sl, (wv,) = wload([wview(win_d, c0, n)])
                        for tt in range(8):
                            ps = psA()
                            for kc in range(8):
                                mm(ps[:, 0:n], hT[:, kc, tt * 128:(tt + 1) * 128], wv[:, kc, :], kc == 0, kc == 7, [sl, (hT, tt // 4)], [ps], kc == 7)
                            kv = kvs[si % 2]
                            evac(kv[:, tt, 0:n], ps[:, 0:n], [ps], [(kv, tt)])
                            if si == 0:
                                for u_ in range(2):
                                    K.op('dve', lambda e, tt=tt, ps=ps, u_=u_: e.tensor_copy(
                                        out=VAt[:, tt].rearrange("p g (u d) -> p g u d", u=2)[:, :, u_, :],
                                        in_=ps[:, 128:256].rearrange("p (g d) -> p g d", g=2)), [ps], [(VAt, (tt, u_))])
                            if si == 2:
                                evac(VBt[:, tt, :], ps[:, 0:512], [ps], [(VBt, tt)])
                        kv = kvs[si % 2]
                        for (d_, lo, w_) in ([(nak_d, 0, 128), (nav_d, 128, 128)] if si == 0 else [((nbk_d, nbv_d)[si - 1], 0, 512)]):
                            if os.environ.get('KNOOUT') == '1':
                                continue
                            if os.environ.get('KNOOUT') == '2':
                                for t8 in range(8):
                                    K.dma('sp', d_[t8 * 128:(t8 + 1) * 128, :], kv[:, t8, lo:lo + w_], r=[kv], is_out=True)
                                continue
                            K.dma('sp', d_.rearrange("(t p) c -> p t c", p=128), kv[:, :, lo:lo + w_], r=[kv], is_out=True)
                    stage_check(1.4)
                    for sq_ in range(4):
                        for g in range(2):
                            for qb in range(2):
                                q0 = sq_ * 256 + qb * 128
                                kbl = []
                                for kb in range(2):
                                    k0 = sq_ * 256 + kb * 128
                                    kbl.append((QK[g * 64:(g + 1) * 64, 4, k0:k0 + 128], QK, VAt[:, sq_ * 2 + kb, g, :], VAt, None))
                                attn_A_unit(tmp, QK[g * 64:(g + 1) * 64, 0:4, q0:q0 + 128], QK, kbl, g, OT, q0, esink)
                        for hb in range(4 if STAGE >= 1.5 else 0):
                            q0 = sq_ * 256
                            kbl = []
                            for kb in range(2):
                                k0 = sq_ * 256 + kb * 128
                                kbl.append((QK[:, 5 + hb, k0:k0 + 128], QK[:, 5 + hb, k0:k0 + 128], QK,
                                            VBt[:, sq_ * 2 + kb, hb * 128:(hb + 1) * 128], VBt))
                            attn_B_unit(tmp, QBz[:, hb, 0, q0:q0 + 256], QBz[:, hb, 1, q0:q0 + 256], QBz, kbl, OT, 4 + hb, q0, neglam, subw)
                        flush_epi()
                    flush_epi()
                    stage_check(1.6)
                    linear_fm([wview(wout_d, s * 512, 512) for s in range(2)], lambda kc, t: OT[:, kc, t * 512:(t + 1) * 512],
                              lambda t: [OT], 2, resid_epi(xT, xk, 0, 0, 0))
                    K.barrier()
                with ExitStack() as sc:
                    alloc_slabs(sc, 4)
                    tmp = dict(sq=K.tile(sc, 'sq2', [128, 8, 512], BF16), rstd=K.tile(sc, 'rstd2', [128, 512], F32),
                               u=[K.tile(sc, f'u2{i}', [128, 512], F32) for i in range(2)])
                    stage_check(2)
                    mlp(sc, xT, xk, TP, 0, 0, tmp, hT, 1)
                    K.barrier()
                with ExitStack() as sc:
                    alloc_slabs(sc, 3)
                    tmp = dict(sq=K.tile(sc, 'sq3', [128, 8, 512], BF16), rstd=K.tile(sc, 'rstd3', [128, 512], F32),
                               u=[K.tile(sc, f'u3{i}', [128, 512], F32) for i in range(2)])
                    for t in range(2):
                        norm_mod(tmp, xT[:, :, t * 512:(t + 1) * 512], xk(t), lambda c, t=t: hT[:, c, t * 512:(t + 1) * 512], (hT, t), 1, 0, 0)
                    stage_check(3)
                    stp = K.tile(sc, 'stp', [128, 2, 4, 10], F32)
                    rec_layer(sc, xT, xk, TP, hT, 0, 256, [([0, 1], True)], 'p', rwa_p_d, rwx_p_d, tmp, state_out=stp)
                    ps = psA()
                    tr(ps[0:80, 0:128], stp[:].rearrange("p d s n -> p (d s n)"), [stp], [ps], True)
                    sto = K.tile(sc, 'sto', [80, 128], F32)
                    evac(sto[:], ps[0:80, 0:128], [ps], [sto], eng='dve')
                    for d, dd in enumerate([nsf_d, nsb_d]):
                        for s_ in range(4):
                            K.dma('sp', dd[s_].rearrange("(n p) -> n p", p=128), sto[d * 40 + s_ * 10:d * 40 + s_ * 10 + 10, :], r=[sto], is_out=True)
                    K.barrier()
                with ExitStack() as sc:
                    alloc_slabs(sc, 4)
                    tmp = dict(sq=K.tile(sc, 'sq4', [128, 8, 512], BF16), rstd=K.tile(sc, 'rstd4', [128, 512], F32),
                               u=[K.tile(sc, f'u4{i}', [128, 512], F32) for i in range(2)], yn=K.tile(sc, 'yn4', [128, 8, 512], F32))
                    stage_check(4)
                    mlp(sc, xT, xk, TP, 1, 0, tmp, hT, 1)
                    ostg = [K.tile(sc, f'ostg{i}', [128, 1024], F32) for i in range(2)]
                    final_out(tmp, xT, xk, TP, yp_d, ostg)
                    K.barrier()

            class TlV(Tl):
                def __init__(s, base_ap, name):
                    s.t = None
                    s.base = base_ap
                    s.name = name
                    s.subs = {}

                def __getitem__(s, idx):
                    return s.base[idx]

            with ExitStack() as S:
                stage_check(5)
                X64 = K.tile(S, 'X64', [128, 8, 4096], BF16)
                hT = X64
                with ExitStack() as S1:
                    OT = K.tile(S1, 'OTs', [128, 8, TS], BF16)
                    with ExitStack() as sc:
                        tmp = dict(sq=K.tile(sc, 'sq5', [128, 8, 512], BF16), rstd=K.tile(sc, 'rstd5', [128, 512], F32),
                                   u=[K.tile(sc, f'u5{i}', [128, 512], F32) for i in range(2)])
                        stg = dict(stg=[K.tile(sc, f'stg5{i}', [128, 1024], F32) for i in range(2)])
                        xt2 = [K.tile(sc, f'xt5{i}', [128, 8, 512], F32) for i in range(2)]
                        for t in range(8):
                            xt = xt2[t % 2]
                            load_xT(stg, xso_d if t < 4 else xst_d, (t % 4) * 512, 4, xt, 0, 'x')
                            norm_mod(tmp, xt[:, :, :], xt, lambda c, t=t: hT[:, c, t * 512:(t + 1) * 512], (hT, t), 0, 0, 1)
                        K.barrier()
                    hk = lambda t: [(hT, t)]
                    hs = lambda kc, t: hT[:, kc, t * 512:(t + 1) * 512]
                    with ExitStack() as sc:
                        alloc_slabs(sc, 3)
                        rp = dict(cs=[K.tile(sc, f'cs{i}', [128, 2, 512], F32) for i in range(2)],
                                  t1=[K.tile(sc, f't1{i}', [128, 512], F32) for i in range(2)],
                                  t2=[K.tile(sc, f't2{i}', [128, 512], F32) for i in range(2)])
                        tmp = dict(P=[K.tile(sc, f'P5{i}', [128, 512], BF16) for i in range(4)], pi=0,
                                   rd=K.tile(sc, 'rd5', [128, 512], F32), ob=K.tile(sc, 'ob5', [128, 256], F32),
                                   sqb=K.tile(sc, 'sqb5', [128, 256], BF16), rs=K.tile(sc, 'rs5', [128, 256], F32))
                        cstg = K.tile(sc, 'cstg', [128, 2, 512], F32)
                        ri = {'i': 0}

                        def rope_proj(src_w, src_wr, ncol, tiles, dst):
                            sl, (wv,) = wload([src_w])
                            slr, (wvr,) = wload([src_wr])
                            for j in range(ncol // 128):
                                for t in tiles:
                                    i = ri['i'] % 2
                                    ri['i'] += 1
                                    cs, t1, t2 = rp['cs'][i], rp['t1'][i], rp['t2'][i]
                                    K.dma('sp', cs[:, 0, :], cos_d[:, t * 512:(t + 1) * 512], w=[(cs, 0)])
                                    K.dma('sp', cs[:, 1, :], sin_d[:, t * 512:(t + 1) * 512], w=[(cs, 1)])
                                    ps, psr = psA(), psA()
                                    for kc in range(8):
                                        mm(ps[:, :], wv[:, kc, j * 128:(j + 1) * 128], hs(kc, t), kc == 0, kc == 7, [sl] + hk(t), [ps], kc == 7)
                                    for kc in range(8):
                                        mm(psr[:, :], wvr[:, kc, j * 128:(j + 1) * 128], hs(kc, t), kc == 0, kc == 7, [slr] + hk(t), [psr], kc == 7)
                                    K.op('dve', lambda e, ps=ps, cs=cs, t1=t1: e.tensor_tensor(out=t1[:], in0=ps[:, :], in1=cs[:, 0, :], op=ALU.mult), [ps, (cs, 0)], [t1])
                                    K.op('dve', lambda e, psr=psr, cs=cs, t2=t2: e.tensor_tensor(out=t2[:], in0=psr[:, :], in1=cs[:, 1, :], op=ALU.mult), [psr, (cs, 1)], [t2])
                                    d_ap, d_key = dst(j, t)
                                    if isinstance(d_ap, tuple):
                                        for hh in range(2):
                                            K.op('pool', lambda e, t1=t1, t2=t2, d_ap=d_ap, hh=hh: e.tensor_tensor(
                                                out=d_ap[hh], in0=t1[hh * 64:(hh + 1) * 64, :], in1=t2[hh * 64:(hh + 1) * 64, :], op=ALU.add), [t1, t2], [(d_key[0], (d_key[1], hh))])
                                    else:
                                        K.op('pool', lambda e, t1=t1, t2=t2, d_ap=d_ap: e.tensor_tensor(out=d_ap, in0=t1[:], in1=t2[:], op=ALU.add), [t1, t2], [d_key])

                        def ctx_kT(c_d, col0, dstK, dkey):
                            ps = psA()
                            for b in range(2):
                                K.dma('sp', cstg[:, b, 0:128], c_d[b * 128:(b + 1) * 128, col0:col0 + 128], w=[(cstg, b)])
                                tr(ps[:, b * 128:(b + 1) * 128], cstg[:, b, 0:128], [(cstg, b)], [ps], b == 1)
                            evac(dstK, ps[:, 0:256], [ps], [dkey])

                        with ExitStack() as ga:
                            stage_check(6)
                            QAs = K.tile(ga, 'QAs', [128, 4, TS], BF16)
                            KAs = K.tile(ga, 'KAs', [128, 4352], BF16)
                            VAs = K.tile(ga, 'VAs', [128, 34, 2, 128], BF16)
                            rope_proj(wview(winx_d, QA, 512), wview(winx_d, QAR, 512), 512, range(4),
                                      lambda j, t: (QAs[:, j, t * 512:(t + 1) * 512], (QAs, (j, t))))
                            rope_proj(wview(winx_d, KA, 128), wview(winx_d, KAR, 128), 128, range(8),
                                      lambda j, t: (KAs[:, 256 + t * 512:256 + (t + 1) * 512], (KAs, t)))
                            ctx_kT(cak_d, 0, KAs[:, 0:256], (KAs, 'c'))
                            sl, (wv,) = wload([wview(winx_d, VA, 128)])
                            for blk in range(32):
                                ps = psA()
                                for kc in range(8):
                                    mm(ps[:, 0:128], hT[:, kc, blk * 128:(blk + 1) * 128], wv[:, kc, :], kc == 0, kc == 7, [sl, (hT, blk // 4)], [ps], kc == 7)
                                for u_ in range(2):
                                    K.op('dve', lambda e, blk=blk, ps=ps, u_=u_: e.tensor_copy(
                                        out=VAs[:, 2 + blk].rearrange("p g (u d) -> p g u d", u=2)[:, :, u_, :],
                                        in_=ps[:, 0:128].rearrange("p (g d) -> p g d", g=2)), [ps], [(VAs, (2 + blk, u_))])
                            K.dma('sp', cstg[:, :, 0:128], cav_d.rearrange("(b p) c -> p b c", p=128), w=[cstg])
                            for b in range(2):
                                for u_ in range(2):
                                    K.op('dve', lambda e, b=b, u_=u_: e.tensor_copy(
                                        out=VAs[:, b].rearrange("p g (u d) -> p g u d", u=2)[:, :, u_, :],
                                        in_=cstg[:, b, 0:128].rearrange("p (g d) -> p g d", g=2)), [cstg], [(VAs, (b, u_))])
                            for g in range(2):
                                for i in range(16):
                                    kbl = []
                                    blks = [(0, None), (1, None)] + ([(2 + i - 1, 0)] if i > 0 else []) + [(2 + i, None), (2 + i + 1, 1)]
                                    for (b, mi) in blks:
                                        kbl.append((KAs[g * 64:(g + 1) * 64, b * 128:(b + 1) * 128], KAs, VAs[:, b, g, :], VAs, mi))
                                    attn_A_unit(tmp, QAs[g * 64:(g + 1) * 64, 0:4, i * 128:(i + 1) * 128], QAs, kbl, g, OT, i * 128, esink)
                            K.barrier()
                        with ExitStack() as gb_:
                            stage_check(7)
                            QBs = K.tile(gb_, 'QBs', [128, 2, TS], BF16)
                            K.op('pool', lambda e: e.memset(QBs[:], 0.0), [], [QBs])
                            KBs = K.tile(gb_, 'KBs', [128, 4352], BF16)
                            VBs = K.tile(gb_, 'VBs', [128, 34, 128], BF16)
                            for hb in range(4):
                                rope_proj(wview(winx_d, QB + hb * 128, 128), wview(winx_d, QBR + hb * 128, 128), 128, range(4),
                                          lambda j, t: ((QBs[0:64, 0, t * 512:(t + 1) * 512], QBs[64:128, 1, t * 512:(t + 1) * 512]), (QBs, t)))
                                rope_proj(wview(winx_d, KB + hb * 128, 128), wview(winx_d, KBR + hb * 128, 128), 128, range(8),
                                          lambda j, t: (KBs[:, 256 + t * 512:256 + (t + 1) * 512], (KBs, t)))
                                ctx_kT(cbk_d, hb * 128, KBs[:, 0:256], (KBs, 'c'))
                                sl, (wv,) = wload([wview(winx_d, VB + hb * 128, 128)])
                                for blk in range(32):
                                    ps = psA()
                                    for kc in range(8):
                                        mm(ps[:, 0:128], hT[:, kc, blk * 128:(blk + 1) * 128], wv[:, kc, :], kc == 0, kc == 7, [sl, (hT, blk // 4)], [ps], kc == 7)
                                    evac(VBs[:, 2 + blk, :], ps[:, 0:128], [ps], [(VBs, 2 + blk)])
                                K.dma('sp', cstg[:, :, 0:128], cbv_d.rearrange("(b p) c -> p b c", p=128)[:, :, hb * 128:(hb + 1) * 128], w=[cstg])
                                K.op('dve', lambda e: e.tensor_copy(out=VBs[:, 0:2, :], in_=cstg[:, :, 0:128]), [cstg], [(VBs, 'c')])
                                for qt in range(8):
                                    q0 = qt * 256
                                    kbl = [(KBs[:, b * 128:(b + 1) * 128], KBs[:, b * 128:(b + 1) * 128], KBs, VBs[:, b, :], VBs) for b in range(34)]
                                    attn_B_unit(tmp, QBs[:, 0, q0:q0 + 256], QBs[:, 1, q0:q0 + 256], QBs, kbl, OT, 4 + hb, q0, neglam, subw)
                            flush_epi()
                            K.barrier()
                        K.barrier()
                    stage_check(8)
                    xT = TlV(X64.t[:].bitcast(F32), 'xTs')
                    xk = lambda t: (xT, ('x', t))
                    with ExitStack() as sc:
                        alloc_slabs(sc, 4)
                        stg = dict(stg=[K.tile(sc, f'stg6{i}', [128, 1024], F32) for i in range(2)])
                        load_xT(stg, xso_d, 0, 16, xT, 0, 'x')
                        linear_fm([wview(wout_d, s * 512, 512) for s in range(2)], lambda kc, t: OT[:, kc, t * 512:(t + 1) * 512],
                                  lambda t: [OT], 4, resid_epi(xT, xk, 0, 0, 1))
                        K.barrier()
                hT2 = K.tile(S, 'hT2', [128, 8, TS], BF16)
                with ExitStack() as sc:
                    alloc_slabs(sc, 4)
                    tmp = dict(sq=K.tile(sc, 'sq7', [128, 8, 512], BF16), rstd=K.tile(sc, 'rstd7', [128, 512], F32),
                               u=[K.tile(sc, f'u7{i}', [128, 512], F32) for i in range(2)])
                    stage_check(9)
                    mlp(sc, xT, xk, TS, 0, 1, tmp, hT2, 4)
                    for t in range(4):
                        norm_mod(tmp, xT[:, :, t * 512:(t + 1) * 512], xk(t), lambda c, t=t: hT2[:, c, t * 512:(t + 1) * 512], (hT2, t), 1, 0, 1)
                    K.barrier()
                with ExitStack() as sc:
                    alloc_slabs(sc, 3, 2048)
                    stage_check(10)
                    stS = K.tile(sc, 'stS', [128, 10], F32)
                    sdn = K.tile(sc, 'sdn', [128, 10], F32)
                    exch = dict(xsrc1=Tl(None, 'xsrc1'), xdst1=Tl(None, 'xdst1'), xsrc2=Tl(None, 'xsrc2'), xdst2=Tl(None, 'xdst2'), stS=stS, sdn=sdn)
                    rec_layer(sc, xT, xk, TS, hT2, 1, TS, [([0], False), ([1], True)], 's', rwa_s_d, rwx_s_d, None, exch=exch)
                    K.barrier()
                with ExitStack() as sc:
                    alloc_slabs(sc, 4)
                    tmp = dict(sq=K.tile(sc, 'sq8', [128, 8, 512], BF16), rstd=K.tile(sc, 'rstd8', [128, 512], F32),
                               u=[K.tile(sc, f'u8{i}', [128, 512], F32) for i in range(2)])
                    stage_check(11)
                    mlp(sc, xT, xk, TS, 1, 1, tmp, hT2, 4)
                    K.barrier()
                with ExitStack() as sc:
                    tmp = dict(sq=K.tile(sc, 'sq9', [128, 8, 512], BF16), rstd=K.tile(sc, 'rstd9', [128, 512], F32),
                               yn=K.tile(sc, 'yn9', [128, 8, 512], F32))
                    ostg = [K.tile(sc, f'ostg9{i}', [128, 1024], F32) for i in range(2)]
                    final_out(tmp, xT, xk, TS, ys_d, ostg)
                    K.barrier()

        except StopBuild:
            pass
        K.finish()
    return nc


_ROT = np.concatenate([np.arange(16, 32), np.arange(0, 16), np.arange(48, 64), np.arange(32, 48)])


def _host_prepare(inp):
    f = np.float32
    g = lambda k: np.asarray(inp[k], dtype=f)
    SPL, NS = sp_layout()
    w_in = g('att_w_in')[0]
    qa_cols = np.concatenate([np.concatenate([np.arange(c * 64, c * 64 + 64), np.arange((4 + c) * 64, (4 + c) * 64 + 64)]) for c in range(4)])

    def rot_cols(cols):
        cols = np.asarray(cols).reshape(-1, 64)
        return cols[:, _ROT].reshape(-1)
    ka_cols = np.arange(512, 640); va_cols = np.arange(640, 768)
    qb_cols = np.arange(768, 1280); kb_cols = np.arange(1280, 1792); vb_cols = np.arange(1792, 2304)
    ext = np.concatenate([qa_cols, rot_cols(qa_cols), ka_cols, rot_cols(ka_cols), qb_cols, rot_cols(qb_cols),
                          kb_cols, rot_cols(kb_cols), va_cols, vb_cols])
    w_in_x = np.ascontiguousarray(w_in[:, ext])
    shared = dict(w_ada=g('w_ada'), w_mlp1=g('w_mlp1'), w_mlp2=g('w_mlp2'), w_in=np.ascontiguousarray(w_in), w_in_x=w_in_x,
                  w_out=np.ascontiguousarray(g('att_w_out')[0]), rec_w_in=np.ascontiguousarray(g('rec_w_in')[0]),
                  rec_w_out=np.ascontiguousarray(g('rec_w_out')[0]),
                  rwa_p=np.ascontiguousarray(g('rec_w_a')[0]), rwx_p=np.ascontiguousarray(g('rec_w_x')[0]),
                  ident=np.eye(128, dtype=f))
    kk = np.arange(128)[:, None]; qq = np.arange(128)[None, :]
    m = np.stack([np.tile((kk >= qq).astype(f), (1, 4)), np.tile((kk <= qq).astype(f), (1, 4))], axis=1)
    shared['masks'] = np.ascontiguousarray(m)
    p = np.arange(128); d = p % 64
    axis = d // 32; ii = d % 16; half = (d % 32) // 16
    inv = (10000.0 ** (-np.arange(16, dtype=np.float64) / 16))
    cw = g('rec_conv_w')[0]
    cw5_nat = np.concatenate([np.zeros((1, 1280), f), cw], 0)
    cw5_ref = np.concatenate([cw[::-1], np.zeros((1, 1280), f)], 0)
    pm = lambda v: np.ascontiguousarray(np.asarray(v, f).reshape(-1, 128).T)
    xp, xs = g('x_prompt'), g('x_sample')
    maps = []
    for i in range(8):
        b, hf = i // 2, i % 2
        loc = np.arange(4096) if hf == 0 else np.arange(4095, -1, -1)
        mp = dict(shared)
        mp['xp'] = np.ascontiguousarray(xp[4 * i:4 * i + 4].reshape(TP, 1024))
        mp['xs_own'] = np.ascontiguousarray(xs[b, loc[:2048]])
        mp['xs_oth'] = np.ascontiguousarray(xs[b, loc[2048:]])
        pos = np.stack([loc // 64, loc % 64], 0).astype(np.float64)
        ang = pos[axis][:, :] * inv[ii][:, None]
        mp['cos'] = np.cos(ang).astype(f)
        mp['sin'] = (np.sin(ang) * np.where(half == 0, -1.0, 1.0)[:, None]).astype(f)
        mp['cak'] = np.ascontiguousarray(g('cache_a_k')[b, 0].reshape(256, 128))
        mp['cav'] = np.ascontiguousarray(g('cache_a_v')[b, 0].reshape(256, 128))
        mp['cbk'] = np.ascontiguousarray(g('cache_b_k')[b, 0].reshape(256, 512))
        mp['cbv'] = np.ascontiguousarray(g('cache_b_v')[b, 0].reshape(256, 512))
        dsel = [0, 1] if hf == 0 else [1, 0]
        mp['rwa_s'] = np.ascontiguousarray(g('rec_w_a')[0][dsel])
        mp['rwx_s'] = np.ascontiguousarray(g('rec_w_x')[0][dsel])
        sp = np.zeros((128, NS), f)

        def put(name, arr):
            o, w = SPL[name]
            arr = np.asarray(arr, f)
            assert arr.shape == (128, w), (name, arr.shape, w)
            sp[:, o:o + w] = arr
        put('g1', np.concatenate([pm(g('norm1')[l]) for l in range(2)], 1))
        put('g2', np.concatenate([pm(g('norm2')[l]) for l in range(2)], 1))
        put('gf', pm(g('final_norm')))
        put('bada', np.concatenate([pm(g('b_ada')[l]) for l in range(2)], 1))
        put('convw_p', np.concatenate([pm(cw5_nat[k]) for k in range(5)], 1))
        put('convw_s', np.concatenate([pm((cw5_nat if hf == 0 else cw5_ref)[k]) for k in range(5)], 1))
        put('convb', pm(g('rec_conv_b')[0]))
        for nm, key in [('ba', 'rec_b_a'), ('bx', 'rec_b_x'), ('lam', 'rec_lam')]:
            v = g(key)[0]
            put(nm + '_p', np.concatenate([pm(v[0]), pm(v[1])], 1))
            put(nm + '_s', np.concatenate([pm(v[dsel[0]]), pm(v[dsel[1]])], 1))
        put('subln', g('att_subln')[0].reshape(128, 1))
        put('state_up', pm((g('state_fwd') if hf == 0 else g('state_bwd'))[b, 0]))
        put('sink', np.tile(g('att_sink')[0][None, :], (128, 1)))
        put('lamqk', np.tile(g('att_lam_qk')[0].reshape(1, 256), (128, 1)))
        put('sel', np.tile(np.array([[0.0, 1.0]] if hf == 0 else [[1.0, 0.0]], f), (128, 1)))
        cT = np.stack([pm(g('c_ctx')), pm(g('c')[b])], 2).reshape(128, 16)
        put('cT', cT)
        mp['sp'] = sp
        maps.append(mp)
    return maps


_CACHE = {}


def kernel(**inputs):
    if 'nc' not in _CACHE:
        _CACHE['nc'] = build_program()
    nc = _CACHE['nc']
    maps = _host_prepare(inputs)
    res = run_bass_kernel_spmd(nc, maps, core_ids=list(range(8)))
    R = res.results
    f = np.float32
    y_prompt = np.concatenate([R[i]['yp'].reshape(4, 256, 1024) for i in range(8)], 0).astype(f)
    y_sample = np.zeros((4, 4096, 1024), f)
    for i in range(8):
        b, hf = i // 2, i % 2
        ys = R[i]['ys']
        if hf == 0:
            y_sample[b, :2048] = ys
        else:
            y_sample[b, 2048:] = ys[::-1]
    nak = np.concatenate([R[i]['nak'].reshape(4, 1, 256, 2, 64) for i in range(8)], 0).astype(f)
    nav = np.concatenate([R[i]['nav'].reshape(4, 1, 256, 2, 64) for i in range(8)], 0).astype(f)
    nbk = np.concatenate([R[i]['nbk'].reshape(4, 1, 256, 4, 128) for i in range(8)], 0).astype(f)
    nbv = np.concatenate([R[i]['nbv'].reshape(4, 1, 256, 4, 128) for i in range(8)], 0).astype(f)
    nsf = np.concatenate([R[i]['nsf'].reshape(4, 1, 1280) for i in range(8)], 0).astype(f)
    nsb = np.concatenate([R[i]['nsb'].reshape(4, 1, 1280) for i in range(8)], 0).astype(f)
    return (y_prompt, y_sample, nak, nav, nbk, nbv, nsf, nsb)
```

```python
import numpy as np
from contextlib import ExitStack
import concourse.bass as bass
import concourse.mybir as mybir
from concourse.bass_utils import run_bass_kernel_spmd

F32, BF16 = mybir.dt.float32, mybir.dt.bfloat16
AF = mybir.ActivationFunctionType
ALU = mybir.AluOpType
NDS = 6
import os
NO_CC = os.environ.get('KNO_CC') == '1'
STAGE = float(os.environ.get('KSTAGE', '99'))


class StopBuild(Exception):
    pass


DEAD = [False]


def stage_check(k):
    if STAGE < k:
        DEAD[0] = True
EPS = 1e-6
SCALE = 0.125
LAM_INIT = 0.8 - 0.6 * 1.0
TP, TS = 1024, 2048
QA, QAR, KA, KAR, QB, QBR, KB, KBR, VA, VB, WX = 0, 512, 1024, 1152, 1280, 1792, 2304, 2816, 3328, 3456, 3968


def sp_layout():
    ents = [('g1', 16), ('g2', 16), ('gf', 8), ('bada', 96), ('convw_p', 50), ('convw_s', 50), ('convb', 10),
            ('ba_p', 20), ('bx_p', 20), ('lam_p', 20), ('ba_s', 20), ('bx_s', 20), ('lam_s', 20),
            ('subln', 1), ('state_up', 10), ('sink', 8), ('lamqk', 256), ('sel', 2), ('cT', 16)]
    off, d = 0, {}
    for n, w in ents:
        d[n] = (off, w)
        off += w
    return d, off


class Dep:
    __slots__ = ('w', 'r')

    def __init__(s):
        s.w = None
        s.r = {}


class Tl:
    def __init__(s, t, name):
        s.t = t
        s.name = name
        s.subs = {}

    def __getitem__(s, idx):
        return s.t[idx]


class KB_:
    def __init__(s, nc, es):
        s.nc, s.es = nc, es
        s.eng = {}
        for n in ['pe', 'act', 'dve', 'pool', 'sp']:
            s.eng[n] = dict(ops=[], sem=es.enter_context(nc.semaphore('s_' + n)), cnt=0, known={})
        s.dq = {}
        for q in ['sp', 'pool', 'act']:
            s.dq[q] = dict(sems=[es.enter_context(nc.semaphore(f'd_{q}{i}')) for i in range(NDS)], n=0)
        s.out_toks = []
        s.tiles = []
        s.cc_sem = es.enter_context(nc.semaphore('s_cc'))
        s.cc_n = 0
        s.H = {'pe': nc.tensor, 'act': nc.scalar, 'dve': nc.vector, 'pool': nc.gpsimd, 'sp': nc.sync}
        es.enter_context(nc.Block())

    def _emit(s, eng, waits, fn, inc):
        h = s.H[eng]
        for sem, val in waits:
            h.wait_ge(sem, val)
        if fn is not None:
            ins = fn(h)
            if inc is not None:
                ins.then_inc(inc[0], inc[1])

    def tile(s, es, name, shape, dt):
        t = Tl(es.enter_context(s.nc.sbuf_tensor('t_' + name, list(shape), dt)), name)
        return t

    def _conf(s, tl, sub):
        if sub is None:
            return list(tl.subs.values())
        return [tl.subs[k] for k in (sub, None) if k in tl.subs]

    @staticmethod
    def _ks(key):
        return key if isinstance(key, tuple) else (key, None)

    def _deps(s, r, w, tok):
        need = []
        for key in r:
            tl, sub = s._ks(key)
            for d in s._conf(tl, sub):
                if d.w:
                    need.append(d.w)
                if getattr(tl, 'psum', False):
                    need.extend(t for t in d.r.values() if t[0] is not tok[0])
        for key in w:
            tl, sub = s._ks(key)
            for d in s._conf(tl, sub):
                if d.w:
                    need.append(d.w)
                need.extend(d.r.values())
        for key in r:
            tl, sub = s._ks(key)
            d = tl.subs.setdefault(sub, Dep())
            d.r[tok[0].num] = tok
        for key in w:
            tl, sub = s._ks(key)
            if sub is None:
                tl.subs = {}
            d = tl.subs.setdefault(sub, Dep())
            d.w = tok
            d.r = {}
        return need

    def _need(s, eng, toks):
        kn = s.eng[eng]['known']
        waits = []
        for sem, val in toks:
            if kn.get(sem.num, 0) < val:
                kn[sem.num] = val
                waits.append((sem, val))
        return waits

    def op(s, eng, fn, r=(), w=(), inc=True):
        if DEAD[0]:
            return None
        e = s.eng[eng]
        tok = (e['sem'], e['cnt'] + 1)
        need = s._deps(r, w, tok)
        need = [t for t in need if not (t[0] is e['sem'] and (eng == 'pe' or t[1] > e['cnt']))]
        waits = s._need(eng, need)
        s._emit(eng, waits, fn, (e['sem'], 1) if inc else None)
        if inc:
            e['cnt'] += 1
        return tok

    def dma(s, q, out, in_, r=(), w=(), is_out=False, fn=None, **kw):
        if DEAD[0]:
            return None
        dq = s.dq[q]
        n = dq['n']
        sem = dq['sems'][n % NDS]
        val = 16 * (n // NDS + 1)
        dq['n'] += 1
        tok = (sem, val)
        need = s._deps(r, w, tok)
        if n >= NDS:
            need.append((sem, val - 16))
        waits = s._need(q, need)
        if fn is None:
            fn = lambda e: e.dma_start(out=out, in_=in_, **kw)
        s._emit(q, waits, fn, (sem, 16))
        if is_out:
            s.out_toks.append(tok)
        return tok

    def cc(s, fn, r=(), w=()):
        if DEAD[0]:
            return None
        s.cc_n += 1
        tok = (s.cc_sem, s.cc_n)
        need = s._deps(r, w, tok)
        waits = s._need('pool', need)
        h = s.H['pool']
        for sem, val in waits:
            h.wait_ge(sem, val)
        fn(h).then_inc(s.cc_sem)
        return tok

    def all_toks(s):
        toks = [(e['sem'], e['cnt']) for e in s.eng.values() if e['cnt'] > 0]
        if s.cc_n > 0:
            toks.append((s.cc_sem, s.cc_n))
        for dq in s.dq.values():
            n = dq['n']
            for i in range(min(n, NDS)):
                last = ((n - 1 - i) // NDS) * NDS + i if False else None
            for i, sem in enumerate(dq['sems']):
                cnt = (n - i + NDS - 1) // NDS if n > i else 0
                if cnt > 0:
                    toks.append((sem, 16 * cnt))
        return toks

    def barrier(s):
        if DEAD[0]:
            return
        toks = s.all_toks()
        for n in s.eng:
            waits = s._need(n, [t for t in toks if not (t[0] is s.eng[n]['sem'])])
            if waits:
                s._emit(n, waits, None, None)

    def finish(s):
        waits = s._need('sp', s.all_toks())
        s._emit('sp', waits, None, None)

    def emit(s):
        with s.nc.Block() as block:
            def mk(name):
                def f(e):
                    for waits, fn, inc in s.eng[name]['ops']:
                        for sem, val in waits:
                            e.wait_ge(sem, val)
                        if fn is not None:
                            ins = fn(e)
                            if inc is not None:
                                ins.then_inc(inc[0], inc[1])
                return f
            block.tensor(mk('pe'))
            block.scalar(mk('act'))
            block.vector(mk('dve'))
            block.gpsimd(mk('pool'))
            block.sync(mk('sp'))


def build_program():
    DEAD[0] = False
    nc = bass.Bass("TRN2", target_bir_lowering=False)
    SPL, NS = sp_layout()

    def din(name, shape):
        return nc.dram_tensor(name, list(shape), F32, kind="ExternalInput").ap()

    def dout(name, shape):
        return nc.dram_tensor(name, list(shape), F32, kind="ExternalOutput").ap()

    xp_d = din('xp', [TP, 1024]); xso_d = din('xs_own', [TS, 1024]); xst_d = din('xs_oth', [TS, 1024])
    cos_d = din('cos', [128, 4096]); sin_d = din('sin', [128, 4096])
    cak_d = din('cak', [256, 128]); cav_d = din('cav', [256, 128]); cbk_d = din('cbk', [256, 512]); cbv_d = din('cbv', [256, 512])
    sp_d = din('sp', [128, NS]); ident_d = din('ident', [128, 128]); mask_d = din('masks', [128, 2, 512])
    wada_d = din('w_ada', [2, 1024, 6144]); w1_d = din('w_mlp1', [2, 1024, 4096]); w2_d = din('w_mlp2', [2, 4096, 1024])
    win_d = din('w_in', [1024, 2304]); winx_d = din('w_in_x', [1024, WX]); wout_d = din('w_out', [1024, 1024])
    rwin_d = din('rec_w_in', [1024, 2560]); rwout_d = din('rec_w_out', [1280, 1024])
    rwa_p_d = din('rwa_p', [2, 10, 128, 128]); rwx_p_d = din('rwx_p', [2, 10, 128, 128])
    rwa_s_d = din('rwa_s', [2, 10, 128, 128]); rwx_s_d = din('rwx_s', [2, 10, 128, 128])
    yp_d = dout('yp', [TP, 1024]); ys_d = dout('ys', [TS, 1024])
    nak_d = dout('nak', [TP, 128]); nav_d = dout('nav', [TP, 128]); nbk_d = dout('nbk', [TP, 512]); nbv_d = dout('nbv', [TP, 512])
    nsf_d = dout('nsf', [4, 1280]); nsb_d = dout('nsb', [4, 1280])
    xsrc1 = nc.dram_tensor('xsrc1', [128, 128], F32).ap()
    xdst1 = nc.dram_tensor('xdst1', [128, 128], F32).ap()
    xsrc2 = nc.dram_tensor('xsrc2', [128, 128], F32).ap()
    xdst2 = nc.dram_tensor('xdst2', [128, 128], F32).ap()
    PAIRS = [[0, 1], [2, 3], [4, 5], [6, 7]]

    with ExitStack() as es:
        K = KB_(nc, es)
        PS = [Tl(es.enter_context(nc.psum_tensor(f'ps{i}', [128, 512], F32)), f'ps{i}') for i in range(8)]
        for p_ in PS:
            p_.psum = True
        ring = {'a': 0, 'b': 0}

        def psA():
            ring['a'] = (ring['a'] + 1) % 4
            return PS[ring['a']]

        def psB():
            ring['b'] = (ring['b'] + 1) % 4
            return PS[4 + ring['b']]

        ident = K.tile(es, 'ident', [128, 128], F32)
        ones = K.tile(es, 'ones', [128, 128], BF16)
        masks = K.tile(es, 'masks', [128, 2, 512], F32)
        spt = K.tile(es, 'spt', [128, NS], F32)
        modv = K.tile(es, 'modv', [128, 2, 48, 2], F32)
        Amod = K.tile(es, 'Amod', [128, 2, 2, 2, 8], F32)
        cons = K.tile(es, 'cons', [128, 64], F32)
        K.dma('sp', ident[:], ident_d, w=[ident])
        K.dma('sp', masks[:], mask_d, w=[masks])
        K.dma('sp', spt[:], sp_d, w=[spt])
        K.op('dve', lambda e: e.memset(ones[:], 1.0), w=[ones])

        def SP(name, a=0, b=None):
            o, w = SPL[name]
            b = w if b is None else b
            return spt[:, o + a:o + b]

        flip = {'i': 0}

        def evac(out, in_, r, w, scale=1.0, bias=0.0, eng=None):
            if eng is None:
                flip['i'] ^= 1
                eng = 'act' if flip['i'] else 'dve'
            if eng == 'act':
                K.op('act', lambda e: e.activation(out=out, in_=in_, func=AF.Identity, bias=bias, scale=scale), r, w)
            else:
                K.op('dve', lambda e: e.tensor_copy(out=out, in_=in_), r, w)

        def mm(out, lhsT, rhs, start, stop, r, w, last):
            K.op('pe', lambda e: e.matmul(out, lhsT, rhs, start=start, stop=stop), r, w, inc=last)

        def tr(out, in_, r, w, last):
            K.op('pe', lambda e: e.transpose(out, in_, ident[:]), list(r) + [ident], w, inc=last)

        slabs = []
        slab_i = {'i': 0}

        def alloc_slabs(scope, n, size=4096):
            slabs.clear()
            slab_i['size'] = size
            for i in range(n):
                slabs.append(K.tile(scope, f'slab{i}_{len(K.tiles)}', [128, size], BF16))
                K.tiles.append(None)
            slab_i['i'] = 0

        def wload(parts):
            sl = slabs[slab_i['i'] % len(slabs)]
            slab_i['i'] += 1
            views, off = [], 0
            for j, src in enumerate(parts):
                kc, n = src.shape[1], src.shape[2]
                v = sl[:, off:off + kc * n].rearrange("p (k n) -> p k n", k=kc)
                K.dma('pool', v, src, w=[(sl, j)] if len(parts) > 1 else [sl])
                views.append(v)
                off += kc * n
            assert off <= slab_i['size']
            return sl, views

        def wview(w2d, c0, n, k0=0, kc=None):
            v = w2d.rearrange("(k p) n -> p k n", p=128)
            kc = v.shape[1] - k0 if kc is None else kc
            return v[:, k0:k0 + kc, c0:c0 + n]

        with ExitStack() as sc:
            alloc_slabs(sc, 4)
            csil = K.tile(sc, 'csil', [128, 8, 2], BF16)
            K.op('act', lambda e: e.activation(out=csil[:], in_=SP('cT').rearrange("p (k j) -> p k j", j=2), func=AF.Silu),
                 [spt], [csil])
            for l in range(2):
                ps = psA()
                for blk in range(12):
                    sl, (wv,) = wload([wview(wada_d[l], blk * 512, 512)])
                    for j in range(4):
                        ch = blk * 4 + j
                        for kc in range(8):
                            mm(ps[:, ch * 2:ch * 2 + 2], wv[:, kc, j * 128:(j + 1) * 128], csil[:, kc, :],
                               kc == 0, kc == 7, [sl, csil], [ps], kc == 7)
                for cv in range(2):
                    K.op('dve', lambda e, l=l, cv=cv, ps=ps: e.tensor_tensor(
                        out=modv[:, l, :, cv], in0=ps[:, 0:96].rearrange("p (c j) -> p c j", j=2)[:, :, cv],
                        in1=SP('bada', l * 48, l * 48 + 48), op=ALU.add), [ps, spt], [(modv, (l, cv))])
                for nrm in range(2):
                    for cv in range(2):
                        g = SP('g1' if nrm == 0 else 'g2', l * 8, l * 8 + 8)
                        K.op('dve', lambda e, l=l, nrm=nrm, cv=cv, g=g: e.scalar_tensor_tensor(
                            out=Amod[:, l, nrm, cv, :], in0=modv[:, l, nrm * 24 + 8:nrm * 24 + 16, cv], scalar=1.0,
                            in1=g, op0=ALU.add, op1=ALU.mult), [(modv, (l, cv)), spt], [(Amod, (l, nrm, cv))])
            K.barrier()

        def MA(l, nrm, cv, c):
            return Amod[:, l, nrm, cv, c:c + 1]

        def MB(l, nrm, cv, c):
            return modv[:, l, nrm * 24 + c, cv:cv + 1]

        def MG(l, nrm, cv, c):
            return modv[:, l, nrm * 24 + 16 + c, cv:cv + 1]

        def load_xT(scope_tiles, x_d, t0, ntt, dst, dst_t0, T_key):
            stg = scope_tiles['stg']
            for tt in range(ntt):
                st = stg[tt % 2]
                K.dma('sp', st[:], x_d[t0 + tt * 128:t0 + (tt + 1) * 128, :], w=[st])
                for hb in range(2):
                    ps = psA()
                    for j in range(4):
                        c = hb * 4 + j
                        tr(ps[:, j * 128:(j + 1) * 128], st[:, c * 128:(c + 1) * 128], [st], [ps], j == 3)
                    o = dst_t0 + tt * 128
                    evac(dst[:, hb * 4:hb * 4 + 4, o:o + 128], ps[:, :].rearrange("p (c t) -> p c t", c=4),
                         [ps], [(dst, (T_key, o // 512))])

        def norm_stats(tmp, xsrc, xkey, n=512):
            sq, rstd = tmp['sq'], tmp['rstd']
            K.op('act', lambda e: e.activation(out=sq[:, :, 0:n], in_=xsrc, func=AF.Square), [xkey], [sq])
            ps = psA()
            for c in range(8):
                mm(ps[:, 0:n], ones[:], sq[:, c, 0:n], c == 0, c == 7, [ones, sq], [ps], c == 7)
            K.op('act', lambda e: e.activation(out=rstd[:, 0:n], in_=ps[:, 0:n], func=AF.Ln, bias=cons[:, 20:21], scale=1.0 / 1024),
                 [ps, cons], [rstd])
            K.op('act', lambda e: e.activation(out=rstd[:, 0:n], in_=rstd[:, 0:n], func=AF.Exp, scale=-0.5), [rstd], [rstd])
            return rstd

        def norm_mod(tmp, xsrc, xkey, hdst, hkey, l, nrm, cv, n=512):
            rstd = norm_stats(tmp, xsrc, xkey, n)
            u = tmp['u']
            for c in range(8):
                uc = u[c % 2]
                K.op('dve', lambda e, c=c, uc=uc: e.scalar_tensor_tensor(
                    out=uc[:, 0:n], in0=xsrc[:, c, :], scalar=MA(l, nrm, cv, c), in1=rstd[:, 0:n],
                    op0=ALU.mult, op1=ALU.mult), [xkey, rstd, Amod], [uc])
                K.op('act', lambda e, c=c, uc=uc: e.activation(out=hdst(c), in_=uc[:, 0:n], func=AF.Identity,
                                                               bias=MB(l, nrm, cv, c), scale=1.0), [uc, modv], [hkey])

        wq_plan, wq_ready = [], []

        def wq_top_up():
            while wq_plan and len(wq_ready) < len(slabs) - 1:
                wq_ready.append(wload([wq_plan.pop(0)]))

        def wq_get():
            wq_top_up()
            r_ = wq_ready.pop(0)
            wq_top_up()
            return r_

        def linear_fm(wparts, hsrc, hkeys, ntile, epi, tile0=0, planned=False):
            ci = 0
            if not planned:
                assert not wq_plan and not wq_ready
                wq_plan.extend(wparts)
            for src in wparts:
                sl, (wv,) = wq_get()
                kcn, n = src.shape[1], src.shape[2]
                for j in range(n // 128):
                    for t in range(tile0, tile0 + ntile):
                        ps = psA()
                        for kc in range(kcn):
                            mm(ps[:, :], wv[:, kc, j * 128:(j + 1) * 128], hsrc(kc, t), kc == 0, kc == kcn - 1,
                               [sl] + hkeys(t), [ps], kc == kcn - 1)
                        epi(ci, t, ps)
                    ci += 1

        def resid_epi(xT, xkey, l, nrm, cv):
            def epi(c, t, ps):
                K.op('dve', lambda e: e.scalar_tensor_tensor(
                    out=xT[:, c, t * 512:(t + 1) * 512], in0=ps[:, :], scalar=MG(l, nrm, cv, c),
                    in1=xT[:, c, t * 512:(t + 1) * 512], op0=ALU.mult, op1=ALU.add), [ps, modv, xkey(t)], [xkey(t)])
            return epi

        def mlp(scope, xT, xkeyf, T, l, cv, tmp, hT, nhalf):
            nt = T // 512
            for t in range(nt):
                norm_mod(tmp, xT[:, :, t * 512:(t + 1) * 512], xkeyf(t),
                         lambda c, t=t: hT[:, c, t * 512:(t + 1) * 512], (hT, t), l, 1, cv)
            HC = 32 // nhalf
            hid = K.tile(scope, f'hid{l}{cv}', [128, HC, T], BF16)
            rl = [K.tile(scope, f'rl{l}{cv}{i}', [128, 512], F32) for i in range(2)]
            ncw = min(512, 4096 // HC)
            assert not wq_plan and not wq_ready
            for half in range(nhalf):
                wq_plan.extend([wview(w1_d[l], half * HC * 128 + s * 512, 512) for s in range(HC // 4)])
                wq_plan.extend([wview(w2_d[l], c * ncw, ncw, k0=half * HC, kc=HC) for c in range(1024 // ncw)])
            for half in range(nhalf):
                def epi1(c, t, ps):
                    r_ = rl[(c + t) % 2]
                    K.op('act', lambda e: e.activation(out=r_[:], in_=ps[:, :], func=AF.Relu), [ps], [r_])
                    K.op('pool', lambda e: e.tensor_tensor(out=hid[:, c, t * 512:(t + 1) * 512], in0=r_[:], in1=r_[:],
                                                           op=ALU.mult), [r_], [(hid, (c, t))])
                linear_fm([wview(w1_d[l], half * HC * 128 + s * 512, 512) for s in range(HC // 4)],
                          lambda kc, t: hT[:, kc, t * 512:(t + 1) * 512], lambda t: [(hT, t)], nt, epi1, planned=True)
                linear_fm([wview(w2_d[l], c * ncw, ncw, k0=half * HC, kc=HC) for c in range(1024 // ncw)],
                          lambda kc, t: hid[:, kc, t * 512:(t + 1) * 512], lambda t: [hid], nt,
                          resid_epi(xT, xkeyf, l, 1, cv), planned=True)

        def final_out(tmp, xT, xkeyf, T, y_d, ostg):
            for t in range(T // 512):
                xs = xT[:, :, t * 512:(t + 1) * 512]
                rstd = norm_stats(tmp, xs, xkeyf(t))
                yn = tmp['yn']
                for c in range(8):
                    K.op('dve', lambda e, c=c: e.scalar_tensor_tensor(
                        out=yn[:, c, :], in0=xs[:, c, :], scalar=SP('gf', c, c + 1), in1=rstd[:, 0:512],
                        op0=ALU.mult, op1=ALU.mult), [xkeyf(t), rstd, spt], [(yn, c)])
                for tt in range(4):
                    og = ostg[tt % 2]
                    for hb in range(2):
                        ps = psA()
                        for j in range(4):
                            c = hb * 4 + j
                            tr(ps[:, j * 128:(j + 1) * 128], yn[:, c, tt * 128:(tt + 1) * 128], [(yn, c)], [ps], j == 3)
                        evac(og[:, hb * 512:(hb + 1) * 512], ps[:, :], [ps], [(og, hb)])
                    r0 = t * 512 + tt * 128
                    K.dma('sp', y_d[r0:r0 + 128, :], og[:], r=[og], is_out=True)

        def attn_A_unit(tmp, qrhs, qkey, kblocks, g, OT, otok0, esink):
            O, D = psB(), psB()
            Pts = tmp['P']
            nb = len(kblocks)
            Ss = [None] * nb

            def qk(i):
                S = psA()
                mm(S[:, :], kblocks[i][0], qrhs, True, True, [kblocks[i][1], qkey], [S], True)
                Ss[i] = S
            qk(0)
            if nb > 1:
                qk(1)
            for i in range(nb):
                if i + 2 < nb:
                    qk(i + 2)
                Pt = Pts[tmp['pi'] % len(Pts)]
                tmp['pi'] += 1
                S = Ss[i]
                K.op('act', lambda e, S=S, Pt=Pt: e.activation(out=Pt[:], in_=S[:, :], func=AF.Exp, scale=SCALE), [S], [Pt])
                mi = kblocks[i][4]
                if mi is not None:
                    K.op('dve', lambda e, Pt=Pt, mi=mi: e.tensor_tensor(out=Pt[:], in0=Pt[:], in1=masks[:, mi, :], op=ALU.mult),
                         [Pt, masks], [Pt])
                mm(O[:, :], kblocks[i][2], Pt[:], i == 0, i == nb - 1, [kblocks[i][3], Pt], [O], i == nb - 1)
                mm(D[:, :], ones[:], Pt[:], i == 0, i == nb - 1, [ones, Pt], [D], i == nb - 1)
            rd = tmp['rd']
            for j in range(4):
                h = g * 4 + j
                K.op('act', lambda e, j=j, h=h: e.activation(out=rd[:, j * 128:(j + 1) * 128], in_=D[:, j * 128:(j + 1) * 128],
                                                             func=AF.Ln, bias=esink[:, h:h + 1], scale=1.0),
                     [D, cons], [(rd, j)])
            K.op('act', lambda e: e.activation(out=rd[:], in_=rd[:], func=AF.Exp, scale=-1.0), [rd], [rd])
            for j in range(4):
                p0 = (j % 2) * 64
                ch = g * 2 + j // 2
                K.op('dve', lambda e, j=j, p0=p0, ch=ch: e.tensor_tensor(
                    out=OT[p0:p0 + 64, ch, otok0:otok0 + 128], in0=O[p0:p0 + 64, j * 128:(j + 1) * 128],
                    in1=rd[p0:p0 + 64, j * 128:(j + 1) * 128], op=ALU.mult), [O, rd], [(OT, (ch, p0, otok0))])

        def attn_B_unit(tmp, q1, q2, qkey, kblocks, OT, och, otok0, neglam, subw):
            O, D = psB(), psB()
            Pts = tmp['P']
            nb = len(kblocks)
            Ss = [None] * nb

            def qk(i):
                S = psA()
                mm(S[:, 0:256], kblocks[i][0], q1, True, True, [kblocks[i][2], qkey], [S], False)
                mm(S[:, 256:512], kblocks[i][1], q2, True, True, [kblocks[i][2], qkey], [S], True)
                Ss[i] = S
            qk(0)
            if nb > 1:
                qk(1)
            for i in range(nb):
                if i + 2 < nb:
                    qk(i + 2)
                Pt = Pts[tmp['pi'] % len(Pts)]
                tmp['pi'] += 1
                S = Ss[i]
                K.op('act', lambda e, S=S, Pt=Pt: e.activation(out=Pt[:], in_=S[:, :], func=AF.Exp, scale=SCALE), [S], [Pt])
                mm(O[:, :], kblocks[i][3], Pt[:], i == 0, i == nb - 1, [kblocks[i][4], Pt], [O], i == nb - 1)
                mm(D[:, :], ones[:], Pt[:], i == 0, i == nb - 1, [ones, Pt], [D], i == nb - 1)
            rd, ob, sqb, rs = tmp['rd'], tmp['ob'], tmp['sqb'], tmp['rs']
            K.op('act', lambda e: e.activation(out=rd[:], in_=D[:, :], func=AF.Ln), [D], [rd])
            K.op('act', lambda e: e.activation(out=rd[:], in_=rd[:], func=AF.Exp, scale=-1.0), [rd], [rd])
            K.op('dve', lambda e: e.tensor_tensor(out=rd[:], in0=O[:, :], in1=rd[:], op=ALU.mult), [O, rd], [rd])
            K.op('dve', lambda e: e.scalar_tensor_tensor(out=ob[:], in0=rd[:, 256:512], scalar=neglam, in1=rd[:, 0:256],
                                                         op0=ALU.mult, op1=ALU.add), [rd, cons], [ob])
            K.op('act', lambda e: e.activation(out=sqb[:], in_=ob[:], func=AF.Square), [ob], [sqb])
            ps = psA()
            mm(ps[:, 0:256], ones[:], sqb[:], True, True, [ones, sqb], [ps], True)
            K.op('act', lambda e: e.activation(out=rs[:], in_=ps[:, 0:256], func=AF.Ln, bias=cons[:, 20:21], scale=1.0 / 128), [ps, cons], [rs])
            K.op('act', lambda e: e.activation(out=rs[:], in_=rs[:], func=AF.Exp, scale=-0.5), [rs], [rs])
            K.op('dve', lambda e: e.scalar_tensor_tensor(out=OT[:, och, otok0:otok0 + 256], in0=ob[:], scalar=subw, in1=rs[:],
                                                         op0=ALU.mult, op1=ALU.mult), [ob, rs, cons], [(OT, (och, 0, otok0))])

        lq = SP('lamqk')
        K.op('dve', lambda e: e.memset(cons[:], 0.0), [], [cons])
        K.op('dve', lambda e: e.memset(cons[:, 20:21], EPS), [cons], [cons])
        prod = K.tile(es, 'lprod', [128, 128], F32)
        K.op('dve', lambda e: e.tensor_tensor(out=prod[:].rearrange("p (a d) -> p a d", a=2),
                                              in0=lq.rearrange("p (a b d) -> p a b d", a=2, b=2)[:, :, 0, :],
                                              in1=lq.rearrange("p (a b d) -> p a b d", a=2, b=2)[:, :, 1, :], op=ALU.mult),
             [spt], [prod])
        K.op('dve', lambda e: e.reduce_sum(out=cons[:, 16:18], in_=prod[:].rearrange("p (a d) -> p a d", a=2),
                                           axis=mybir.AxisListType.X), [prod, cons], [cons])
        K.op('act', lambda e: e.activation(out=cons[:, 18:20], in_=cons[:, 16:18], func=AF.Exp), [cons], [cons])
        K.op('act', lambda e: e.activation(out=cons[:, 0:8], in_=SP('sink'), func=AF.Exp), [spt, cons], [cons])
        K.op('dve', lambda e: e.tensor_tensor(out=cons[:, 8:9], in0=cons[:, 19:20], in1=cons[:, 18:19], op=ALU.subtract), [cons], [cons])
        K.op('dve', lambda e: e.tensor_scalar(out=cons[:, 8:9], in0=cons[:, 8:9], scalar1=-LAM_INIT, scalar2=None, op0=ALU.add), [cons], [cons])
        K.op('dve', lambda e: e.tensor_scalar(out=cons[:, 9:10], in0=SP('subln'), scalar1=1.0 - LAM_INIT, scalar2=None, op0=ALU.mult),
             [spt, cons], [cons])
        esink, neglam, subw = cons, cons[:, 8:9], cons[:, 9:10]

        def exchange(scope, nm, data_ap, data_keys, W, xsrc, xdst, ksrc, kdst, out_ap, out_key):
            cb = K.tile(scope, 'cb' + nm, [128, 128], F32)
            hg = K.tile(scope, 'hg' + nm, [128, 128], F32)
            K.op('dve', lambda e: e.memset(cb[:], 0.0), [], [cb])
            K.op('dve', lambda e: e.tensor_scalar(out=cb[:, 0:W], in0=data_ap, scalar1=SP('sel', 1, 2), scalar2=None, op0=ALU.mult),
                 list(data_keys) + [spt, cb], [cb])
            K.op('dve', lambda e: e.tensor_scalar(out=cb[:, 64:64 + W], in0=data_ap, scalar1=SP('sel', 0, 1), scalar2=None, op0=ALU.mult),
                 list(data_keys) + [spt, cb], [cb])
            K.dma('sp', xsrc, cb[:], r=[cb], w=[ksrc])
            if NO_CC:
                K.dma('pool', xdst, xsrc, r=[ksrc], w=[kdst])
            else:
                K.cc(lambda e: e.collective_compute("AllReduce", ALU.add, replica_groups=PAIRS, ins=[xsrc.opt()], outs=[xdst.opt()]),
                     r=[ksrc], w=[kdst])
            K.dma('sp', hg[:], xdst, r=[kdst], w=[hg])
            K.op('dve', lambda e: e.tensor_scalar(out=out_ap, in0=hg[:, 0:W], scalar1=SP('sel', 0, 1), scalar2=None, op0=ALU.mult),
                 [hg, spt], [out_key])
            K.op('dve', lambda e: e.scalar_tensor_tensor(out=out_ap, in0=hg[:, 64:64 + W], scalar=SP('sel', 1, 2), in1=out_ap,
                                                         op0=ALU.mult, op1=ALU.add), [hg, spt, out_key], [out_key])

        def rec_layer(scope, xT, xkeyf, T, hT, cv, segL, phases, sfx, rwa_d, rwx_d, tmp=None, exch=None, state_out=None):
            nseg = T // segL
            nt = T // 512
            PL = 512
            npc = T // PL
            xrp2 = [K.tile(scope, f'xrp{i}' + sfx, [128, nseg, segL + 4], F32) for i in range(2)]
            xc2 = [K.tile(scope, f'xc{i}' + sfx, [128, 512], F32) for i in range(3)]
            xcb2 = [K.tile(scope, f'xcb{i}' + sfx, [128, 512], BF16) for i in range(3)]
            gb = K.tile(scope, 'gb' + sfx, [128, T], BF16)
            bA2 = [K.tile(scope, f'bA{i}' + sfx, [128, PL], F32) for i in range(2)]
            bR2 = [K.tile(scope, f'bR{i}' + sfx, [128, PL], F32) for i in range(2)]
            bI2 = [K.tile(scope, f'bI{i}' + sfx, [128, PL], F32) for i in range(2)]
            y = K.tile(scope, 'y' + sfx, [128, 10, T], BF16)
            wg2 = [K.tile(scope, f'wg{i}' + sfx, [128, 4, 128], BF16) for i in range(3)]
            carry = K.tile(scope, 'carry' + sfx, [128, 4], F32)
            cl = K.tile(scope, 'cl' + sfx, [128, 2, 2, 10], F32)
            pfx = '_p' if sfx == 'p' else '_s'
            lam = SP('lam' + pfx).rearrange("p (d c) -> p d c", d=2)
            K.op('act', lambda e: e.activation(out=cl[:, :, 0, :], in_=lam, func=AF.Exp, scale=-1.0), [spt], [cl])
            K.op('act', lambda e: e.activation(out=cl[:, :, 0, :], in_=cl[:, :, 0, :], func=AF.Ln, bias=1.0, scale=1.0), [cl], [cl])
            K.op('dve', lambda e: e.tensor_scalar(out=cl[:, :, 1, :], in0=cl[:, :, 0, :], scalar1=-16.0, scalar2=None, op0=ALU.mult), [cl], [cl])
            K.op('dve', lambda e: e.tensor_scalar(out=cl[:, :, 0, :], in0=cl[:, :, 0, :], scalar1=-8.0, scalar2=None, op0=ALU.mult), [cl], [cl])
            for xr_ in xrp2:
                K.op('pool', lambda e, xr_=xr_: e.memset(xr_[:], 0.0), [], [xr_])
            cw = SP('convw' + pfx)
            halo = None
            if exch is not None:
                hl = K.tile(scope, 'hl', [128, 10, 2], F32)
                halo = K.tile(scope, 'halo', [128, 10, 2], F32)
                for n in range(10):
                    sl, (wv,) = wload([wview(rwin_d, 1280 + n * 128, 128)])
                    ps = psA()
                    for kc in range(8):
                        mm(ps[:, 0:2], wv[:, kc, :], hT[:, kc, T - 2:T], kc == 0, kc == 7, [sl, (hT, nt - 1)], [ps], kc == 7)
                    evac(hl[:, n, :], ps[:, 0:2], [ps], [(hl, n)], eng='dve')
                exchange(scope, 'x1', hl[:].rearrange("p a b -> p (a b)"), [hl], 20, xsrc1, xdst1, exch['xsrc1'], exch['xdst1'],
                         halo[:].rearrange("p a b -> p (a b)"), halo)
            gcount = {'i': 0}
            for (dirs, final) in phases:
                if exch is not None and final:
                    exchange(scope, 'x2', exch['stS'][:], [exch['stS']], 10, xsrc2, xdst2, exch['xsrc2'], exch['xdst2'],
                             exch['sdn'][:], exch['sdn'])
                pend = {}

                wq = {}

                def wfetch(n, dirs=dirs, final=final):
                    parts = [wview(rwin_d, 1280 + n * 128, 128)]
                    if final:
                        parts.append(wview(rwin_d, n * 128, 128))
                    sl, wvs = wload(parts)
                    slk = [(sl, j) for j in range(len(parts))] if len(parts) > 1 else [sl]
                    wg = wg2[n % 3]
                    gparts = []
                    for d in dirs:
                        gparts += [rwa_d[d, n].rearrange("c (o d) -> c o d", o=1), rwx_d[d, n].rearrange("c (o d) -> c o d", o=1)]
                    for j, gp in enumerate(gparts):
                        K.dma('pool', wg[:, j:j + 1, :], gp, w=[(wg, j)])
                    wq[n] = (sl, wvs, slk)

                def inproj(n, dirs=dirs, final=final):
                    xrp = xrp2[n % 2]
                    sl, wvs, slk = wq.pop(n)
                    wg = wg2[n % 3]
                    for t in range(nt):
                        ps = psA()
                        for kc in range(8):
                            mm(ps[:, :], wvs[0][:, kc, :], hT[:, kc, t * 512:(t + 1) * 512], kc == 0, kc == 7,
                               [slk[0], (hT, t)], [ps], kc == 7)
                        if segL >= 512:
                            sgi, o = (t * 512) // segL, (t * 512) % segL
                            evac(xrp[:, sgi, 2 + o:2 + o + 512], ps[:, :], [ps], [(xrp, t)], eng='act')
                        else:
                            ns = 512 // segL
                            evac(xrp[:, t * ns:(t + 1) * ns, 2:2 + segL], ps[:, :].rearrange("p (s l) -> p s l", s=ns), [ps], [(xrp, t)], eng='act')
                    if halo is not None:
                        K.op('dve', lambda e, n=n: e.tensor_copy(out=xrp[:, 0, 2 + segL:4 + segL], in_=halo[:, n, :][:, ::-1]), [halo], [(xrp, 'halo')])
                    pend[n] = (sl, wvs, slk)

                def gateproj(n):
                    sl, wvs, slk = pend[n]
                    for t in range(nt):
                        ps2 = psA()
                        for kc in range(8):
                            mm(ps2[:, :], wvs[1][:, kc, :], hT[:, kc, t * 512:(t + 1) * 512], kc == 0, kc == 7,
                               [slk[1], (hT, t)], [ps2], kc == 7)
                        K.op('act', lambda e, t=t, ps2=ps2: e.activation(out=gb[:, t * 512:(t + 1) * 512], in_=ps2[:, :],
                                                                         func=AF.Gelu_apprx_tanh), [ps2], [(gb, t)])

                def stageA(n, pc, k):
                    xrp = xrp2[n % 2]
                    xcp, xbp = xc2[k % 3], xcb2[k % 3]
                    c0 = pc * PL
                    if segL >= PL:
                        segs = [(c0 // segL, c0 % segL, PL, 0)]
                    else:
                        segs = [(c0 // segL + s_, 0, segL, s_ * segL) for s_ in range(PL // segL)]
                    for (sgi, o, L_, bo) in segs:
                        o_ = xcp[:, bo:bo + L_]
                        K.op('dve', lambda e, sgi=sgi, o=o, L_=L_, o_=o_: e.tensor_scalar(
                            out=o_, in0=xrp[:, sgi, o:o + L_], scalar1=cw[:, n:n + 1], scalar2=SP('convb', n, n + 1),
                            op0=ALU.mult, op1=ALU.add), [xrp, spt], [xcp])
                        for k_ in range(1, 5):
                            K.op('dve', lambda e, sgi=sgi, o=o, L_=L_, o_=o_, k_=k_: e.scalar_tensor_tensor(
                                out=o_, in0=xrp[:, sgi, o + k_:o + k_ + L_], scalar=cw[:, k_ * 10 + n:k_ * 10 + n + 1], in1=o_,
                                op0=ALU.mult, op1=ALU.add), [xrp, spt, xcp], [xcp])
                    K.op('pool', lambda e: e.tensor_copy(out=xbp[:], in_=xcp[:]), [xcp], [xbp])

                def stageB(n, pc, k, dirs=dirs, final=final):
                    wg = wg2[n % 3]
                    xcp, xbp = xc2[k % 3], xcb2[k % 3]
                    c0 = pc * PL
                    for di, d in enumerate(dirs):
                        up = (d == 0)
                        gi = gcount['i'] % 2
                        gcount['i'] += 1
                        bA, bR, bI = bA2[gi], bR2[gi], bI2[gi]
                        psr, psi = psA(), psA()
                        mm(psr[:, :], wg[:, 2 * di, :], xbp[:], True, True, [(wg, 2 * di), xbp], [psr], True)
                        mm(psi[:, :], wg[:, 2 * di + 1, :], xbp[:], True, True, [(wg, 2 * di + 1), xbp], [psi], True)
                        K.op('act', lambda e, psr=psr, bR=bR, d=d: e.activation(
                            out=bR[:], in_=psr[:, :], func=AF.Sigmoid, bias=SP('ba' + pfx, d * 10 + n, d * 10 + n + 1), scale=1.0),
                            [psr, spt], [bR])
                        K.op('act', lambda e, psi=psi, bI=bI, d=d: e.activation(
                            out=bI[:], in_=psi[:, :], func=AF.Sigmoid, bias=SP('bx' + pfx, d * 10 + n, d * 10 + n + 1), scale=1.0),
                            [psi, spt], [bI])
                        K.op('act', lambda e, d=d, bA=bA, bR=bR: e.activation(out=bA[:], in_=bR[:], func=AF.Exp, scale=cl[:, d, 0, n:n + 1]), [bR, cl], [bA])
                        K.op('act', lambda e, d=d, bR=bR: e.activation(out=bR[:], in_=bR[:], func=AF.Exp, scale=cl[:, d, 1, n:n + 1]), [bR, cl], [bR])
                        K.op('pool', lambda e, bI=bI: e.tensor_tensor(out=bI[:], in0=bI[:], in1=xcp[:], op=ALU.mult), [bI, xcp], [bI])
                        K.op('act', lambda e, bR=bR: e.activation(out=bR[:], in_=bR[:], func=AF.Sqrt, bias=1.0, scale=-1.0), [bR], [bR])
                        K.op('pool', lambda e, bI=bI, bR=bR: e.tensor_tensor(out=bI[:], in0=bI[:], in1=bR[:], op=ALU.mult), [bI, bR], [bI])
                        nsc = PL // segL if segL < PL else 1
                        L = PL // nsc
                        seg_first = (c0 % segL == 0) if up else ((c0 + PL) % segL == 0)
                        for sc_ in range(nsc):
                            lo = sc_ * L
                            if segL <= PL or (seg_first and exch is None):
                                init, ik = 0.0, []
                            elif seg_first:
                                init = (SP('state_up', n, n + 1) if up else exch['sdn'][:, n:n + 1])
                                ik = [spt if up else exch['sdn']]
                            else:
                                init, ik = carry[:, 0:1], [carry]
                            if up:
                                K.op('dve', lambda e, lo=lo, L=L, init=init, bA=bA, bR=bR, bI=bI: e.tensor_tensor_scan(
                                    out=bR[:, lo:lo + L], data0=bA[:, lo:lo + L], data1=bI[:, lo:lo + L], initial=init,
                                    op0=ALU.mult, op1=ALU.add), [bA, bI] + ik, [bR])
                            else:
                                K.op('dve', lambda e, lo=lo, L=L, init=init, bA=bA, bR=bR, bI=bI: e.tensor_tensor_scan(
                                    out=bR[:, lo:lo + L][:, ::-1], data0=bA[:, lo:lo + L][:, ::-1], data1=bI[:, lo:lo + L][:, ::-1],
                                    initial=init, op0=ALU.mult, op1=ALU.add), [bA, bI] + ik, [bR])
                        last_col = PL - 1 if up else 0
                        if segL > PL:
                            K.op('dve', lambda e, last_col=last_col, bR=bR: e.tensor_copy(out=carry[:, 0:1], in_=bR[:, last_col:last_col + 1]), [bR], [carry])
                            if exch is not None and up and pc == npc - 1:
                                K.op('dve', lambda e, last_col=last_col, bR=bR: e.tensor_copy(out=exch['stS'][:, n:n + 1], in_=bR[:, last_col:last_col + 1]),
                                     [bR], [exch['stS']])
                        elif state_out is not None:
                            lc = segL - 1 if up else 0
                            ns_ = PL // segL
                            s0 = c0 // segL
                            K.op('dve', lambda e, d=d, lc=lc, bR=bR, s0=s0, ns_=ns_: e.tensor_copy(
                                out=state_out[:, d, s0:s0 + ns_, n], in_=bR[:].rearrange("p (s l) -> p s l", l=segL)[:, :, lc]), [bR], [(state_out, (d, n, s0))])
                        yv = y[:, n, c0:c0 + PL]
                        first = (di == 0 and not (final and len(dirs) == 1))
                        lastd = final and di == len(dirs) - 1
                        if first and not lastd:
                            K.op('pool', lambda e, yv=yv, bR=bR: e.tensor_copy(out=yv, in_=bR[:]), [bR], [(y, (n, pc))])
                        else:
                            if not first:
                                K.op('pool', lambda e, yv=yv, bR=bR: e.tensor_tensor(out=bR[:], in0=bR[:], in1=yv, op=ALU.add), [bR, (y, (n, pc))], [bR])
                            if lastd:
                                K.op('dve', lambda e, yv=yv, c0=c0, bR=bR: e.tensor_tensor(out=yv, in0=bR[:], in1=gb[:, c0:c0 + PL], op=ALU.mult),
                                     [bR, gb], [(y, (n, pc))])
                            else:
                                K.op('pool', lambda e, yv=yv, bR=bR: e.tensor_copy(out=yv, in_=bR[:]), [bR], [(y, (n, pc))])

                items = []
                for n in range(10):
                    porder = list(range(npc)) if dirs[0] == 0 else list(range(npc - 1, -1, -1))
                    items += [(n, pc) for pc in porder]
                wfetch(0)
                wfetch(1)
                inproj(0)
                stageA(items[0][0], items[0][1], 0)
                stageA(items[1][0], items[1][1], 1)
                for i, (n, pc) in enumerate(items):
                    if i % npc == 0:
                        if n + 1 < 10:
                            inproj(n + 1)
                        if n + 2 < 10:
                            wfetch(n + 2)
                        if final:
                            gateproj(n)
                    if i + 2 < len(items):
                        stageA(items[i + 2][0], items[i + 2][1], i + 2)
                    stageB(n, pc, i)
            l = 1
            linear_fm([wview(rwout_d, s * 128, 128) for s in range(8)], lambda kc, t: y[:, kc, t * 512:(t + 1) * 512],
                      lambda t: [y], nt, resid_epi(xT, xkeyf, l, 0, cv))

        try:
            stage_check(1)
            with ExitStack() as P:
                xT = K.tile(P, 'xTp', [128, 8, TP], F32)
                hT = K.tile(P, 'hTp', [128, 8, TP], BF16)
                xk = lambda t: (xT, ('x', t))
                with ExitStack() as sc:
                    alloc_slabs(sc, 4)
                    tmp = dict(sq=K.tile(sc, 'sq', [128, 8, 512], BF16), rstd=K.tile(sc, 'rstd', [128, 512], F32),
                               u=[K.tile(sc, f'u{i}', [128, 512], F32) for i in range(2)],
                               P=[K.tile(sc, f'P{i}', [128, 512], BF16) for i in range(4)], pi=0,
                               rd=K.tile(sc, 'rd', [128, 512], F32), ob=K.tile(sc, 'ob', [128, 256], F32),
                               sqb=K.tile(sc, 'sqb', [128, 256], BF16), rs=K.tile(sc, 'rs', [128, 256], F32))
                    stg = dict(stg=[K.tile(sc, f'stg{i}', [128, 1024], F32) for i in range(2)])
                    QK = K.tile(sc, 'QKp', [128, 9, TP], BF16)
                    VAt = K.tile(sc, 'VAp', [128, 8, 2, 128], BF16)
                    VBt = K.tile(sc, 'VBp', [128, 8, 512], BF16)
                    OT = K.tile(sc, 'OTp', [128, 8, TP], BF16)
                    kvs = [K.tile(sc, 'kvs0', [128, 8, 512], F32)] * 2
                    load_xT(stg, xp_d, 0, 8, xT, 0, 'x')
                    stage_check(1.1)
                    for t in range(2):
                        norm_mod(tmp, xT[:, :, t * 512:(t + 1) * 512], xk(t), lambda c, t=t: hT[:, c, t * 512:(t + 1) * 512], (hT, t), 0, 0, 0)
                    QBz = K.tile(sc, 'QBzp', [128, 4, 2, TP], BF16)
                    K.op('pool', lambda e: e.memset(QBz[:], 0.0), [], [QBz])

                    def epi_qk(c, t, ps):
                        if 5 <= c <= 8:
                            for hh in range(2):
                                evac(QBz[hh * 64:(hh + 1) * 64, c - 5, hh, t * 512:(t + 1) * 512], ps[hh * 64:(hh + 1) * 64, :], [ps], [(QBz, (c, t, hh))])
                        else:
                            cq = c if c < 5 else c - 4
                            evac(QK[:, cq, t * 512:(t + 1) * 512], ps[:, :], [ps], [(QK, (cq, t))])
                    stage_check(1.2)
                    linear_fm([wview(winx_d, QA, 512), wview(winx_d, KA, 128), wview(winx_d, QB, 512), wview(winx_d, KB, 512)],
                              lambda kc, t: hT[:, kc, t * 512:(t + 1) * 512], lambda t: [(hT, t)], 2, epi_qk)
                    stage_check(1.3)
                    for si, (c0, n, o0) in enumerate([(512, 256, 0), (1280, 512, 256), (1792, 512, 768)]):
                        if str(si) not in os.environ.get('KSI', '012'):
                            continue
                        sl, (wv,) = wload([wview(win_d, c0, n)])
                        for tt in range(8):
                            ps = psA()
                            for kc in range(8):
                                mm(ps[:, 0:n], hT[:, kc, tt * 128:(tt + 1) * 128], wv[:, kc, :], kc == 0, kc == 7, [sl, (hT, tt // 4)], [ps], kc == 7)
                            kv = kvs[si % 2]
                            evac(kv[:, tt, 0:n], ps[:, 0:n], [ps], [(kv, tt)])
                            if si == 0:
                                for u_ in range(2):
                                    K.op('dve', lambda e, tt=tt, ps=ps, u_=u_: e.tensor_copy(
                                        out=VAt[:, tt].rearrange("p g (u d) -> p g u d", u=2)[:, :, u_, :],
                                        in_=ps[:, 128:256].rearrange("p (g d) -> p g d", g=2)), [ps], [(VAt, (tt, u_))])
                            if si == 2:
                                evac(VBt[:, tt, :], ps[:, 0:512], [ps], [(VBt, tt)])
                        kv = kvs[si % 2]
                        for (d_, lo, w_) in ([(nak_d, 0, 128), (nav_d, 128, 128)] if si == 0 else [((nbk_d, nbv_d)[si - 1], 0, 512)]):
                            if os.environ.get('KNOOUT') == '1':
                                continue
                            if os.environ.get('KNOOUT') == '2':
                                for t8 in range(8):
                                    K.dma('sp', d_[t8 * 128:(t8 + 1) * 128, :], kv[:, t8, lo:lo + w_], r=[kv], is_out=True)
                                continue
                            K.dma('sp', d_.rearrange("(t p) c -> p t c", p=128), kv[:, :, lo:lo + w_], r=[kv], is_out=True)
                    stage_check(1.4)
                    for sq_ in range(4):
                        for g in range(2):
                            for qb in range(2):
                                q0 = sq_ * 256 + qb * 128
                                kbl = []
                                for kb in range(2):
                                    k0 = sq_ * 256 + kb * 128
                                    kbl.append((QK[g * 64:(g + 1) * 64, 4, k0:k0 + 128], QK, VAt[:, sq_ * 2 + kb, g, :], VAt, None))
                                attn_A_unit(tmp, QK[g * 64:(g + 1) * 64, 0:4, q0:q0 + 128], QK, kbl, g, OT, q0, esink)
                        for hb in range(4 if STAGE >= 1.5 else 0):
                            q0 = sq_ * 256
                            kbl = []
                            for kb in range(2):
                                k0 = sq_ * 256 + kb * 128
                                kbl.append((QK[:, 5 + hb, k0:k0 + 128], QK[:, 5 + hb, k0:k0 + 128], QK,
                                            VBt[:, sq_ * 2 + kb, hb * 128:(hb + 1) * 128], VBt))
                            attn_B_unit(tmp, QBz[:, hb, 0, q0:q0 + 256], QBz[:, hb, 1, q0:q0 + 256], QBz, kbl, OT, 4 + hb, q0, neglam, subw)
                    stage_check(1.6)
                    linear_fm([wview(wout_d, s * 512, 512) for s in range(2)], lambda kc, t: OT[:, kc, t * 512:(t + 1) * 512],
                              lambda t: [OT], 2, resid_epi(xT, xk, 0, 0, 0))
                    K.barrier()
                with ExitStack() as sc:
                    alloc_slabs(sc, 4)
                    tmp = dict(sq=K.tile(sc, 'sq2', [128, 8, 512], BF16), rstd=K.tile(sc, 'rstd2', [128, 512], F32),
                               u=[K.tile(sc, f'u2{i}', [128, 512], F32) for i in range(2)])
                    stage_check(2)
                    mlp(sc, xT, xk, TP, 0, 0, tmp, hT, 1)
                    K.barrier()
                with ExitStack() as sc:
                    alloc_slabs(sc, 3)
                    tmp = dict(sq=K.tile(sc, 'sq3', [128, 8, 512], BF16), rstd=K.tile(sc, 'rstd3', [128, 512], F32),
                               u=[K.tile(sc, f'u3{i}', [128, 512], F32) for i in range(2)])
                    for t in range(2):
                        norm_mod(tmp, xT[:, :, t * 512:(t + 1) * 512], xk(t), lambda c, t=t: hT[:, c, t * 512:(t + 1) * 512], (hT, t), 1, 0, 0)
                    stage_check(3)
                    stp = K.tile(sc, 'stp', [128, 2, 4, 10], F32)
                    rec_layer(sc, xT, xk, TP, hT, 0, 256, [([0, 1], True)], 'p', rwa_p_d, rwx_p_d, tmp, state_out=stp)
                    ps = psA()
                    tr(ps[0:80, 0:128], stp[:].rearrange("p d s n -> p (d s n)"), [stp], [ps], True)
                    sto = K.tile(sc, 'sto', [80, 128], F32)
                    evac(sto[:], ps[0:80, 0:128], [ps], [sto], eng='dve')
                    for d, dd in enumerate([nsf_d, nsb_d]):
                        for s_ in range(4):
                            K.dma('sp', dd[s_].rearrange("(n p) -> n p", p=128), sto[d * 40 + s_ * 10:d * 40 + s_ * 10 + 10, :], r=[sto], is_out=True)
                    K.barrier()
                with ExitStack() as sc:
                    alloc_slabs(sc, 4)
                    tmp = dict(sq=K.tile(sc, 'sq4', [128, 8, 512], BF16), rstd=K.tile(sc, 'rstd4', [128, 512], F32),
                               u=[K.tile(sc, f'u4{i}', [128, 512], F32) for i in range(2)], yn=K.tile(sc, 'yn4', [128, 8, 512], F32))
                    stage_check(4)
                    mlp(sc, xT, xk, TP, 1, 0, tmp, hT, 1)
                    ostg = [K.tile(sc, f'ostg{i}', [128, 1024], F32) for i in range(2)]
                    final_out(tmp, xT, xk, TP, yp_d, ostg)
                    K.barrier()

            class TlV(Tl):
                def __init__(s, base_ap, name):
                    s.t = None
                    s.base = base_ap
                    s.name = name
                    s.subs = {}

                def __getitem__(s, idx):
                    return s.base[idx]

            with ExitStack() as S:
                stage_check(5)
                X64 = K.tile(S, 'X64', [128, 8, 4096], BF16)
                hT = X64
                with ExitStack() as S1:
                    OT = K.tile(S1, 'OTs', [128, 8, TS], BF16)
                    with ExitStack() as sc:
                        tmp = dict(sq=K.tile(sc, 'sq5', [128, 8, 512], BF16), rstd=K.tile(sc, 'rstd5', [128, 512], F32),
                                   u=[K.tile(sc, f'u5{i}', [128, 512], F32) for i in range(2)])
                        stg = dict(stg=[K.tile(sc, f'stg5{i}', [128, 1024], F32) for i in range(2)])
                        xt2 = [K.tile(sc, f'xt5{i}', [128, 8, 512], F32) for i in range(2)]
                        for t in range(8):
                            xt = xt2[t % 2]
                            load_xT(stg, xso_d if t < 4 else xst_d, (t % 4) * 512, 4, xt, 0, 'x')
                            norm_mod(tmp, xt[:, :, :], xt, lambda c, t=t: hT[:, c, t * 512:(t + 1) * 512], (hT, t), 0, 0, 1)
                        K.barrier()
                    hk = lambda t: [(hT, t)]
                    hs = lambda kc, t: hT[:, kc, t * 512:(t + 1) * 512]
                    with ExitStack() as sc:
                        alloc_slabs(sc, 3)
                        rp = dict(cs=[K.tile(sc, f'cs{i}', [128, 2, 512], F32) for i in range(2)],
                                  t1=[K.tile(sc, f't1{i}', [128, 512], F32) for i in range(2)],
                                  t2=[K.tile(sc, f't2{i}', [128, 512], F32) for i in range(2)])
                        tmp = dict(P=[K.tile(sc, f'P5{i}', [128, 512], BF16) for i in range(4)], pi=0,
                                   rd=K.tile(sc, 'rd5', [128, 512], F32), ob=K.tile(sc, 'ob5', [128, 256], F32),
                                   sqb=K.tile(sc, 'sqb5', [128, 256], BF16), rs=K.tile(sc, 'rs5', [128, 256], F32))
                        cstg = K.tile(sc, 'cstg', [128, 2, 512], F32)
                        ri = {'i': 0}

                        def rope_proj(src_w, src_wr, ncol, tiles, dst):
                            sl, (wv,) = wload([src_w])
                            slr, (wvr,) = wload([src_wr])
                            for j in range(ncol // 128):
                                for t in tiles:
                                    i = ri['i'] % 2
                                    ri['i'] += 1
                                    cs, t1, t2 = rp['cs'][i], rp['t1'][i], rp['t2'][i]
                                    K.dma('sp', cs[:, 0, :], cos_d[:, t * 512:(t + 1) * 512], w=[(cs, 0)])
                                    K.dma('sp', cs[:, 1, :], sin_d[:, t * 512:(t + 1) * 512], w=[(cs, 1)])
                                    ps, psr = psA(), psA()
                                    for kc in range(8):
                                        mm(ps[:, :], wv[:, kc, j * 128:(j + 1) * 128], hs(kc, t), kc == 0, kc == 7, [sl] + hk(t), [ps], kc == 7)
                                    for kc in range(8):
                                        mm(psr[:, :], wvr[:, kc, j * 128:(j + 1) * 128], hs(kc, t), kc == 0, kc == 7, [slr] + hk(t), [psr], kc == 7)
                                    K.op('dve', lambda e, ps=ps, cs=cs, t1=t1: e.tensor_tensor(out=t1[:], in0=ps[:, :], in1=cs[:, 0, :], op=ALU.mult), [ps, (cs, 0)], [t1])
                                    K.op('dve', lambda e, psr=psr, cs=cs, t2=t2: e.tensor_tensor(out=t2[:], in0=psr[:, :], in1=cs[:, 1, :], op=ALU.mult), [psr, (cs, 1)], [t2])
                                    d_ap, d_key = dst(j, t)
                                    if isinstance(d_ap, tuple):
                                        for hh in range(2):
                                            K.op('pool', lambda e, t1=t1, t2=t2, d_ap=d_ap, hh=hh: e.tensor_tensor(
                                                out=d_ap[hh], in0=t1[hh * 64:(hh + 1) * 64, :], in1=t2[hh * 64:(hh + 1) * 64, :], op=ALU.add), [t1, t2], [(d_key[0], (d_key[1], hh))])
                                    else:
                                        K.op('pool', lambda e, t1=t1, t2=t2, d_ap=d_ap: e.tensor_tensor(out=d_ap, in0=t1[:], in1=t2[:], op=ALU.add), [t1, t2], [d_key])

                        def ctx_kT(c_d, col0, dstK, dkey):
                            ps = psA()
                            for b in range(2):
                                K.dma('sp', cstg[:, b, 0:128], c_d[b * 128:(b + 1) * 128, col0:col0 + 128], w=[(cstg, b)])
                                tr(ps[:, b * 128:(b + 1) * 128], cstg[:, b, 0:128], [(cstg, b)], [ps], b == 1)
                            evac(dstK, ps[:, 0:256], [ps], [dkey])

                        with ExitStack() as ga:
                            stage_check(6)
                            QAs = K.tile(ga, 'QAs', [128, 4, TS], BF16)
                            KAs = K.tile(ga, 'KAs', [128, 4352], BF16)
                            VAs = K.tile(ga, 'VAs', [128, 34, 2, 128], BF16)
                            rope_proj(wview(winx_d, QA, 512), wview(winx_d, QAR, 512), 512, range(4),
                                      lambda j, t: (QAs[:, j, t * 512:(t + 1) * 512], (QAs, (j, t))))
                            rope_proj(wview(winx_d, KA, 128), wview(winx_d, KAR, 128), 128, range(8),
                                      lambda j, t: (KAs[:, 256 + t * 512:256 + (t + 1) * 512], (KAs, t)))
                            ctx_kT(cak_d, 0, KAs[:, 0:256], (KAs, 'c'))
                            sl, (wv,) = wload([wview(winx_d, VA, 128)])
                            for blk in range(32):
                                ps = psA()
                                for kc in range(8):
                                    mm(ps[:, 0:128], hT[:, kc, blk * 128:(blk + 1) * 128], wv[:, kc, :], kc == 0, kc == 7, [sl, (hT, blk // 4)], [ps], kc == 7)
                                for u_ in range(2):
                                    K.op('dve', lambda e, blk=blk, ps=ps, u_=u_: e.tensor_copy(
                                        out=VAs[:, 2 + blk].rearrange("p g (u d) -> p g u d", u=2)[:, :, u_, :],
                                        in_=ps[:, 0:128].rearrange("p (g d) -> p g d", g=2)), [ps], [(VAs, (2 + blk, u_))])
                            K.dma('sp', cstg[:, :, 0:128], cav_d.rearrange("(b p) c -> p b c", p=128), w=[cstg])
                            for b in range(2):
                                for u_ in range(2):
                                    K.op('dve', lambda e, b=b, u_=u_: e.tensor_copy(
                                        out=VAs[:, b].rearrange("p g (u d) -> p g u d", u=2)[:, :, u_, :],
                                        in_=cstg[:, b, 0:128].rearrange("p (g d) -> p g d", g=2)), [cstg], [(VAs, (b, u_))])
                            for g in range(2):
                                for i in range(16):
                                    kbl = []
                                    blks = [(0, None), (1, None)] + ([(2 + i - 1, 0)] if i > 0 else []) + [(2 + i, None), (2 + i + 1, 1)]
                                    for (b, mi) in blks:
                                        kbl.append((KAs[g * 64:(g + 1) * 64, b * 128:(b + 1) * 128], KAs, VAs[:, b, g, :], VAs, mi))
                                    attn_A_unit(tmp, QAs[g * 64:(g + 1) * 64, 0:4, i * 128:(i + 1) * 128], QAs, kbl, g, OT, i * 128, esink)
                            K.barrier()
                        with ExitStack() as gb_:
                            stage_check(7)
                            QBs = K.tile(gb_, 'QBs', [128, 2, TS], BF16)
                            K.op('pool', lambda e: e.memset(QBs[:], 0.0), [], [QBs])
                            KBs = K.tile(gb_, 'KBs', [128, 4352], BF16)
                            VBs = K.tile(gb_, 'VBs', [128, 34, 128], BF16)
                            for hb in range(4):
                                rope_proj(wview(winx_d, QB + hb * 128, 128), wview(winx_d, QBR + hb * 128, 128), 128, range(4),
                                          lambda j, t: ((QBs[0:64, 0, t * 512:(t + 1) * 512], QBs[64:128, 1, t * 512:(t + 1) * 512]), (QBs, t)))
                                rope_proj(wview(winx_d, KB + hb * 128, 128), wview(winx_d, KBR + hb * 128, 128), 128, range(8),
                                          lambda j, t: (KBs[:, 256 + t * 512:256 + (t + 1) * 512], (KBs, t)))
                                ctx_kT(cbk_d, hb * 128, KBs[:, 0:256], (KBs, 'c'))
                                sl, (wv,) = wload([wview(winx_d, VB + hb * 128, 128)])
                                for blk in range(32):
                                    ps = psA()
                                    for kc in range(8):
                                        mm(ps[:, 0:128], hT[:, kc, blk * 128:(blk + 1) * 128], wv[:, kc, :], kc == 0, kc == 7, [sl, (hT, blk // 4)], [ps], kc == 7)
                                    evac(VBs[:, 2 + blk, :], ps[:, 0:128], [ps], [(VBs, 2 + blk)])
                                K.dma('sp', cstg[:, :, 0:128], cbv_d.rearrange("(b p) c -> p b c", p=128)[:, :, hb * 128:(hb + 1) * 128], w=[cstg])
                                K.op('dve', lambda e: e.tensor_copy(out=VBs[:, 0:2, :], in_=cstg[:, :, 0:128]), [cstg], [(VBs, 'c')])
                                for qt in range(8):
                                    q0 = qt * 256
                                    kbl = [(KBs[:, b * 128:(b + 1) * 128], KBs[:, b * 128:(b + 1) * 128], KBs, VBs[:, b, :], VBs) for b in range(34)]
                                    attn_B_unit(tmp, QBs[:, 0, q0:q0 + 256], QBs[:, 1, q0:q0 + 256], QBs, kbl, OT, 4 + hb, q0, neglam, subw)
                            K.barrier()
                        K.barrier()
                    stage_check(8)
                    xT = TlV(X64.t[:].bitcast(F32), 'xTs')
                    xk = lambda t: (xT, ('x', t))
                    with ExitStack() as sc:
                        alloc_slabs(sc, 4)
                        stg = dict(stg=[K.tile(sc, f'stg6{i}', [128, 1024], F32) for i in range(2)])
                        load_xT(stg, xso_d, 0, 16, xT, 0, 'x')
                        linear_fm([wview(wout_d, s * 512, 512) for s in range(2)], lambda kc, t: OT[:, kc, t * 512:(t + 1) * 512],
                                  lambda t: [OT], 4, resid_epi(xT, xk, 0, 0, 1))
                        K.barrier()
                hT2 = K.tile(S, 'hT2', [128, 8, TS], BF16)
                with ExitStack() as sc:
                    alloc_slabs(sc, 4)
                    tmp = dict(sq=K.tile(sc, 'sq7', [128, 8, 512], BF16), rstd=K.tile(sc, 'rstd7', [128, 512], F32),
                               u=[K.tile(sc, f'u7{i}', [128, 512], F32) for i in range(2)])
                    stage_check(9)
                    mlp(sc, xT, xk, TS, 0, 1, tmp, hT2, 4)
                    for t in range(4):
                        norm_mod(tmp, xT[:, :, t * 512:(t + 1) * 512], xk(t), lambda c, t=t: hT2[:, c, t * 512:(t + 1) * 512], (hT2, t), 1, 0, 1)
                    K.barrier()
                with ExitStack() as sc:
                    alloc_slabs(sc, 3, 2048)
                    stage_check(10)
                    stS = K.tile(sc, 'stS', [128, 10], F32)
                    sdn = K.tile(sc, 'sdn', [128, 10], F32)
                    exch = dict(xsrc1=Tl(None, 'xsrc1'), xdst1=Tl(None, 'xdst1'), xsrc2=Tl(None, 'xsrc2'), xdst2=Tl(None, 'xdst2'), stS=stS, sdn=sdn)
                    rec_layer(sc, xT, xk, TS, hT2, 1, TS, [([0], False), ([1], True)], 's', rwa_s_d, rwx_s_d, None, exch=exch)
                    K.barrier()
                with ExitStack() as sc:
                    alloc_slabs(sc, 4)
                    tmp = dict(sq=K.tile(sc, 'sq8', [128, 8, 512], BF16), rstd=K.tile(sc, 'rstd8', [128, 512], F32),
                               u=[K.tile(sc, f'u8{i}', [128, 512], F32) for i in range(2)])
                    stage_check(11)
                    mlp(sc, xT, xk, TS, 1, 1, tmp, hT2, 4)
                    K.barrier()
                with ExitStack() as sc:
                    tmp = dict(sq=K.tile(sc, 'sq9', [128, 8, 512], BF16), rstd=K.tile(sc, 'rstd9', [128, 512], F32),
                               yn=K.tile(sc, 'yn9', [128, 8, 512], F32))
                    ostg = [K.tile(sc, f'ostg9{i}', [128, 1024], F32) for i in range(2)]
                    final_out(tmp, xT, xk, TS, ys_d, ostg)
                    K.barrier()

        except StopBuild:
            pass
        K.finish()
    return nc


_ROT = np.concatenate([np.arange(16, 32), np.arange(0, 16), np.arange(48, 64), np.arange(32, 48)])


def _host_prepare(inp):
    f = np.float32
    g = lambda k: np.asarray(inp[k], dtype=f)
    SPL, NS = sp_layout()
    w_in = g('att_w_in')[0]
    qa_cols = np.concatenate([np.concatenate([np.arange(c * 64, c * 64 + 64), np.arange((4 + c) * 64, (4 + c) * 64 + 64)]) for c in range(4)])

    def rot_cols(cols):
        cols = np.asarray(cols).reshape(-1, 64)
        return cols[:, _ROT].reshape(-1)
    ka_cols = np.arange(512, 640); va_cols = np.arange(640, 768)
    qb_cols = np.arange(768, 1280); kb_cols = np.arange(1280, 1792); vb_cols = np.arange(1792, 2304)
    ext = np.concatenate([qa_cols, rot_cols(qa_cols), ka_cols, rot_cols(ka_cols), qb_cols, rot_cols(qb_cols),
                          kb_cols, rot_cols(kb_cols), va_cols, vb_cols])
    w_in_x = np.ascontiguousarray(w_in[:, ext])
    shared = dict(w_ada=g('w_ada'), w_mlp1=g('w_mlp1'), w_mlp2=g('w_mlp2'), w_in=np.ascontiguousarray(w_in), w_in_x=w_in_x,
                  w_out=np.ascontiguousarray(g('att_w_out')[0]), rec_w_in=np.ascontiguousarray(g('rec_w_in')[0]),
                  rec_w_out=np.ascontiguousarray(g('rec_w_out')[0]),
                  rwa_p=np.ascontiguousarray(g('rec_w_a')[0]), rwx_p=np.ascontiguousarray(g('rec_w_x')[0]),
                  ident=np.eye(128, dtype=f))
    kk = np.arange(128)[:, None]; qq = np.arange(128)[None, :]
    m = np.stack([np.tile((kk >= qq).astype(f), (1, 4)), np.tile((kk <= qq).astype(f), (1, 4))], axis=1)
    shared['masks'] = np.ascontiguousarray(m)
    p = np.arange(128); d = p % 64
    axis = d // 32; ii = d % 16; half = (d % 32) // 16
    inv = (10000.0 ** (-np.arange(16, dtype=np.float64) / 16))
    cw = g('rec_conv_w')[0]
    cw5_nat = np.concatenate([np.zeros((1, 1280), f), cw], 0)
    cw5_ref = np.concatenate([cw[::-1], np.zeros((1, 1280), f)], 0)
    pm = lambda v: np.ascontiguousarray(np.asarray(v, f).reshape(-1, 128).T)
    xp, xs = g('x_prompt'), g('x_sample')
    maps = []
    for i in range(8):
        b, hf = i // 2, i % 2
        loc = np.arange(4096) if hf == 0 else np.arange(4095, -1, -1)
        mp = dict(shared)
        mp['xp'] = np.ascontiguousarray(xp[4 * i:4 * i + 4].reshape(TP, 1024))
        mp['xs_own'] = np.ascontiguousarray(xs[b, loc[:2048]])
        mp['xs_oth'] = np.ascontiguousarray(xs[b, loc[2048:]])
        pos = np.stack([loc // 64, loc % 64], 0).astype(np.float64)
        ang = pos[axis][:, :] * inv[ii][:, None]
        mp['cos'] = np.cos(ang).astype(f)
        mp['sin'] = (np.sin(ang) * np.where(half == 0, -1.0, 1.0)[:, None]).astype(f)
        mp['cak'] = np.ascontiguousarray(g('cache_a_k')[b, 0].reshape(256, 128))
        mp['cav'] = np.ascontiguousarray(g('cache_a_v')[b, 0].reshape(256, 128))
        mp['cbk'] = np.ascontiguousarray(g('cache_b_k')[b, 0].reshape(256, 512))
        mp['cbv'] = np.ascontiguousarray(g('cache_b_v')[b, 0].reshape(256, 512))
        dsel = [0, 1] if hf == 0 else [1, 0]
        mp['rwa_s'] = np.ascontiguousarray(g('rec_w_a')[0][dsel])
        mp['rwx_s'] = np.ascontiguousarray(g('rec_w_x')[0][dsel])
        sp = np.zeros((128, NS), f)

        def put(name, arr):
            o, w = SPL[name]
            arr = np.asarray(arr, f)
            assert arr.shape == (128, w), (name, arr.shape, w)
            sp[:, o:o + w] = arr
        put('g1', np.concatenate([pm(g('norm1')[l]) for l in range(2)], 1))
        put('g2', np.concatenate([pm(g('norm2')[l]) for l in range(2)], 1))
        put('gf', pm(g('final_norm')))
        put('bada', np.concatenate([pm(g('b_ada')[l]) for l in range(2)], 1))
        put('convw_p', np.concatenate([pm(cw5_nat[k]) for k in range(5)], 1))
        put('convw_s', np.concatenate([pm((cw5_nat if hf == 0 else cw5_ref)[k]) for k in range(5)], 1))
        put('convb', pm(g('rec_conv_b')[0]))
        for nm, key in [('ba', 'rec_b_a'), ('bx', 'rec_b_x'), ('lam', 'rec_lam')]:
            v = g(key)[0]
            put(nm + '_p', np.concatenate([pm(v[0]), pm(v[1])], 1))
            put(nm + '_s', np.concatenate([pm(v[dsel[0]]), pm(v[dsel[1]])], 1))
        put('subln', g('att_subln')[0].reshape(128, 1))
        put('state_up', pm((g('state_fwd') if hf == 0 else g('state_bwd'))[b, 0]))
        put('sink', np.tile(g('att_sink')[0][None, :], (128, 1)))
        put('lamqk', np.tile(g('att_lam_qk')[0].reshape(1, 256), (128, 1)))
        put('sel', np.tile(np.array([[0.0, 1.0]] if hf == 0 else [[1.0, 0.0]], f), (128, 1)))
        cT = np.stack([pm(g('c_ctx')), pm(g('c')[b])], 2).reshape(128, 16)
        put('cT', cT)
        mp['sp'] = sp
        maps.append(mp)
    return maps


_CACHE = {}


def kernel(**inputs):
    if 'nc' not in _CACHE:
        _CACHE['nc'] = build_program()
    nc = _CACHE['nc']
    maps = _host_prepare(inputs)
    res = run_bass_kernel_spmd(nc, maps, core_ids=list(range(8)))
    R = res.results
    f = np.float32
    y_prompt = np.concatenate([R[i]['yp'].reshape(4, 256, 1024) for i in range(8)], 0).astype(f)
    y_sample = np.zeros((4, 4096, 1024), f)
    for i in range(8):
        b, hf = i // 2, i % 2
        ys = R[i]['ys']
        if hf == 0:
            y_sample[b, :2048] = ys
        else:
            y_sample[b, 2048:] = ys[::-1]
    nak = np.concatenate([R[i]['nak'].reshape(4, 1, 256, 2, 64) for i in range(8)], 0).astype(f)
    nav = np.concatenate([R[i]['nav'].reshape(4, 1, 256, 2, 64) for i in range(8)], 0).astype(f)
    nbk = np.concatenate([R[i]['nbk'].reshape(4, 1, 256, 4, 128) for i in range(8)], 0).astype(f)
    nbv = np.concatenate([R[i]['nbv'].reshape(4, 1, 256, 4, 128) for i in range(8)], 0).astype(f)
    nsf = np.concatenate([R[i]['nsf'].reshape(4, 1, 1280) for i in range(8)], 0).astype(f)
    nsb = np.concatenate([R[i]['nsb'].reshape(4, 1, 1280) for i in range(8)], 0).astype(f)
    return (y_prompt, y_sample, nak, nav, nbk, nbv, nsf, nsb)
```

```python
import numpy as np
from contextlib import ExitStack
import concourse.bass as bass
import concourse.mybir as mybir
from concourse.bass_utils import run_bass_kernel_spmd

F32, BF16 = mybir.dt.float32, mybir.dt.bfloat16
AF = mybir.ActivationFunctionType
ALU = mybir.AluOpType
NDS = 6
import os
NO_CC = os.environ.get('KNO_CC') == '1'
STAGE = float(os.environ.get('KSTAGE', '99'))


class StopBuild(Exception):
    pass


DEAD = [False]


def stage_check(k):
    if STAGE < k:
        DEAD[0] = True
EPS = 1e-6
SCALE = 0.125
LAM_INIT = 0.8 - 0.6 * 1.0
TP, TS = 1024, 2048
QA, QAR, KA, KAR, QB, QBR, KB, KBR, VA, VB, WX = 0, 512, 1024, 1152, 1280, 1792, 2304, 2816, 3328, 3456, 3968


def sp_layout():
    ents = [('g1', 16), ('g2', 16), ('gf', 8), ('bada', 96), ('convw_p', 50), ('convw_s', 50), ('convb', 10),
            ('ba_p', 20), ('bx_p', 20), ('lam_p', 20), ('ba_s', 20), ('bx_s', 20), ('lam_s', 20),
            ('subln', 1), ('state_up', 10), ('sink', 8), ('lamqk', 256), ('sel', 2), ('cT', 16)]
    off, d = 0, {}
    for n, w in ents:
        d[n] = (off, w)
        off += w
    return d, off


class Dep:
    __slots__ = ('w', 'r')

    def __init__(s):
        s.w = None
        s.r = {}


class Tl:
    def __init__(s, t, name):
        s.t = t
        s.name = name
        s.subs = {}

    def __getitem__(s, idx):
        return s.t[idx]


class KB_:
    def __init__(s, nc, es):
        s.nc, s.es = nc, es
        s.eng = {}
        for n in ['pe', 'act', 'dve', 'pool', 'sp']:
            s.eng[n] = dict(ops=[], sem=es.enter_context(nc.semaphore('s_' + n)), cnt=0, known={})
        s.dq = {}
        for q in ['sp', 'pool', 'act']:
            s.dq[q] = dict(sems=[es.enter_context(nc.semaphore(f'd_{q}{i}')) for i in range(NDS)], n=0)
        s.out_toks = []
        s.tiles = []
        s.cc_sem = es.enter_context(nc.semaphore('s_cc'))
        s.cc_n = 0
        s.H = {'pe': nc.tensor, 'act': nc.scalar, 'dve': nc.vector, 'pool': nc.gpsimd, 'sp': nc.sync}
        es.enter_context(nc.Block())

    def _emit(s, eng, waits, fn, inc):
        h = s.H[eng]
        for sem, val in waits:
            h.wait_ge(sem, val)
        if fn is not None:
            ins = fn(h)
            if inc is not None:
                ins.then_inc(inc[0], inc[1])

    def tile(s, es, name, shape, dt):
        t = Tl(es.enter_context(s.nc.sbuf_tensor('t_' + name, list(shape), dt)), name)
        return t

    def _conf(s, tl, sub):
        if sub is None:
            return list(tl.subs.values())
        return [tl.subs[k] for k in (sub, None) if k in tl.subs]

    @staticmethod
    def _ks(key):
        return key if isinstance(key, tuple) else (key, None)

    def _deps(s, r, w, tok):
        need = []
        for key in r:
            tl, sub = s._ks(key)
            for d in s._conf(tl, sub):
                if d.w:
                    need.append(d.w)
                if getattr(tl, 'psum', False):
                    need.extend(t for t in d.r.values() if t[0] is not tok[0])
        for key in w:
            tl, sub = s._ks(key)
            for d in s._conf(tl, sub):
                if d.w:
                    need.append(d.w)
                need.extend(d.r.values())
        for key in r:
            tl, sub = s._ks(key)
            d = tl.subs.setdefault(sub, Dep())
            d.r[tok[0].num] = tok
        for key in w:
            tl, sub = s._ks(key)
            if sub is None:
                tl.subs = {}
            d = tl.subs.setdefault(sub, Dep())
            d.w = tok
            d.r = {}
        return need

    def _need(s, eng, toks):
        kn = s.eng[eng]['known']
        waits = []
        for sem, val in toks:
            if kn.get(sem.num, 0) < val:
                kn[sem.num] = val
                waits.append((sem, val))
        return waits

    def op(s, eng, fn, r=(), w=(), inc=True):
        if DEAD[0]:
            return None
        e = s.eng[eng]
        tok = (e['sem'], e['cnt'] + 1)
        need = s._deps(r, w, tok)
        need = [t for t in need if not (t[0] is e['sem'] and (eng == 'pe' or t[1] > e['cnt']))]
        waits = s._need(eng, need)
        s._emit(eng, waits, fn, (e['sem'], 1) if inc else None)
        if inc:
            e['cnt'] += 1
        return tok

    def dma(s, q, out, in_, r=(), w=(), is_out=False, fn=None, **kw):
        if DEAD[0]:
            return None
        dq = s.dq[q]
        n = dq['n']
        sem = dq['sems'][n % NDS]
        val = 16 * (n // NDS + 1)
        dq['n'] += 1
        tok = (sem, val)
        need = s._deps(r, w, tok)
        if n >= NDS:
            need.append((sem, val - 16))
        waits = s._need(q, need)
        if fn is None:
            fn = lambda e: e.dma_start(out=out, in_=in_, **kw)
        s._emit(q, waits, fn, (sem, 16))
        if is_out:
            s.out_toks.append(tok)
        return tok

    def cc(s, fn, r=(), w=()):
        if DEAD[0]:
            return None
        s.cc_n += 1
        tok = (s.cc_sem, s.cc_n)
        need = s._deps(r, w, tok)
        waits = s._need('pool', need)
        h = s.H['pool']
        for sem, val in waits:
            h.wait_ge(sem, val)
        fn(h).then_inc(s.cc_sem)
        return tok

    def all_toks(s):
        toks = [(e['sem'], e['cnt']) for e in s.eng.values() if e['cnt'] > 0]
        if s.cc_n > 0:
            toks.append((s.cc_sem, s.cc_n))
        for dq in s.dq.values():
            n = dq['n']
            for i in range(min(n, NDS)):
                last = ((n - 1 - i) // NDS) * NDS + i if False else None
            for i, sem in enumerate(dq['sems']):
                cnt = (n - i + NDS - 1) // NDS if n > i else 0
                if cnt > 0:
                    toks.append((sem, 16 * cnt))
        return toks

    def barrier(s):
        if DEAD[0]:
            return
        toks = s.all_toks()
        for n in s.eng:
            waits = s._need(n, [t for t in toks if not (t[0] is s.eng[n]['sem'])])
            if waits:
                s._emit(n, waits, None, None)

    def finish(s):
        waits = s._need('sp', s.all_toks())
        s._emit('sp', waits, None, None)

    def emit(s):
        with s.nc.Block() as block:
            def mk(name):
                def f(e):
                    for waits, fn, inc in s.eng[name]['ops']:
                        for sem, val in waits:
                            e.wait_ge(sem, val)
                        if fn is not None:
                            ins = fn(e)
                            if inc is not None:
                                ins.then_inc(inc[0], inc[1])
                return f
            block.tensor(mk('pe'))
            block.scalar(mk('act'))
            block.vector(mk('dve'))
            block.gpsimd(mk('pool'))
            block.sync(mk('sp'))


def build_program():
    DEAD[0] = False
    nc = bass.Bass("TRN2", target_bir_lowering=False)
    SPL, NS = sp_layout()

    def din(name, shape):
        return nc.dram_tensor(name, list(shape), F32, kind="ExternalInput").ap()

    def dout(name, shape):
        return nc.dram_tensor(name, list(shape), F32, kind="ExternalOutput").ap()

    xp_d = din('xp', [TP, 1024]); xso_d = din('xs_own', [TS, 1024]); xst_d = din('xs_oth', [TS, 1024])
    cos_d = din('cos', [128, 4096]); sin_d = din('sin', [128, 4096])
    cak_d = din('cak', [256, 128]); cav_d = din('cav', [256, 128]); cbk_d = din('cbk', [256, 512]); cbv_d = din('cbv', [256, 512])
    sp_d = din('sp', [128, NS]); ident_d = din('ident', [128, 128]); mask_d = din('masks', [128, 2, 512])
    wada_d = din('w_ada', [2, 1024, 6144]); w1_d = din('w_mlp1', [2, 1024, 4096]); w2_d = din('w_mlp2', [2, 4096, 1024])
    win_d = din('w_in', [1024, 2304]); winx_d = din('w_in_x', [1024, WX]); wout_d = din('w_out', [1024, 1024])
    rwin_d = din('rec_w_in', [1024, 2560]); rwout_d = din('rec_w_out', [1280, 1024])
    rwa_p_d = din('rwa_p', [2, 10, 128, 128]); rwx_p_d = din('rwx_p', [2, 10, 128, 128])
    rwa_s_d = din('rwa_s', [2, 10, 128, 128]); rwx_s_d = din('rwx_s', [2, 10, 128, 128])
    yp_d = dout('yp', [TP, 1024]); ys_d = dout('ys', [TS, 1024])
    nak_d = dout('nak', [TP, 128]); nav_d = dout('nav', [TP, 128]); nbk_d = dout('nbk', [TP, 512]); nbv_d = dout('nbv', [TP, 512])
    nsf_d = dout('nsf', [4, 1280]); nsb_d = dout('nsb', [4, 1280])
    xsrc1 = nc.dram_tensor('xsrc1', [128, 128], F32).ap()
    xdst1 = nc.dram_tensor('xdst1', [128, 128], F32).ap()
    xsrc2 = nc.dram_tensor('xsrc2', [128, 128], F32).ap()
    xdst2 = nc.dram_tensor('xdst2', [128, 128], F32).ap()
    PAIRS = [[0, 1], [2, 3], [4, 5], [6, 7]]

    with ExitStack() as es:
        K = KB_(nc, es)
        PS = [Tl(es.enter_context(nc.psum_tensor(f'ps{i}', [128, 512], F32)), f'ps{i}') for i in range(8)]
        for p_ in PS:
            p_.psum = True
        ring = {'a': 0, 'b': 0}

        def psA():
            ring['a'] = (ring['a'] + 1) % 4
            return PS[ring['a']]

        def psB():
            ring['b'] = (ring['b'] + 1) % 4
            return PS[4 + ring['b']]

        ident = K.tile(es, 'ident', [128, 128], F32)
        ones = K.tile(es, 'ones', [128, 128], BF16)
        masks = K.tile(es, 'masks', [128, 2, 512], F32)
        spt = K.tile(es, 'spt', [128, NS], F32)
        modv = K.tile(es, 'modv', [128, 2, 48, 2], F32)
        Amod = K.tile(es, 'Amod', [128, 2, 2, 2, 8], F32)
        cons = K.tile(es, 'cons', [128, 64], F32)
        K.dma('sp', ident[:], ident_d, w=[ident])
        K.dma('sp', masks[:], mask_d, w=[masks])
        K.dma('sp', spt[:], sp_d, w=[spt])
        K.op('dve', lambda e: e.memset(ones[:], 1.0), w=[ones])

        def SP(name, a=0, b=None):
            o, w = SPL[name]
            b = w if b is None else b
            return spt[:, o + a:o + b]

        flip = {'i': 0}

        def evac(out, in_, r, w, scale=1.0, bias=0.0, eng=None):
            if eng is None:
                flip['i'] ^= 1
                eng = 'act' if flip['i'] else 'dve'
            if eng == 'act':
                K.op('act', lambda e: e.activation(out=out, in_=in_, func=AF.Identity, bias=bias, scale=scale), r, w)
            else:
                K.op('dve', lambda e: e.tensor_copy(out=out, in_=in_), r, w)

        def mm(out, lhsT, rhs, start, stop, r, w, last):
            K.op('pe', lambda e: e.matmul(out, lhsT, rhs, start=start, stop=stop), r, w, inc=last)

        def tr(out, in_, r, w, last):
            K.op('pe', lambda e: e.transpose(out, in_, ident[:]), list(r) + [ident], w, inc=last)

        slabs = []
        slab_i = {'i': 0}

        def alloc_slabs(scope, n, size=4096):
            slabs.clear()
            slab_i['size'] = size
            for i in range(n):
                slabs.append(K.tile(scope, f'slab{i}_{len(K.tiles)}', [128, size], BF16))
                K.tiles.append(None)
            slab_i['i'] = 0

        def wload(parts):
            sl = slabs[slab_i['i'] % len(slabs)]
            slab_i['i'] += 1
            views, off = [], 0
            for j, src in enumerate(parts):
                kc, n = src.shape[1], src.shape[2]
                v = sl[:, off:off + kc * n].rearrange("p (k n) -> p k n", k=kc)
                K.dma('pool', v, src, w=[(sl, j)] if len(parts) > 1 else [sl])
                views.append(v)
                off += kc * n
            assert off <= slab_i['size']
            return sl, views

        def wview(w2d, c0, n, k0=0, kc=None):
            v = w2d.rearrange("(k p) n -> p k n", p=128)
            kc = v.shape[1] - k0 if kc is None else kc
            return v[:, k0:k0 + kc, c0:c0 + n]

        with ExitStack() as sc:
            alloc_slabs(sc, 4)
            csil = K.tile(sc, 'csil', [128, 8, 2], BF16)
            K.op('act', lambda e: e.activation(out=csil[:], in_=SP('cT').rearrange("p (k j) -> p k j", j=2), func=AF.Silu),
                 [spt], [csil])
            for l in range(2):
                ps = psA()
                for blk in range(12):
                    sl, (wv,) = wload([wview(wada_d[l], blk * 512, 512)])
                    for j in range(4):
                        ch = blk * 4 + j
                        for kc in range(8):
                            mm(ps[:, ch * 2:ch * 2 + 2], wv[:, kc, j * 128:(j + 1) * 128], csil[:, kc, :],
                               kc == 0, kc == 7, [sl, csil], [ps], kc == 7)
                for cv in range(2):
                    K.op('dve', lambda e, l=l, cv=cv, ps=ps: e.tensor_tensor(
                        out=modv[:, l, :, cv], in0=ps[:, 0:96].rearrange("p (c j) -> p c j", j=2)[:, :, cv],
                        in1=SP('bada', l * 48, l * 48 + 48), op=ALU.add), [ps, spt], [(modv, (l, cv))])
                for nrm in range(2):
                    for cv in range(2):
                        g = SP('g1' if nrm == 0 else 'g2', l * 8, l * 8 + 8)
                        K.op('dve', lambda e, l=l, nrm=nrm, cv=cv, g=g: e.scalar_tensor_tensor(
                            out=Amod[:, l, nrm, cv, :], in0=modv[:, l, nrm * 24 + 8:nrm * 24 + 16, cv], scalar=1.0,
                            in1=g, op0=ALU.add, op1=ALU.mult), [(modv, (l, cv)), spt], [(Amod, (l, nrm, cv))])
            K.barrier()

        def MA(l, nrm, cv, c):
            return Amod[:, l, nrm, cv, c:c + 1]

        def MB(l, nrm, cv, c):
            return modv[:, l, nrm * 24 + c, cv:cv + 1]

        def MG(l, nrm, cv, c):
            return modv[:, l, nrm * 24 + 16 + c, cv:cv + 1]

        def load_xT(scope_tiles, x_d, t0, ntt, dst, dst_t0, T_key):
            stg = scope_tiles['stg']
            for tt in range(ntt):
                st = stg[tt % 2]
                K.dma('sp', st[:], x_d[t0 + tt * 128:t0 + (tt + 1) * 128, :], w=[st])
                for hb in range(2):
                    ps = psA()
                    for j in range(4):
                        c = hb * 4 + j
                        tr(ps[:, j * 128:(j + 1) * 128], st[:, c * 128:(c + 1) * 128], [st], [ps], j == 3)
                    o = dst_t0 + tt * 128
                    evac(dst[:, hb * 4:hb * 4 + 4, o:o + 128], ps[:, :].rearrange("p (c t) -> p c t", c=4),
                         [ps], [(dst, (T_key, o // 512))])

        def norm_stats(tmp, xsrc, xkey, n=512):
            sq, rstd = tmp['sq'], tmp['rstd']
            K.op('act', lambda e: e.activation(out=sq[:, :, 0:n], in_=xsrc, func=AF.Square), [xkey], [sq])
            ps = psA()
            for c in range(8):
                mm(ps[:, 0:n], ones[:], sq[:, c, 0:n], c == 0, c == 7, [ones, sq], [ps], c == 7)
            K.op('act', lambda e: e.activation(out=rstd[:, 0:n], in_=ps[:, 0:n], func=AF.Ln, bias=cons[:, 20:21], scale=1.0 / 1024),
                 [ps, cons], [rstd])
            K.op('act', lambda e: e.activation(out=rstd[:, 0:n], in_=rstd[:, 0:n], func=AF.Exp, scale=-0.5), [rstd], [rstd])
            return rstd

        def norm_mod(tmp, xsrc, xkey, hdst, hkey, l, nrm, cv, n=512):
            rstd = norm_stats(tmp, xsrc, xkey, n)
            u = tmp['u']
            for c in range(8):
                uc = u[c % 2]
                K.op('dve', lambda e, c=c, uc=uc: e.scalar_tensor_tensor(
                    out=uc[:, 0:n], in0=xsrc[:, c, :], scalar=MA(l, nrm, cv, c), in1=rstd[:, 0:n],
                    op0=ALU.mult, op1=ALU.mult), [xkey, rstd, Amod], [uc])
                K.op('act', lambda e, c=c, uc=uc: e.activation(out=hdst(c), in_=uc[:, 0:n], func=AF.Identity,
                                                               bias=MB(l, nrm, cv, c), scale=1.0), [uc, modv], [hkey])

        wq_plan, wq_ready = [], []

        def wq_top_up():
            while wq_plan and len(wq_ready) < len(slabs) - 1:
                wq_ready.append(wload([wq_plan.pop(0)]))

        def wq_get():
            wq_top_up()
            r_ = wq_ready.pop(0)
            wq_top_up()
            return r_

        def linear_fm(wparts, hsrc, hkeys, ntile, epi, tile0=0, planned=False):
            ci = 0
            if not planned:
                assert not wq_plan and not wq_ready
                wq_plan.extend(wparts)
            for src in wparts:
                sl, (wv,) = wq_get()
                kcn, n = src.shape[1], src.shape[2]
                for j in range(n // 128):
                    for t in range(tile0, tile0 + ntile):
                        ps = psA()
                        for kc in range(kcn):
                            mm(ps[:, :], wv[:, kc, j * 128:(j + 1) * 128], hsrc(kc, t), kc == 0, kc == kcn - 1,
                               [sl] + hkeys(t), [ps], kc == kcn - 1)
                        epi(ci, t, ps)
                    ci += 1

        def resid_epi(xT, xkey, l, nrm, cv):
            def epi(c, t, ps):
                K.op('dve', lambda e: e.scalar_tensor_tensor(
                    out=xT[:, c, t * 512:(t + 1) * 512], in0=ps[:, :], scalar=MG(l, nrm, cv, c),
                    in1=xT[:, c, t * 512:(t + 1) * 512], op0=ALU.mult, op1=ALU.add), [ps, modv, xkey(t)], [xkey(t)])
            return epi

        def mlp(scope, xT, xkeyf, T, l, cv, tmp, hT, nhalf):
            nt = T // 512
            for t in range(nt):
                norm_mod(tmp, xT[:, :, t * 512:(t + 1) * 512], xkeyf(t),
                         lambda c, t=t: hT[:, c, t * 512:(t + 1) * 512], (hT, t), l, 1, cv)
            HC = 32 // nhalf
            hid = K.tile(scope, f'hid{l}{cv}', [128, HC, T], BF16)
            rl = [K.tile(scope, f'rl{l}{cv}{i}', [128, 512], F32) for i in range(2)]
            ncw = min(512, 4096 // HC)
            assert not wq_plan and not wq_ready
            for half in range(nhalf):
                wq_plan.extend([wview(w1_d[l], half * HC * 128 + s * 512, 512) for s in range(HC // 4)])
                wq_plan.extend([wview(w2_d[l], c * ncw, ncw, k0=half * HC, kc=HC) for c in range(1024 // ncw)])
            for half in range(nhalf):
                def epi1(c, t, ps):
                    r_ = rl[(c + t) % 2]
                    K.op('act', lambda e: e.activation(out=r_[:], in_=ps[:, :], func=AF.Relu), [ps], [r_])
                    K.op('pool', lambda e: e.tensor_tensor(out=hid[:, c, t * 512:(t + 1) * 512], in0=r_[:], in1=r_[:],
                                                           op=ALU.mult), [r_], [(hid, (c, t))])
                linear_fm([wview(w1_d[l], half * HC * 128 + s * 512, 512) for s in range(HC // 4)],
                          lambda kc, t: hT[:, kc, t * 512:(t + 1) * 512], lambda t: [(hT, t)], nt, epi1, planned=True)
                linear_fm([wview(w2_d[l], c * ncw, ncw, k0=half * HC, kc=HC) for c in range(1024 // ncw)],
                          lambda kc, t: hid[:, kc, t * 512:(t + 1) * 512], lambda t: [hid], nt,
                          resid_epi(xT, xkeyf, l, 1, cv), planned=True)

        def final_out(tmp, xT, xkeyf, T, y_d, ostg):
            for t in range(T // 512):
                xs = xT[:, :, t * 512:(t + 1) * 512]
                rstd = norm_stats(tmp, xs, xkeyf(t))
                yn = tmp['yn']
                for c in range(8):
                    K.op('dve', lambda e, c=c: e.scalar_tensor_tensor(
                        out=yn[:, c, :], in0=xs[:, c, :], scalar=SP('gf', c, c + 1), in1=rstd[:, 0:512],
                        op0=ALU.mult, op1=ALU.mult), [xkeyf(t), rstd, spt], [(yn, c)])
                for tt in range(4):
                    og = ostg[tt % 2]
                    for hb in range(2):
                        ps = psA()
                        for j in range(4):
                            c = hb * 4 + j
                            tr(ps[:, j * 128:(j + 1) * 128], yn[:, c, tt * 128:(tt + 1) * 128], [(yn, c)], [ps], j == 3)
                        evac(og[:, hb * 512:(hb + 1) * 512], ps[:, :], [ps], [(og, hb)])
                    r0 = t * 512 + tt * 128
                    K.dma('sp', y_d[r0:r0 + 128, :], og[:], r=[og], is_out=True)

        def attn_A_unit(tmp, qrhs, qkey, kblocks, g, OT, otok0, esink):
            O, D = psB(), psB()
            Pts = tmp['P']
            nb = len(kblocks)
            Ss = [None] * nb

            def qk(i):
                S = psA()
                mm(S[:, :], kblocks[i][0], qrhs, True, True, [kblocks[i][1], qkey], [S], True)
                Ss[i] = S
            qk(0)
            if nb > 1:
                qk(1)
            for i in range(nb):
                if i + 2 < nb:
                    qk(i + 2)
                Pt = Pts[tmp['pi'] % len(Pts)]
                tmp['pi'] += 1
                S = Ss[i]
                K.op('act', lambda e, S=S, Pt=Pt: e.activation(out=Pt[:], in_=S[:, :], func=AF.Exp, scale=SCALE), [S], [Pt])
                mi = kblocks[i][4]
                if mi is not None:
                    K.op('dve', lambda e, Pt=Pt, mi=mi: e.tensor_tensor(out=Pt[:], in0=Pt[:], in1=masks[:, mi, :], op=ALU.mult),
                         [Pt, masks], [Pt])
                mm(O[:, :], kblocks[i][2], Pt[:], i == 0, i == nb - 1, [kblocks[i][3], Pt], [O], i == nb - 1)
                mm(D[:, :], ones[:], Pt[:], i == 0, i == nb - 1, [ones, Pt], [D], i == nb - 1)
            rd = tmp['rd']
            for j in range(4):
                h = g * 4 + j
                K.op('act', lambda e, j=j, h=h: e.activation(out=rd[:, j * 128:(j + 1) * 128], in_=D[:, j * 128:(j + 1) * 128],
                                                             func=AF.Ln, bias=esink[:, h:h + 1], scale=1.0),
                     [D, cons], [(rd, j)])
            K.op('act', lambda e: e.activation(out=rd[:], in_=rd[:], func=AF.Exp, scale=-1.0), [rd], [rd])
            for j in range(4):
                p0 = (j % 2) * 64
                ch = g * 2 + j // 2
                K.op('dve', lambda e, j=j, p0=p0, ch=ch: e.tensor_tensor(
                    out=OT[p0:p0 + 64, ch, otok0:otok0 + 128], in0=O[p0:p0 + 64, j * 128:(j + 1) * 128],
                    in1=rd[p0:p0 + 64, j * 128:(j + 1) * 128], op=ALU.mult), [O, rd], [(OT, (ch, p0, otok0))])

        def attn_B_unit(tmp, q1, q2, qkey, kblocks, OT, och, otok0, neglam, subw):
            O, D = psB(), psB()
            Pts = tmp['P']
            nb = len(kblocks)
            Ss = [None] * nb

            def qk(i):
                S = psA()
                mm(S[:, 0:256], kblocks[i][0], q1, True, True, [kblocks[i][2], qkey], [S], False)
                mm(S[:, 256:512], kblocks[i][1], q2, True, True, [kblocks[i][2], qkey], [S], True)
                Ss[i] = S
            qk(0)
            if nb > 1:
                qk(1)
            for i in range(nb):
                if i + 2 < nb:
                    qk(i + 2)
                Pt = Pts[tmp['pi'] % len(Pts)]
                tmp['pi'] += 1
                S = Ss[i]
                K.op('act', lambda e, S=S, Pt=Pt: e.activation(out=Pt[:], in_=S[:, :], func=AF.Exp, scale=SCALE), [S], [Pt])
                mm(O[:, :], kblocks[i][3], Pt[:], i == 0, i == nb - 1, [kblocks[i][4], Pt], [O], i == nb - 1)
                mm(D[:, :], ones[:], Pt[:], i == 0, i == nb - 1, [ones, Pt], [D], i == nb - 1)
            rd, ob, sqb, rs = tmp['rd'], tmp['ob'], tmp['sqb'], tmp['rs']
            K.op('act', lambda e: e.activation(out=rd[:], in_=D[:, :], func=AF.Ln), [D], [rd])
            K.op('act', lambda e: e.activation(out=rd[:], in_=rd[:], func=AF.Exp, scale=-1.0), [rd], [rd])
            K.op('dve', lambda e: e.tensor_tensor(out=rd[:], in0=O[:, :], in1=rd[:], op=ALU.mult), [O, rd], [rd])
            K.op('dve', lambda e: e.scalar_tensor_tensor(out=ob[:], in0=rd[:, 256:512], scalar=neglam, in1=rd[:, 0:256],
                                                         op0=ALU.mult, op1=ALU.add), [rd, cons], [ob])
            K.op('act', lambda e: e.activation(out=sqb[:], in_=ob[:], func=AF.Square), [ob], [sqb])
            ps = psA()
            mm(ps[:, 0:256], ones[:], sqb[:], True, True, [ones, sqb], [ps], True)
            K.op('act', lambda e: e.activation(out=rs[:], in_=ps[:, 0:256], func=AF.Ln, bias=cons[:, 20:21], scale=1.0 / 128), [ps, cons], [rs])
            K.op('act', lambda e: e.activation(out=rs[:], in_=rs[:], func=AF.Exp, scale=-0.5), [rs], [rs])
            K.op('dve', lambda e: e.scalar_tensor_tensor(out=OT[:, och, otok0:otok0 + 256], in0=ob[:], scalar=subw, in1=rs[:],
                                                         op0=ALU.mult, op1=ALU.mult), [ob, rs, cons], [(OT, (och, 0, otok0))])

        lq = SP('lamqk')
        K.op('dve', lambda e: e.memset(cons[:], 0.0), [], [cons])
        K.op('dve', lambda e: e.memset(cons[:, 20:21], EPS), [cons], [cons])
        prod = K.tile(es, 'lprod', [128, 128], F32)
        K.op('dve', lambda e: e.tensor_tensor(out=prod[:].rearrange("p (a d) -> p a d", a=2),
                                              in0=lq.rearrange("p (a b d) -> p a b d", a=2, b=2)[:, :, 0, :],
                                              in1=lq.rearrange("p (a b d) -> p a b d", a=2, b=2)[:, :, 1, :], op=ALU.mult),
             [spt], [prod])
        K.op('dve', lambda e: e.reduce_sum(out=cons[:, 16:18], in_=prod[:].rearrange("p (a d) -> p a d", a=2),
                                           axis=mybir.AxisListType.X), [prod, cons], [cons])
        K.op('act', lambda e: e.activation(out=cons[:, 18:20], in_=cons[:, 16:18], func=AF.Exp), [cons], [cons])
        K.op('act', lambda e: e.activation(out=cons[:, 0:8], in_=SP('sink'), func=AF.Exp), [spt, cons], [cons])
        K.op('dve', lambda e: e.tensor_tensor(out=cons[:, 8:9], in0=cons[:, 19:20], in1=cons[:, 18:19], op=ALU.subtract), [cons], [cons])
        K.op('dve', lambda e: e.tensor_scalar(out=cons[:, 8:9], in0=cons[:, 8:9], scalar1=-LAM_INIT, scalar2=None, op0=ALU.add), [cons], [cons])
        K.op('dve', lambda e: e.tensor_scalar(out=cons[:, 9:10], in0=SP('subln'), scalar1=1.0 - LAM_INIT, scalar2=None, op0=ALU.mult),
             [spt, cons], [cons])
        esink, neglam, subw = cons, cons[:, 8:9], cons[:, 9:10]

        def exchange(scope, nm, data_ap, data_keys, W, xsrc, xdst, ksrc, kdst, out_ap, out_key):
            cb = K.tile(scope, 'cb' + nm, [128, 128], F32)
            hg = K.tile(scope, 'hg' + nm, [128, 128], F32)
            K.op('dve', lambda e: e.memset(cb[:], 0.0), [], [cb])
            K.op('dve', lambda e: e.tensor_scalar(out=cb[:, 0:W], in0=data_ap, scalar1=SP('sel', 1, 2), scalar2=None, op0=ALU.mult),
                 list(data_keys) + [spt, cb], [cb])
            K.op('dve', lambda e: e.tensor_scalar(out=cb[:, 64:64 + W], in0=data_ap, scalar1=SP('sel', 0, 1), scalar2=None, op0=ALU.mult),
                 list(data_keys) + [spt, cb], [cb])
            K.dma('sp', xsrc, cb[:], r=[cb], w=[ksrc])
            if NO_CC:
                K.dma('pool', xdst, xsrc, r=[ksrc], w=[kdst])
            else:
                K.cc(lambda e: e.collective_compute("AllReduce", ALU.add, replica_groups=PAIRS, ins=[xsrc.opt()], outs=[xdst.opt()]),
                     r=[ksrc], w=[kdst])
            K.dma('sp', hg[:], xdst, r=[kdst], w=[hg])
            K.op('dve', lambda e: e.tensor_scalar(out=out_ap, in0=hg[:, 0:W], scalar1=SP('sel', 0, 1), scalar2=None, op0=ALU.mult),
                 [hg, spt], [out_key])
            K.op('dve', lambda e: e.scalar_tensor_tensor(out=out_ap, in0=hg[:, 64:64 + W], scalar=SP('sel', 1, 2), in1=out_ap,
                                                         op0=ALU.mult, op1=ALU.add), [hg, spt, out_key], [out_key])

        def rec_layer(scope, xT, xkeyf, T, hT, cv, segL, phases, sfx, rwa_d, rwx_d, tmp=None, exch=None, state_out=None):
            nseg = T // segL
            nt = T // 512
            PL = 512
            npc = T // PL
            xrp2 = [K.tile(scope, f'xrp{i}' + sfx, [128, nseg, segL + 4], F32) for i in range(2)]
            xc2 = [K.tile(scope, f'xc{i}' + sfx, [128, 512], F32) for i in range(3)]
            xcb2 = [K.tile(scope, f'xcb{i}' + sfx, [128, 512], BF16) for i in range(3)]
            gb = K.tile(scope, 'gb' + sfx, [128, T], BF16)
            bA2 = [K.tile(scope, f'bA{i}' + sfx, [128, PL], F32) for i in range(2)]
            bR2 = [K.tile(scope, f'bR{i}' + sfx, [128, PL], F32) for i in range(2)]
            bI2 = [K.tile(scope, f'bI{i}' + sfx, [128, PL], F32) for i in range(2)]
            y = K.tile(scope, 'y' + sfx, [128, 10, T], BF16)
            wg2 = [K.tile(scope, f'wg{i}' + sfx, [128, 4, 128], BF16) for i in range(3)]
            carry = K.tile(scope, 'carry' + sfx, [128, 4], F32)
            cl = K.tile(scope, 'cl' + sfx, [128, 2, 2, 10], F32)
            pfx = '_p' if sfx == 'p' else '_s'
            lam = SP('lam' + pfx).rearrange("p (d c) -> p d c", d=2)
            K.op('act', lambda e: e.activation(out=cl[:, :, 0, :], in_=lam, func=AF.Exp, scale=-1.0), [spt], [cl])
            K.op('act', lambda e: e.activation(out=cl[:, :, 0, :], in_=cl[:, :, 0, :], func=AF.Ln, bias=1.0, scale=1.0), [cl], [cl])
            K.op('dve', lambda e: e.tensor_scalar(out=cl[:, :, 1, :], in0=cl[:, :, 0, :], scalar1=-16.0, scalar2=None, op0=ALU.mult), [cl], [cl])
            K.op('dve', lambda e: e.tensor_scalar(out=cl[:, :, 0, :], in0=cl[:, :, 0, :], scalar1=-8.0, scalar2=None, op0=ALU.mult), [cl], [cl])
            for xr_ in xrp2:
                K.op('pool', lambda e, xr_=xr_: e.memset(xr_[:], 0.0), [], [xr_])
            cw = SP('convw' + pfx)
            halo = None
            if exch is not None:
                hl = K.tile(scope, 'hl', [128, 10, 2], F32)
                halo = K.tile(scope, 'halo', [128, 10, 2], F32)
                for n in range(10):
                    sl, (wv,) = wload([wview(rwin_d, 1280 + n * 128, 128)])
                    ps = psA()
                    for kc in range(8):
                        mm(ps[:, 0:2], wv[:, kc, :], hT[:, kc, T - 2:T], kc == 0, kc == 7, [sl, (hT, nt - 1)], [ps], kc == 7)
                    evac(hl[:, n, :], ps[:, 0:2], [ps], [(hl, n)], eng='dve')
                exchange(scope, 'x1', hl[:].rearrange("p a b -> p (a b)"), [hl], 20, xsrc1, xdst1, exch['xsrc1'], exch['xdst1'],
                         halo[:].rearrange("p a b -> p (a b)"), halo)
            gcount = {'i': 0}
            for (dirs, final) in phases:
                if exch is not None and final:
                    exchange(scope, 'x2', exch['stS'][:], [exch['stS']], 10, xsrc2, xdst2, exch['xsrc2'], exch['xdst2'],
                             exch['sdn'][:], exch['sdn'])
                pend = {}

                wq = {}

                def wfetch(n, dirs=dirs, final=final):
                    parts = [wview(rwin_d, 1280 + n * 128, 128)]
                    if final:
                        parts.append(wview(rwin_d, n * 128, 128))
                    sl, wvs = wload(parts)
                    slk = [(sl, j) for j in range(len(parts))] if len(parts) > 1 else [sl]
                    wg = wg2[n % 3]
                    gparts = []
                    for d in dirs:
                        gparts += [rwa_d[d, n].rearrange("c (o d) -> c o d", o=1), rwx_d[d, n].rearrange("c (o d) -> c o d", o=1)]
                    for j, gp in enumerate(gparts):
                        K.dma('pool', wg[:, j:j + 1, :], gp, w=[(wg, j)])
                    wq[n] = (sl, wvs, slk)

                def inproj(n, dirs=dirs, final=final):
                    xrp = xrp2[n % 2]
                    sl, wvs, slk = wq.pop(n)
                    wg = wg2[n % 3]
                    for t in range(nt):
                        ps = psA()
                        for kc in range(8):
                            mm(ps[:, :], wvs[0][:, kc, :], hT[:, kc, t * 512:(t + 1) * 512], kc == 0, kc == 7,
                               [slk[0], (hT, t)], [ps], kc == 7)
                        if segL >= 512:
                            sgi, o = (t * 512) // segL, (t * 512) % segL
                            evac(xrp[:, sgi, 2 + o:2 + o + 512], ps[:, :], [ps], [(xrp, t)], eng='act')
                        else:
                            ns = 512 // segL
                            evac(xrp[:, t * ns:(t + 1) * ns, 2:2 + segL], ps[:, :].rearrange("p (s l) -> p s l", s=ns), [ps], [(xrp, t)], eng='act')
                    if halo is not None:
                        K.op('dve', lambda e, n=n: e.tensor_copy(out=xrp[:, 0, 2 + segL:4 + segL], in_=halo[:, n, :][:, ::-1]), [halo], [(xrp, 'halo')])
                    pend[n] = (sl, wvs, slk)

                def gateproj(n):
                    sl, wvs, slk = pend[n]
                    for t in range(nt):
                        ps2 = psA()
                        for kc in range(8):
                            mm(ps2[:, :], wvs[1][:, kc, :], hT[:, kc, t * 512:(t + 1) * 512], kc == 0, kc == 7,
                               [slk[1], (hT, t)], [ps2], kc == 7)
                        K.op('act', lambda e, t=t, ps2=ps2: e.activation(out=gb[:, t * 512:(t + 1) * 512], in_=ps2[:, :],
                                                                         func=AF.Gelu_apprx_tanh), [ps2], [(gb, t)])

                def stageA(n, pc, k):
                    xrp = xrp2[n % 2]
                    xcp, xbp = xc2[k % 3], xcb2[k % 3]
                    c0 = pc * PL
                    if segL >= PL:
                        segs = [(c0 // segL, c0 % segL, PL, 0)]
                    else:
                        segs = [(c0 // segL + s_, 0, segL, s_ * segL) for s_ in range(PL // segL)]
                    for (sgi, o, L_, bo) in segs:
                        o_ = xcp[:, bo:bo + L_]
                        K.op('dve', lambda e, sgi=sgi, o=o, L_=L_, o_=o_: e.tensor_scalar(
                            out=o_, in0=xrp[:, sgi, o:o + L_], scalar1=cw[:, n:n + 1], scalar2=SP('convb', n, n + 1),
                            op0=ALU.mult, op1=ALU.add), [xrp, spt], [xcp])
                        for k_ in range(1, 5):
                            K.op('dve', lambda e, sgi=sgi, o=o, L_=L_, o_=o_, k_=k_: e.scalar_tensor_tensor(
                                out=o_, in0=xrp[:, sgi, o + k_:o + k_ + L_], scalar=cw[:, k_ * 10 + n:k_ * 10 + n + 1], in1=o_,
                                op0=ALU.mult, op1=ALU.add), [xrp, spt, xcp], [xcp])
                    K.op('pool', lambda e: e.tensor_copy(out=xbp[:], in_=xcp[:]), [xcp], [xbp])

                def stageB(n, pc, k, dirs=dirs, final=final):
                    wg = wg2[n % 3]
                    xcp, xbp = xc2[k % 3], xcb2[k % 3]
                    c0 = pc * PL
                    for di, d in enumerate(dirs):
                        up = (d == 0)
                        gi = gcount['i'] % 2
                        gcount['i'] += 1
                        bA, bR, bI = bA2[gi], bR2[gi], bI2[gi]
                        psr, psi = psA(), psA()
                        mm(psr[:, :], wg[:, 2 * di, :], xbp[:], True, True, [(wg, 2 * di), xbp], [psr], True)
                        mm(psi[:, :], wg[:, 2 * di + 1, :], xbp[:], True, True, [(wg, 2 * di + 1), xbp], [psi], True)
                        K.op('act', lambda e, psr=psr, bR=bR, d=d: e.activation(
                            out=bR[:], in_=psr[:, :], func=AF.Sigmoid, bias=SP('ba' + pfx, d * 10 + n, d * 10 + n + 1), scale=1.0),
                            [psr, spt], [bR])
                        K.op('act', lambda e, psi=psi, bI=bI, d=d: e.activation(
                            out=bI[:], in_=psi[:, :], func=AF.Sigmoid, bias=SP('bx' + pfx, d * 10 + n, d * 10 + n + 1), scale=1.0),
                            [psi, spt], [bI])
                        K.op('act', lambda e, d=d, bA=bA, bR=bR: e.activation(out=bA[:], in_=bR[:], func=AF.Exp, scale=cl[:, d, 0, n:n + 1]), [bR, cl], [bA])
                        K.op('act', lambda e, d=d, bR=bR: e.activation(out=bR[:], in_=bR[:], func=AF.Exp, scale=cl[:, d, 1, n:n + 1]), [bR, cl], [bR])
                        K.op('pool', lambda e, bI=bI: e.tensor_tensor(out=bI[:], in0=bI[:], in1=xcp[:], op=ALU.mult), [bI, xcp], [bI])
                        K.op('act', lambda e, bR=bR: e.activation(out=bR[:], in_=bR[:], func=AF.Sqrt, bias=1.0, scale=-1.0), [bR], [bR])
                        K.op('pool', lambda e, bI=bI, bR=bR: e.tensor_tensor(out=bI[:], in0=bI[:], in1=bR[:], op=ALU.mult), [bI, bR], [bI])
                        nsc = PL // segL if segL < PL else 1
                        L = PL // nsc
                        seg_first = (c0 % segL == 0) if up else ((c0 + PL) % segL == 0)
                        for sc_ in range(nsc):
                            lo = sc_ * L
                            if segL <= PL or (seg_first and exch is None):
                                init, ik = 0.0, []
                            elif seg_first:
                                init = (SP('state_up', n, n + 1) if up else exch['sdn'][:, n:n + 1])
                                ik = [spt if up else exch['sdn']]
                            else:
                                init, ik = carry[:, 0:1], [carry]
                            if up:
                                K.op('dve', lambda e, lo=lo, L=L, init=init, bA=bA, bR=bR, bI=bI: e.tensor_tensor_scan(
                                    out=bR[:, lo:lo + L], data0=bA[:, lo:lo + L], data1=bI[:, lo:lo + L], initial=init,
                                    op0=ALU.mult, op1=ALU.add), [bA, bI] + ik, [bR])
                            else:
                                K.op('dve', lambda e, lo=lo, L=L, init=init, bA=bA, bR=bR, bI=bI: e.tensor_tensor_scan(
                                    out=bR[:, lo:lo + L][:, ::-1], data0=bA[:, lo:lo + L][:, ::-1], data1=bI[:, lo:lo + L][:, ::-1],
                                    initial=init, op0=ALU.mult, op1=ALU.add), [bA, bI] + ik, [bR])
                        last_col = PL - 1 if up else 0
                        if segL > PL:
                            K.op('dve', lambda e, last_col=last_col, bR=bR: e.tensor_copy(out=carry[:, 0:1], in_=bR[:, last_col:last_col + 1]), [bR], [carry])
                            if exch is not None and up and pc == npc - 1:
                                K.op('dve', lambda e, last_col=last_col, bR=bR: e.tensor_copy(out=exch['stS'][:, n:n + 1], in_=bR[:, last_col:last_col + 1]),
                                     [bR], [exch['stS']])
                        elif state_out is not None:
                            lc = segL - 1 if up else 0
                            ns_ = PL // segL
                            s0 = c0 // segL
                            K.op('dve', lambda e, d=d, lc=lc, bR=bR, s0=s0, ns_=ns_: e.tensor_copy(
                                out=state_out[:, d, s0:s0 + ns_, n], in_=bR[:].rearrange("p (s l) -> p s l", l=segL)[:, :, lc]), [bR], [(state_out, (d, n, s0))])
                        yv = y[:, n, c0:c0 + PL]
                        first = (di == 0 and not (final and len(dirs) == 1))
                        lastd = final and di == len(dirs) - 1
                        if first and not lastd:
                            K.op('pool', lambda e, yv=yv, bR=bR: e.tensor_copy(out=yv, in_=bR[:]), [bR], [(y, (n, pc))])
                        else:
                            if not first:
                                K.op('pool', lambda e, yv=yv, bR=bR: e.tensor_tensor(out=bR[:], in0=bR[:], in1=yv, op=ALU.add), [bR, (y, (n, pc))], [bR])
                            if lastd:
                                K.op('dve', lambda e, yv=yv, c0=c0, bR=bR: e.tensor_tensor(out=yv, in0=bR[:], in1=gb[:, c0:c0 + PL], op=ALU.mult),
                                     [bR, gb], [(y, (n, pc))])
                            else:
                                K.op('pool', lambda e, yv=yv, bR=bR: e.tensor_copy(out=yv, in_=bR[:]), [bR], [(y, (n, pc))])

                items = []
                for n in range(10):
                    porder = list(range(npc)) if dirs[0] == 0 else list(range(npc - 1, -1, -1))
                    items += [(n, pc) for pc in porder]
                wfetch(0)
                wfetch(1)
                inproj(0)
                stageA(items[0][0], items[0][1], 0)
                stageA(items[1][0], items[1][1], 1)
                for i, (n, pc) in enumerate(items):
                    if i % npc == 0:
                        if n + 1 < 10:
                            inproj(n + 1)
                        if n + 2 < 10:
                            wfetch(n + 2)
                        if final:
                            gateproj(n)
                    if i + 2 < len(items):
                        stageA(items[i + 2][0], items[i + 2][1], i + 2)
                    stageB(n, pc, i)
            l = 1
            linear_fm([wview(rwout_d, s * 128, 128) for s in range(8)], lambda kc, t: y[:, kc, t * 512:(t + 1) * 512],
                      lambda t: [y], nt, resid_epi(xT, xkeyf, l, 0, cv))

        try:
            stage_check(1)
            with ExitStack() as P:
                xT = K.tile(P, 'xTp', [128, 8, TP], F32)
                hT = K.tile(P, 'hTp', [128, 8, TP], BF16)
                xk = lambda t: (xT, ('x', t))
                with ExitStack() as sc:
                    alloc_slabs(sc, 4)
                    tmp = dict(sq=K.tile(sc, 'sq', [128, 8, 512], BF16), rstd=K.tile(sc, 'rstd', [128, 512], F32),
                               u=[K.tile(sc, f'u{i}', [128, 512], F32) for i in range(2)],
                               P=[K.tile(sc, f'P{i}', [128, 512], BF16) for i in range(4)], pi=0,
                               rd=K.tile(sc, 'rd', [128, 512], F32), ob=K.tile(sc, 'ob', [128, 256], F32),
                               sqb=K.tile(sc, 'sqb', [128, 256], BF16), rs=K.tile(sc, 'rs', [128, 256], F32))
                    stg = dict(stg=[K.tile(sc, f'stg{i}', [128, 1024], F32) for i in range(2)])
                    QK = K.tile(sc, 'QKp', [128, 9, TP], BF16)
                    VAt = K.tile(sc, 'VAp', [128, 8, 2, 128], BF16)
                    VBt = K.tile(sc, 'VBp', [128, 8, 512], BF16)
                    OT = K.tile(sc, 'OTp', [128, 8, TP], BF16)
                    kvs = [K.tile(sc, 'kvs0', [128, 8, 512], F32)] * 2
                    load_xT(stg, xp_d, 0, 8, xT, 0, 'x')
                    stage_check(1.1)
                    for t in range(2):
                        norm_mod(tmp, xT[:, :, t * 512:(t + 1) * 512], xk(t), lambda c, t=t: hT[:, c, t * 512:(t + 1) * 512], (hT, t), 0, 0, 0)
                    QBz = K.tile(sc, 'QBzp', [128, 4, 2, TP], BF16)
                    K.op('pool', lambda e: e.memset(QBz[:], 0.0), [], [QBz])

                    def epi_qk(c, t, ps):
                        if 5 <= c <= 8:
                            for hh in range(2):
                                evac(QBz[hh * 64:(hh + 1) * 64, c - 5, hh, t * 512:(t + 1) * 512], ps[hh * 64:(hh + 1) * 64, :], [ps], [(QBz, (c, t, hh))])
                        else:
                            cq = c if c < 5 else c - 4
                            evac(QK[:, cq, t * 512:(t + 1) * 512], ps[:, :], [ps], [(QK, (cq, t))])
                    stage_check(1.2)
                    linear_fm([wview(winx_d, QA, 512), wview(winx_d, KA, 128), wview(winx_d, QB, 512), wview(winx_d, KB, 512)],
                              lambda kc, t: hT[:, kc, t * 512:(t + 1) * 512], lambda t: [(hT, t)], 2, epi_qk)
                    stage_check(1.3)
                    for si, (c0, n, o0) in enumerate([(512, 256, 0), (1280, 512, 256), (1792, 512, 768)]):
                        if str(si) not in os.environ.get('KSI', '012'):
                            continue
                        sl, (wv,) = wload([wview(win_d, c0, n)])
                        for tt in range(8):
                            ps = psA()
                            for kc in range(8):
                                mm(ps[:, 0:n], hT[:, kc, tt * 128:(tt + 1) * 128], wv[:, kc, :], kc == 0, kc == 7, [sl, (hT, tt // 4)], [ps], kc == 7)
                            kv = kvs[si % 2]
                            evac(kv[:, tt, 0:n], ps[:, 0:n], [ps], [(kv, tt)])
                            if si == 0:
                                for u_ in range(2):
                                    K.op('dve', lambda e, tt=tt, ps=ps, u_=u_: e.tensor_copy(
                                        out=VAt[:, tt].rearrange("p g (u d) -> p g u d", u=2)[:, :, u_, :],
                                        in_=ps[:, 128:256].rearrange("p (g d) -> p g d", g=2)), [ps], [(VAt, (tt, u_))])
                            if si == 2:
                                evac(VBt[:, tt, :], ps[:, 0:512], [ps], [(VBt, tt)])
                        kv = kvs[si % 2]
                        for (d_, lo, w_) in ([(nak_d, 0, 128), (nav_d, 128, 128)] if si == 0 else [((nbk_d, nbv_d)[si - 1], 0, 512)]):
                            if os.environ.get('KNOOUT') == '1':
                                continue
                            if os.environ.get('KNOOUT') == '2':
                                for t8 in range(8):
                                    K.dma('sp', d_[t8 * 128:(t8 + 1) * 128, :], kv[:, t8, lo:lo + w_], r=[kv], is_out=True)
                                continue
                            K.dma('sp', d_.rearrange("(t p) c -> p t c", p=128), kv[:, :, lo:lo + w_], r=[kv], is_out=True)
                    stage_check(1.4)
                    for sq_ in range(4):
                        for g in range(2):
                            for qb in range(2):
                                q0 = sq_ * 256 + qb * 128
                                kbl = []
                                for kb in range(2):
                                    k0 = sq_ * 256 + kb * 128
                                    kbl.append((QK[g * 64:(g + 1) * 64, 4, k0:k0 + 128], QK, VAt[:, sq_ * 2 + kb, g, :], VAt, None))
                                attn_A_unit(tmp, QK[g * 64:(g + 1) * 64, 0:4, q0:q0 + 128], QK, kbl, g, OT, q0, esink)
                        for hb in range(4 if STAGE >= 1.5 else 0):
                            q0 = sq_ * 256
                            kbl = []
                            for kb in range(2):
                                k0 = sq_ * 256 + kb * 128
                                kbl.append((QK[:, 5 + hb, k0:k0 + 128], QK[:, 5 + hb, k0:k0 + 128], QK,
                                            VBt[:, sq_ * 2 + kb, hb * 128:(hb + 1) * 128], VBt))
                            attn_B_unit(tmp, QBz[:, hb, 0, q0:q0 + 256], QBz[:, hb, 1, q0:q0 + 256], QBz, kbl, OT, 4 + hb, q0, neglam, subw)
                    stage_check(1.6)
                    linear_fm([wview(wout_d, s * 512, 512) for s in range(2)], lambda kc, t: OT[:, kc, t * 512:(t + 1) * 512],
                              lambda t: [OT], 2, resid_epi(xT, xk, 0, 0, 0))
                    K.barrier()
                with ExitStack() as sc:
                    alloc_slabs(sc, 4)
                    tmp = dict(sq=K.tile(sc, 'sq2', [128, 8, 512], BF16), rstd=K.tile(sc, 'rstd2', [128, 512], F32),
                               u=[K.tile(sc, f'u2{i}', [128, 512], F32) for i in range(2)])
                    stage_check(2)
                    mlp(sc, xT, xk, TP, 0, 0, tmp, hT, 1)
                    K.barrier()
                with ExitStack() as sc:
                    alloc_slabs(sc, 3)
                    tmp = dict(sq=K.tile(sc, 'sq3', [128, 8, 512], BF16), rstd=K.tile(sc, 'rstd3', [128, 512], F32),
                               u=[K.tile(sc, f'u3{i}', [128, 512], F32) for i in range(2)])
                    for t in range(2):
                        norm_mod(tmp, xT[:, :, t * 512:(t + 1) * 512], xk(t), lambda c, t=t: hT[:, c, t * 512:(t + 1) * 512], (hT, t), 1, 0, 0)
                    stage_check(3)
                    stp = K.tile(sc, 'stp', [128, 2, 4, 10], F32)
                    rec_layer(sc, xT, xk, TP, hT, 0, 256, [([0, 1], True)], 'p', rwa_p_d, rwx_p_d, tmp, state_out=stp)
                    ps = psA()
                    tr(ps[0:80, 0:128], stp[:].rearrange("p d s n -> p (d s n)"), [stp], [ps], True)
                    sto = K.tile(sc, 'sto', [80, 128], F32)
                    evac(sto[:], ps[0:80, 0:128], [ps], [sto], eng='dve')
                    for d, dd in enumerate([nsf_d, nsb_d]):
                        for s_ in range(4):
                            K.dma('sp', dd[s_].rearrange("(n p) -> n p", p=128), sto[d * 40 + s_ * 10:d * 40 + s_ * 10 + 10, :], r=[sto], is_out=True)
                    K.barrier()
                with ExitStack() as sc:
                    alloc_slabs(sc, 4)
                    tmp = dict(sq=K.tile(sc, 'sq4', [128, 8, 512], BF16), rstd=K.tile(sc, 'rstd4', [128, 512], F32),
                               u=[K.tile(sc, f'u4{i}', [128, 512], F32) for i in range(2)], yn=K.tile(sc, 'yn4', [128, 8, 512], F32))
                    stage_check(4)
                    mlp(sc, xT, xk, TP, 1, 0, tmp, hT, 1)
                    ostg = [K.tile(sc, f'ostg{i}', [128, 1024], F32) for i in range(2)]
                    final_out(tmp, xT, xk, TP, yp_d, ostg)
                    K.barrier()

            class TlV(Tl):
                def __init__(s, base_ap, name):
                    s.t = None
                    s.base = base_ap
                    s.name = name
                    s.subs = {}

                def __getitem__(s, idx):
                    return s.base[idx]

            with ExitStack() as S:
                stage_check(5)
                X64 = K.tile(S, 'X64', [128, 8, 4096], BF16)
                hT = X64
                with ExitStack() as S1:
                    OT = K.tile(S1, 'OTs', [128, 8, TS], BF16)
                    with ExitStack() as sc:
                        tmp = dict(sq=K.tile(sc, 'sq5', [128, 8, 512], BF16), rstd=K.tile(sc, 'rstd5', [128, 512], F32),
                                   u=[K.tile(sc, f'u5{i}', [128, 512], F32) for i in range(2)])
                        stg = dict(stg=[K.tile(sc, f'stg5{i}', [128, 1024], F32) for i in range(2)])
                        xt2 = [K.tile(sc, f'xt5{i}', [128, 8, 512], F32) for i in range(2)]
                        for t in range(8):
                            xt = xt2[t % 2]
                            load_xT(stg, xso_d if t < 4 else xst_d, (t % 4) * 512, 4, xt, 0, 'x')
                            norm_mod(tmp, xt[:, :, :], xt, lambda c, t=t: hT[:, c, t * 512:(t + 1) * 512], (hT, t), 0, 0, 1)
                        K.barrier()
                    hk = lambda t: [(hT, t)]
                    hs = lambda kc, t: hT[:, kc, t * 512:(t + 1) * 512]
                    with ExitStack() as sc:
                        rp = dict(cs=[K.tile(sc, f'cs{i}', [128, 2, 512], F32) for i in range(2)],
                                  t1=[K.tile(sc, f't1{i}', [128, 512], F32) for i in range(2)],
                                  t2=[K.tile(sc, f't2{i}', [128, 512], F32) for i in range(2)])
                        tmp = dict(P=[K.tile(sc, f'P5{i}', [128, 512], BF16) for i in range(4)], pi=0,
                                   rd=K.tile(sc, 'rd5', [128, 512], F32), ob=K.tile(sc, 'ob5', [128, 256], F32),
                                   sqb=K.tile(sc, 'sqb5', [128, 256], BF16), rs=K.tile(sc, 'rs5', [128, 256], F32))
                        cstg = K.tile(sc, 'cstg', [128, 2, 512], F32)
                        ri = {'i': 0}

                        def rope_proj(src_w, src_wr, ncol, tiles, dst, pre=None):
                            if pre is None:
                                sl, (wv,) = wload([src_w])
                                slr, (wvr,) = wload([src_wr])
                            else:
                                (sl, (wv,)), (slr, (wvr,)) = pre
                            for j in range(ncol // 128):
                                for t in tiles:
                                    i = ri['i'] % 2
                                    ri['i'] += 1
                                    cs, t1, t2 = rp['cs'][i], rp['t1'][i], rp['t2'][i]
                                    K.dma('sp', cs[:, 0, :], cos_d[:, t * 512:(t + 1) * 512], w=[(cs, 0)])
                                    K.dma('sp', cs[:, 1, :], sin_d[:, t * 512:(t + 1) * 512], w=[(cs, 1)])
                                    ps, psr = psA(), psA()
                                    for kc in range(8):
                                        mm(ps[:, :], wv[:, kc, j * 128:(j + 1) * 128], hs(kc, t), kc == 0, kc == 7, [sl] + hk(t), [ps], kc == 7)
                                    for kc in range(8):
                                        mm(psr[:, :], wvr[:, kc, j * 128:(j + 1) * 128], hs(kc, t), kc == 0, kc == 7, [slr] + hk(t), [psr], kc == 7)
                                    K.op('dve', lambda e, ps=ps, cs=cs, t1=t1: e.tensor_tensor(out=t1[:], in0=ps[:, :], in1=cs[:, 0, :], op=ALU.mult), [ps, (cs, 0)], [t1])
                                    K.op('dve', lambda e, psr=psr, cs=cs, t2=t2: e.tensor_tensor(out=t2[:], in0=psr[:, :], in1=cs[:, 1, :], op=ALU.mult), [psr, (cs, 1)], [t2])
                                    d_ap, d_key = dst(j, t)
                                    if isinstance(d_ap, tuple):
                                        for hh in range(2):
                                            K.op('pool', lambda e, t1=t1, t2=t2, d_ap=d_ap, hh=hh: e.tensor_tensor(
                                                out=d_ap[hh], in0=t1[hh * 64:(hh + 1) * 64, :], in1=t2[hh * 64:(hh + 1) * 64, :], op=ALU.add), [t1, t2], [(d_key[0], (d_key[1], hh))])
                                    else:
                                        K.op('pool', lambda e, t1=t1, t2=t2, d_ap=d_ap: e.tensor_tensor(out=d_ap, in0=t1[:], in1=t2[:], op=ALU.add), [t1, t2], [d_key])

                        def ctx_kT(c_d, col0, dstK, dkey):
                            ps = psA()
                            for b in range(2):
                                K.dma('sp', cstg[:, b, 0:128], c_d[b * 128:(b + 1) * 128, col0:col0 + 128], w=[(cstg, b)])
                                tr(ps[:, b * 128:(b + 1) * 128], cstg[:, b, 0:128], [(cstg, b)], [ps], b == 1)
                            evac(dstK, ps[:, 0:256], [ps], [dkey])

                        with ExitStack() as ga:
                            stage_check(6)
                            alloc_slabs(ga, 3)
                            QAs = K.tile(ga, 'QAs', [128, 4, TS], BF16)
                            KAs = K.tile(ga, 'KAs', [128, 4352], BF16)
                            VAs = K.tile(ga, 'VAs', [128, 34, 2, 128], BF16)
                            rope_proj(wview(winx_d, QA, 512), wview(winx_d, QAR, 512), 512, range(4),
                                      lambda j, t: (QAs[:, j, t * 512:(t + 1) * 512], (QAs, (j, t))))
                            rope_proj(wview(winx_d, KA, 128), wview(winx_d, KAR, 128), 128, range(8),
                                      lambda j, t: (KAs[:, 256 + t * 512:256 + (t + 1) * 512], (KAs, t)))
                            ctx_kT(cak_d, 0, KAs[:, 0:256], (KAs, 'c'))
                            sl, (wv,) = wload([wview(winx_d, VA, 128)])
                            for blk in range(32):
                                ps = psA()
                                for kc in range(8):
                                    mm(ps[:, 0:128], hT[:, kc, blk * 128:(blk + 1) * 128], wv[:, kc, :], kc == 0, kc == 7, [sl, (hT, blk // 4)], [ps], kc == 7)
                                for u_ in range(2):
                                    K.op('dve', lambda e, blk=blk, ps=ps, u_=u_: e.tensor_copy(
                                        out=VAs[:, 2 + blk].rearrange("p g (u d) -> p g u d", u=2)[:, :, u_, :],
                                        in_=ps[:, 0:128].rearrange("p (g d) -> p g d", g=2)), [ps], [(VAs, (2 + blk, u_))])
                            K.dma('sp', cstg[:, :, 0:128], cav_d.rearrange("(b p) c -> p b c", p=128), w=[cstg])
                            for b in range(2):
                                for u_ in range(2):
                                    K.op('dve', lambda e, b=b, u_=u_: e.tensor_copy(
                                        out=VAs[:, b].rearrange("p g (u d) -> p g u d", u=2)[:, :, u_, :],
                                        in_=cstg[:, b, 0:128].rearrange("p (g d) -> p g d", g=2)), [cstg], [(VAs, (b, u_))])
                            for g in range(2):
                                for i in range(16):
                                    kbl = []
                                    blks = [(0, None), (1, None)] + ([(2 + i - 1, 0)] if i > 0 else []) + [(2 + i, None), (2 + i + 1, 1)]
                                    for (b, mi) in blks:
                                        kbl.append((KAs[g * 64:(g + 1) * 64, b * 128:(b + 1) * 128], KAs, VAs[:, b, g, :], VAs, mi))
                                    attn_A_unit(tmp, QAs[g * 64:(g + 1) * 64, 0:4, i * 128:(i + 1) * 128], QAs, kbl, g, OT, i * 128, esink)
                            K.barrier()
                        with ExitStack() as gb_:
                            stage_check(7)
                            QBs = K.tile(gb_, 'QBs', [128, 2, TS], BF16)
                            K.op('pool', lambda e: e.memset(QBs[:], 0.0), [], [QBs])
                            KBs = K.tile(gb_, 'KBs', [128, 4352], BF16)
                            VBs = K.tile(gb_, 'VBs', [128, 34, 128], BF16)
                            alloc_slabs(gb_, 10, 1024)

                            def load_head(hb_):
                                return [wload([wview(winx_d, off_ + hb_ * 128, 128)]) for off_ in (QB, QBR, KB, KBR, VB)]
                            nxt_w = load_head(0)
                            for hb in range(4):
                                cur_w = nxt_w
                                rope_proj(None, None, 128, range(4),
                                          lambda j, t: ((QBs[0:64, 0, t * 512:(t + 1) * 512], QBs[64:128, 1, t * 512:(t + 1) * 512]), (QBs, t)),
                                          pre=(cur_w[0], cur_w[1]))
                                rope_proj(None, None, 128, range(8),
                                          lambda j, t: (KBs[:, 256 + t * 512:256 + (t + 1) * 512], (KBs, t)), pre=(cur_w[2], cur_w[3]))
                                ctx_kT(cbk_d, hb * 128, KBs[:, 0:256], (KBs, 'c'))
                                sl, (wv,) = cur_w[4]
                                for blk in range(32):
                                    ps = psA()
                                    for kc in range(8):
                                        mm(ps[:, 0:128], hT[:, kc, blk * 128:(blk + 1) * 128], wv[:, kc, :], kc == 0, kc == 7, [sl, (hT, blk // 4)], [ps], kc == 7)
                                    evac(VBs[:, 2 + blk, :], ps[:, 0:128], [ps], [(VBs, 2 + blk)])
                                K.dma('sp', cstg[:, :, 0:128], cbv_d.rearrange("(b p) c -> p b c", p=128)[:, :, hb * 128:(hb + 1) * 128], w=[cstg])
                                K.op('dve', lambda e: e.tensor_copy(out=VBs[:, 0:2, :], in_=cstg[:, :, 0:128]), [cstg], [(VBs, 'c')])
                                if hb + 1 < 4:
                                    nxt_w = load_head(hb + 1)
                                for qt in range(8):
                                    q0 = qt * 256
                                    kbl = [(KBs[:, b * 128:(b + 1) * 128], KBs[:, b * 128:(b + 1) * 128], KBs, VBs[:, b, :], VBs) for b in range(34)]
                                    attn_B_unit(tmp, QBs[:, 0, q0:q0 + 256], QBs[:, 1, q0:q0 + 256], QBs, kbl, OT, 4 + hb, q0, neglam, subw)
                            K.barrier()
                        K.barrier()
                    stage_check(8)
                    xT = TlV(X64.t[:].bitcast(F32), 'xTs')
                    xk = lambda t: (xT, ('x', t))
                    with ExitStack() as sc:
                        alloc_slabs(sc, 4)
                        stg = dict(stg=[K.tile(sc, f'stg6{i}', [128, 1024], F32) for i in range(2)])
                        load_xT(stg, xso_d, 0, 16, xT, 0, 'x')
                        linear_fm([wview(wout_d, s * 512, 512) for s in range(2)], lambda kc, t: OT[:, kc, t * 512:(t + 1) * 512],
                                  lambda t: [OT], 4, resid_epi(xT, xk, 0, 0, 1))
                        K.barrier()
                hT2 = K.tile(S, 'hT2', [128, 8, TS], BF16)
                with ExitStack() as sc:
                    alloc_slabs(sc, 4)
                    tmp = dict(sq=K.tile(sc, 'sq7', [128, 8, 512], BF16), rstd=K.tile(sc, 'rstd7', [128, 512], F32),
                               u=[K.tile(sc, f'u7{i}', [128, 512], F32) for i in range(2)])
                    stage_check(9)
                    mlp(sc, xT, xk, TS, 0, 1, tmp, hT2, 4)
                    for t in range(4):
                        norm_mod(tmp, xT[:, :, t * 512:(t + 1) * 512], xk(t), lambda c, t=t: hT2[:, c, t * 512:(t + 1) * 512], (hT2, t), 1, 0, 1)
                    K.barrier()
                with ExitStack() as sc:
                    alloc_slabs(sc, 3, 2048)
                    stage_check(10)
                    stS = K.tile(sc, 'stS', [128, 10], F32)
                    sdn = K.tile(sc, 'sdn', [128, 10], F32)
                    exch = dict(xsrc1=Tl(None, 'xsrc1'), xdst1=Tl(None, 'xdst1'), xsrc2=Tl(None, 'xsrc2'), xdst2=Tl(None, 'xdst2'), stS=stS, sdn=sdn)
                    rec_layer(sc, xT, xk, TS, hT2, 1, TS, [([0], False), ([1], True)], 's', rwa_s_d, rwx_s_d, None, exch=exch)
                    K.barrier()
                with ExitStack() as sc:
                    alloc_slabs(sc, 4)
                    tmp = dict(sq=K.tile(sc, 'sq8', [128, 8, 512], BF16), rstd=K.tile(sc, 'rstd8', [128, 512], F32),
                               u=[K.tile(sc, f'u8{i}', [128, 512], F32) for i in range(2)])
                    stage_check(11)
                    mlp(sc, xT, xk, TS, 1, 1, tmp, hT2, 4)
                    K.barrier()
                with ExitStack() as sc:
                    tmp = dict(sq=K.tile(sc, 'sq9', [128, 8, 512], BF16), rstd=K.tile(sc, 'rstd9', [128, 512], F32),
                               yn=K.tile(sc, 'yn9', [128, 8, 512], F32))
                    ostg = [K.tile(sc, f'ostg9{i}', [128, 1024], F32) for i in range(2)]
                    final_out(tmp, xT, xk, TS, ys_d, ostg)
                    K.barrier()

        except StopBuild:
            pass
        K.finish()
    return nc


_ROT = np.concatenate([np.arange(16, 32), np.arange(0, 16), np.arange(48, 64), np.arange(32, 48)])


def _host_prepare(inp):
    f = np.float32
    g = lambda k: np.asarray(inp[k], dtype=f)
    SPL, NS = sp_layout()
    w_in = g('att_w_in')[0]
    qa_cols = np.concatenate([np.concatenate([np.arange(c * 64, c * 64 + 64), np.arange((4 + c) * 64, (4 + c) * 64 + 64)]) for c in range(4)])

    def rot_cols(cols):
        cols = np.asarray(cols).reshape(-1, 64)
        return cols[:, _ROT].reshape(-1)
    ka_cols = np.arange(512, 640); va_cols = np.arange(640, 768)
    qb_cols = np.arange(768, 1280); kb_cols = np.arange(1280, 1792); vb_cols = np.arange(1792, 2304)
    ext = np.concatenate([qa_cols, rot_cols(qa_cols), ka_cols, rot_cols(ka_cols), qb_cols, rot_cols(qb_cols),
                          kb_cols, rot_cols(kb_cols), va_cols, vb_cols])
    w_in_x = np.ascontiguousarray(w_in[:, ext])
    shared = dict(w_ada=g('w_ada'), w_mlp1=g('w_mlp1'), w_mlp2=g('w_mlp2'), w_in=np.ascontiguousarray(w_in), w_in_x=w_in_x,
                  w_out=np.ascontiguousarray(g('att_w_out')[0]), rec_w_in=np.ascontiguousarray(g('rec_w_in')[0]),
                  rec_w_out=np.ascontiguousarray(g('rec_w_out')[0]),
                  rwa_p=np.ascontiguousarray(g('rec_w_a')[0]), rwx_p=np.ascontiguousarray(g('rec_w_x')[0]),
                  ident=np.eye(128, dtype=f))
    kk = np.arange(128)[:, None]; qq = np.arange(128)[None, :]
    m = np.stack([np.tile((kk >= qq).astype(f), (1, 4)), np.tile((kk <= qq).astype(f), (1, 4))], axis=1)
    shared['masks'] = np.ascontiguousarray(m)
    p = np.arange(128); d = p % 64
    axis = d // 32; ii = d % 16; half = (d % 32) // 16
    inv = (10000.0 ** (-np.arange(16, dtype=np.float64) / 16))
    cw = g('rec_conv_w')[0]
    cw5_nat = np.concatenate([np.zeros((1, 1280), f), cw], 0)
    cw5_ref = np.concatenate([cw[::-1], np.zeros((1, 1280), f)], 0)
    pm = lambda v: np.ascontiguousarray(np.asarray(v, f).reshape(-1, 128).T)
    xp, xs = g('x_prompt'), g('x_sample')
    maps = []
    for i in range(8):
        b, hf = i // 2, i % 2
        loc = np.arange(4096) if hf == 0 else np.arange(4095, -1, -1)
        mp = dict(shared)
        mp['xp'] = np.ascontiguousarray(xp[4 * i:4 * i + 4].reshape(TP, 1024))
        mp['xs_own'] = np.ascontiguousarray(xs[b, loc[:2048]])
        mp['xs_oth'] = np.ascontiguousarray(xs[b, loc[2048:]])
        pos = np.stack([loc // 64, loc % 64], 0).astype(np.float64)
        ang = pos[axis][:, :] * inv[ii][:, None]
        mp['cos'] = np.cos(ang).astype(f)
        mp['sin'] = (np.sin(ang) * np.where(half == 0, -1.0, 1.0)[:, None]).astype(f)
        mp['cak'] = np.ascontiguousarray(g('cache_a_k')[b, 0].reshape(256, 128))
        mp['cav'] = np.ascontiguousarray(g('cache_a_v')[b, 0].reshape(256, 128))
        mp['cbk'] = np.ascontiguousarray(g('cache_b_k')[b, 0].reshape(256, 512))
        mp['cbv'] = np.ascontiguousarray(g('cache_b_v')[b, 0].reshape(256, 512))
        dsel = [0, 1] if hf == 0 else [1, 0]
        mp['rwa_s'] = np.ascontiguousarray(g('rec_w_a')[0][dsel])
        mp['rwx_s'] = np.ascontiguousarray(g('rec_w_x')[0][dsel])
        sp = np.zeros((128, NS), f)

        def put(name, arr):
            o, w = SPL[name]
            arr = np.asarray(arr, f)
            assert arr.shape == (128, w), (name, arr.shape, w)
            sp[:, o:o + w] = arr
        put('g1', np.concatenate([pm(g('norm1')[l]) for l in range(2)], 1))
        put('g2', np.concatenate([pm(g('norm2')[l]) for l in range(2)], 1))
        put('gf', pm(g('final_norm')))
        put('bada', np.concatenate([pm(g('b_ada')[l]) for l in range(2)], 1))
        put('convw_p', np.concatenate([pm(cw5_nat[k]) for k in range(5)], 1))
        put('convw_s', np.concatenate([pm((cw5_nat if hf == 0 else cw5_ref)[k]) for k in range(5)], 1))
        put('convb', pm(g('rec_conv_b')[0]))
        for nm, key in [('ba', 'rec_b_a'), ('bx', 'rec_b_x'), ('lam', 'rec_lam')]:
            v = g(key)[0]
            put(nm + '_p', np.concatenate([pm(v[0]), pm(v[1])], 1))
            put(nm + '_s', np.concatenate([pm(v[dsel[0]]), pm(v[dsel[1]])], 1))
        put('subln', g('att_subln')[0].reshape(128, 1))
        put('state_up', pm((g('state_fwd') if hf == 0 else g('state_bwd'))[b, 0]))
        put('sink', np.tile(g('att_sink')[0][None, :], (128, 1)))
        put('lamqk', np.tile(g('att_lam_qk')[0].reshape(1, 256), (128, 1)))
        put('sel', np.tile(np.array([[0.0, 1.0]] if hf == 0 else [[1.0, 0.0]], f), (128, 1)))
        cT = np.stack([pm(g('c_ctx')), pm(g('c')[b])], 2).reshape(128, 16)
        put('cT', cT)
        mp['sp'] = sp
        maps.append(mp)
    return maps


_CACHE = {}


def kernel(**inputs):
    if 'nc' not in _CACHE:
        _CACHE['nc'] = build_program()
    nc = _CACHE['nc']
    maps = _host_prepare(inputs)
    res = run_bass_kernel_spmd(nc, maps, core_ids=list(range(8)))
    R = res.results
    f = np.float32
    y_prompt = np.concatenate([R[i]['yp'].reshape(4, 256, 1024) for i in range(8)], 0).astype(f)
    y_sample = np.zeros((4, 4096, 1024), f)
    for i in range(8):
        b, hf = i // 2, i % 2
        ys = R[i]['ys']
        if hf == 0:
            y_sample[b, :2048] = ys
        else:
            y_sample[b, 2048:] = ys[::-1]
    nak = np.concatenate([R[i]['nak'].reshape(4, 1, 256, 2, 64) for i in range(8)], 0).astype(f)
    nav = np.concatenate([R[i]['nav'].reshape(4, 1, 256, 2, 64) for i in range(8)], 0).astype(f)
    nbk = np.concatenate([R[i]['nbk'].reshape(4, 1, 256, 4, 128) for i in range(8)], 0).astype(f)
    nbv = np.concatenate([R[i]['nbv'].reshape(4, 1, 256, 4, 128) for i in range(8)], 0).astype(f)
    nsf = np.concatenate([R[i]['nsf'].reshape(4, 1, 1280) for i in range(8)], 0).astype(f)
    nsb = np.concatenate([R[i]['nsb'].reshape(4, 1, 1280) for i in range(8)], 0).astype(f)
    return (y_prompt, y_sample, nak, nav, nbk, nbv, nsf, nsb)
```

```python
import numpy as np
from contextlib import ExitStack
import concourse.bass as bass
import concourse.mybir as mybir
from concourse.bass_utils import run_bass_kernel_spmd

F32, BF16 = mybir.dt.float32, mybir.dt.bfloat16
AF = mybir.ActivationFunctionType
ALU = mybir.AluOpType
NDS = 6
import os
NO_CC = os.environ.get('KNO_CC') == '1'
STAGE = float(os.environ.get('KSTAGE', '99'))


class StopBuild(Exception):
    pass


DEAD = [False]


def stage_check(k):
    if STAGE < k:
        DEAD[0] = True
EPS = 1e-6
SCALE = 0.125
LAM_INIT = 0.8 - 0.6 * 1.0
TP, TS = 1024, 2048
QA, QAR, KA, KAR, QB, QBR, KB, KBR, VA, VB, WX = 0, 512, 1024, 1152, 1280, 1792, 2304, 2816, 3328, 3456, 3968


def sp_layout():
    ents = [('g1', 16), ('g2', 16), ('gf', 8), ('bada', 96), ('convw_p', 50), ('convw_s', 50), ('convb', 10),
            ('ba_p', 20), ('bx_p', 20), ('lam_p', 20), ('ba_s', 20), ('bx_s', 20), ('lam_s', 20),
            ('subln', 1), ('state_up', 10), ('sink', 8), ('lamqk', 256), ('sel', 2), ('cT', 16)]
    off, d = 0, {}
    for n, w in ents:
        d[n] = (off, w)
        off += w
    return d, off


class Dep:
    __slots__ = ('w', 'r')

    def __init__(s):
        s.w = None
        s.r = {}


class Tl:
    def __init__(s, t, name):
        s.t = t
        s.name = name
        s.subs = {}

    def __getitem__(s, idx):
        return s.t[idx]


class KB_:
    def __init__(s, nc, es):
        s.nc, s.es = nc, es
        s.eng = {}
        for n in ['pe', 'act', 'dve', 'pool', 'sp']:
            s.eng[n] = dict(ops=[], sem=es.enter_context(nc.semaphore('s_' + n)), cnt=0, known={})
        s.dq = {}
        for q in ['sp', 'pool', 'act']:
            s.dq[q] = dict(sems=[es.enter_context(nc.semaphore(f'd_{q}{i}')) for i in range(NDS)], n=0)
        s.out_toks = []
        s.tiles = []
        s.cc_sem = es.enter_context(nc.semaphore('s_cc'))
        s.cc_n = 0
        s.H = {'pe': nc.tensor, 'act': nc.scalar, 'dve': nc.vector, 'pool': nc.gpsimd, 'sp': nc.sync}
        es.enter_context(nc.Block())

    def _emit(s, eng, waits, fn, inc):
        h = s.H[eng]
        for sem, val in waits:
            h.wait_ge(sem, val)
        if fn is not None:
            ins = fn(h)
            if inc is not None:
                ins.then_inc(inc[0], inc[1])

    def tile(s, es, name, shape, dt):
        t = Tl(es.enter_context(s.nc.sbuf_tensor('t_' + name, list(shape), dt)), name)
        return t

    def _conf(s, tl, sub):
        if sub is None:
            return list(tl.subs.values())
        return [tl.subs[k] for k in (sub, None) if k in tl.subs]

    @staticmethod
    def _ks(key):
        return key if isinstance(key, tuple) else (key, None)

    def _deps(s, r, w, tok):
        need = []
        for key in r:
            tl, sub = s._ks(key)
            for d in s._conf(tl, sub):
                if d.w:
                    need.append(d.w)
                if getattr(tl, 'psum', False):
                    need.extend(t for t in d.r.values() if t[0] is not tok[0])
        for key in w:
            tl, sub = s._ks(key)
            for d in s._conf(tl, sub):
                if d.w:
                    need.append(d.w)
                need.extend(d.r.values())
        for key in r:
            tl, sub = s._ks(key)
            d = tl.subs.setdefault(sub, Dep())
            d.r[tok[0].num] = tok
        for key in w:
            tl, sub = s._ks(key)
            if sub is None:
                tl.subs = {}
            d = tl.subs.setdefault(sub, Dep())
            d.w = tok
            d.r = {}
        return need

    def _need(s, eng, toks):
        kn = s.eng[eng]['known']
        waits = []
        for sem, val in toks:
            if kn.get(sem.num, 0) < val:
                kn[sem.num] = val
                waits.append((sem, val))
        return waits

    def op(s, eng, fn, r=(), w=(), inc=True):
        if DEAD[0]:
            return None
        e = s.eng[eng]
        tok = (e['sem'], e['cnt'] + 1)
        need = s._deps(r, w, tok)
        need = [t for t in need if not (t[0] is e['sem'] and (eng == 'pe' or t[1] > e['cnt']))]
        waits = s._need(eng, need)
        s._emit(eng, waits, fn, (e['sem'], 1) if inc else None)
        if inc:
            e['cnt'] += 1
        return tok

    def dma(s, q, out, in_, r=(), w=(), is_out=False, fn=None, **kw):
        if DEAD[0]:
            return None
        dq = s.dq[q]
        n = dq['n']
        sem = dq['sems'][n % NDS]
        val = 16 * (n // NDS + 1)
        dq['n'] += 1
        tok = (sem, val)
        need = s._deps(r, w, tok)
        if n >= NDS:
            need.append((sem, val - 16))
        waits = s._need(q, need)
        if fn is None:
            fn = lambda e: e.dma_start(out=out, in_=in_, **kw)
        s._emit(q, waits, fn, (sem, 16))
        if is_out:
            s.out_toks.append(tok)
        return tok

    def cc(s, fn, r=(), w=()):
        if DEAD[0]:
            return None
        s.cc_n += 1
        tok = (s.cc_sem, s.cc_n)
        need = s._deps(r, w, tok)
        waits = s._need('pool', need)
        h = s.H['pool']
        for sem, val in waits:
            h.wait_ge(sem, val)
        fn(h).then_inc(s.cc_sem)
        return tok

    def all_toks(s):
        toks = [(e['sem'], e['cnt']) for e in s.eng.values() if e['cnt'] > 0]
        if s.cc_n > 0:
            toks.append((s.cc_sem, s.cc_n))
        for dq in s.dq.values():
            n = dq['n']
            for i in range(min(n, NDS)):
                last = ((n - 1 - i) // NDS) * NDS + i if False else None
            for i, sem in enumerate(dq['sems']):
                cnt = (n - i + NDS - 1) // NDS if n > i else 0
                if cnt > 0:
                    toks.append((sem, 16 * cnt))
        return toks

    def barrier(s):
        if DEAD[0]:
            return
        toks = s.all_toks()
        for n in s.eng:
            waits = s._need(n, [t for t in toks if not (t[0] is s.eng[n]['sem'])])
            if waits:
                s._emit(n, waits, None, None)

    def finish(s):
        waits = s._need('sp', s.all_toks())
        s._emit('sp', waits, None, None)

    def emit(s):
        with s.nc.Block() as block:
            def mk(name):
                def f(e):
                    for waits, fn, inc in s.eng[name]['ops']:
                        for sem, val in waits:
                            e.wait_ge(sem, val)
                        if fn is not None:
                            ins = fn(e)
                            if inc is not None:
                                ins.then_inc(inc[0], inc[1])
                return f
            block.tensor(mk('pe'))
            block.scalar(mk('act'))
            block.vector(mk('dve'))
            block.gpsimd(mk('pool'))
            block.sync(mk('sp'))


def build_program():
    DEAD[0] = False
    nc = bass.Bass("TRN2", target_bir_lowering=False)
    SPL, NS = sp_layout()

    def din(name, shape):
        return nc.dram_tensor(name, list(shape), F32, kind="ExternalInput").ap()

    def dout(name, shape):
        return nc.dram_tensor(name, list(shape), F32, kind="ExternalOutput").ap()

    xp_d = din('xp', [TP, 1024]); xso_d = din('xs_own', [TS, 1024]); xst_d = din('xs_oth', [TS, 1024])
    cos_d = din('cos', [128, 4096]); sin_d = din('sin', [128, 4096])
    cak_d = din('cak', [256, 128]); cav_d = din('cav', [256, 128]); cbk_d = din('cbk', [256, 512]); cbv_d = din('cbv', [256, 512])
    sp_d = din('sp', [128, NS]); ident_d = din('ident', [128, 128]); mask_d = din('masks', [128, 2, 512])
    wada_d = din('w_ada', [2, 1024, 6144]); w1_d = din('w_mlp1', [2, 1024, 4096]); w2_d = din('w_mlp2', [2, 4096, 1024])
    win_d = din('w_in', [1024, 2304]); winx_d = din('w_in_x', [1024, WX]); wout_d = din('w_out', [1024, 1024])
    rwin_d = din('rec_w_in', [1024, 2560]); rwout_d = din('rec_w_out', [1280, 1024])
    rwa_p_d = din('rwa_p', [2, 10, 128, 128]); rwx_p_d = din('rwx_p', [2, 10, 128, 128])
    rwa_s_d = din('rwa_s', [2, 10, 128, 128]); rwx_s_d = din('rwx_s', [2, 10, 128, 128])
    yp_d = dout('yp', [TP, 1024]); ys_d = dout('ys', [TS, 1024])
    nak_d = dout('nak', [TP, 128]); nav_d = dout('nav', [TP, 128]); nbk_d = dout('nbk', [TP, 512]); nbv_d = dout('nbv', [TP, 512])
    nsf_d = dout('nsf', [4, 1280]); nsb_d = dout('nsb', [4, 1280])
    xsrc1 = nc.dram_tensor('xsrc1', [128, 128], F32).ap()
    xdst1 = nc.dram_tensor('xdst1', [128, 128], F32).ap()
    xsrc2 = nc.dram_tensor('xsrc2', [128, 128], F32).ap()
    xdst2 = nc.dram_tensor('xdst2', [128, 128], F32).ap()
    PAIRS = [[0, 1], [2, 3], [4, 5], [6, 7]]

    with ExitStack() as es:
        K = KB_(nc, es)
        PS = [Tl(es.enter_context(nc.psum_tensor(f'ps{i}', [128, 512], F32)), f'ps{i}') for i in range(8)]
        for p_ in PS:
            p_.psum = True
        ring = {'a': 0, 'b': 0}

        def psA():
            ring['a'] = (ring['a'] + 1) % 4
            return PS[ring['a']]

        def psB():
            ring['b'] = (ring['b'] + 1) % 4
            return PS[4 + ring['b']]

        ident = K.tile(es, 'ident', [128, 128], F32)
        ones = K.tile(es, 'ones', [128, 128], BF16)
        masks = K.tile(es, 'masks', [128, 2, 512], F32)
        spt = K.tile(es, 'spt', [128, NS], F32)
        modv = K.tile(es, 'modv', [128, 2, 48, 2], F32)
        Amod = K.tile(es, 'Amod', [128, 2, 2, 2, 8], F32)
        cons = K.tile(es, 'cons', [128, 64], F32)
        K.dma('sp', ident[:], ident_d, w=[ident])
        K.dma('sp', masks[:], mask_d, w=[masks])
        K.dma('sp', spt[:], sp_d, w=[spt])
        K.op('dve', lambda e: e.memset(ones[:], 1.0), w=[ones])

        def SP(name, a=0, b=None):
            o, w = SPL[name]
            b = w if b is None else b
            return spt[:, o + a:o + b]

        flip = {'i': 0}

        def evac(out, in_, r, w, scale=1.0, bias=0.0, eng=None):
            if eng is None:
                flip['i'] ^= 1
                eng = 'act' if flip['i'] else 'dve'
            if eng == 'act':
                K.op('act', lambda e: e.activation(out=out, in_=in_, func=AF.Identity, bias=bias, scale=scale), r, w)
            else:
                K.op('dve', lambda e: e.tensor_copy(out=out, in_=in_), r, w)

        def mm(out, lhsT, rhs, start, stop, r, w, last):
            K.op('pe', lambda e: e.matmul(out, lhsT, rhs, start=start, stop=stop), r, w, inc=last)

        def tr(out, in_, r, w, last):
            K.op('pe', lambda e: e.transpose(out, in_, ident[:]), list(r) + [ident], w, inc=last)

        slabs = []
        slab_i = {'i': 0}

        def alloc_slabs(scope, n, size=4096):
            slabs.clear()
            slab_i['size'] = size
            for i in range(n):
                slabs.append(K.tile(scope, f'slab{i}_{len(K.tiles)}', [128, size], BF16))
                K.tiles.append(None)
            slab_i['i'] = 0

        def wload(parts):
            sl = slabs[slab_i['i'] % len(slabs)]
            slab_i['i'] += 1
            views, off = [], 0
            for j, src in enumerate(parts):
                kc, n = src.shape[1], src.shape[2]
                v = sl[:, off:off + kc * n].rearrange("p (k n) -> p k n", k=kc)
                K.dma('pool', v, src, w=[(sl, j)] if len(parts) > 1 else [sl])
                views.append(v)
                off += kc * n
            assert off <= slab_i['size']
            return sl, views

        def wview(w2d, c0, n, k0=0, kc=None):
            v = w2d.rearrange("(k p) n -> p k n", p=128)
            kc = v.shape[1] - k0 if kc is None else kc
            return v[:, k0:k0 + kc, c0:c0 + n]

        with ExitStack() as sc:
            alloc_slabs(sc, 4)
            csil = K.tile(sc, 'csil', [128, 8, 2], BF16)
            K.op('act', lambda e: e.activation(out=csil[:], in_=SP('cT').rearrange("p (k j) -> p k j", j=2), func=AF.Silu),
                 [spt], [csil])
            for l in range(2):
                ps = psA()
                for blk in range(12):
                    sl, (wv,) = wload([wview(wada_d[l], blk * 512, 512)])
                    for j in range(4):
                        ch = blk * 4 + j
                        for kc in range(8):
                            mm(ps[:, ch * 2:ch * 2 + 2], wv[:, kc, j * 128:(j + 1) * 128], csil[:, kc, :],
                               kc == 0, kc == 7, [sl, csil], [ps], kc == 7)
                for cv in range(2):
                    K.op('dve', lambda e, l=l, cv=cv, ps=ps: e.tensor_tensor(
                        out=modv[:, l, :, cv], in0=ps[:, 0:96].rearrange("p (c j) -> p c j", j=2)[:, :, cv],
                        in1=SP('bada', l * 48, l * 48 + 48), op=ALU.add), [ps, spt], [(modv, (l, cv))])
                for nrm in range(2):
                    for cv in range(2):
                        g = SP('g1' if nrm == 0 else 'g2', l * 8, l * 8 + 8)
                        K.op('dve', lambda e, l=l, nrm=nrm, cv=cv, g=g: e.scalar_tensor_tensor(
                            out=Amod[:, l, nrm, cv, :], in0=modv[:, l, nrm * 24 + 8:nrm * 24 + 16, cv], scalar=1.0,
                            in1=g, op0=ALU.add, op1=ALU.mult), [(modv, (l, cv)), spt], [(Amod, (l, nrm, cv))])
            K.barrier()

        def MA(l, nrm, cv, c):
            return Amod[:, l, nrm, cv, c:c + 1]

        def MB(l, nrm, cv, c):
            return modv[:, l, nrm * 24 + c, cv:cv + 1]

        def MG(l, nrm, cv, c):
            return modv[:, l, nrm * 24 + 16 + c, cv:cv + 1]

        def load_xT(scope_tiles, x_d, t0, ntt, dst, dst_t0, T_key):
            stg = scope_tiles['stg']
            for tt in range(ntt):
                st = stg[tt % 2]
                K.dma('sp', st[:], x_d[t0 + tt * 128:t0 + (tt + 1) * 128, :], w=[st])
                for hb in range(2):
                    ps = psA()
                    for j in range(4):
                        c = hb * 4 + j
                        tr(ps[:, j * 128:(j + 1) * 128], st[:, c * 128:(c + 1) * 128], [st], [ps], j == 3)
                    o = dst_t0 + tt * 128
                    evac(dst[:, hb * 4:hb * 4 + 4, o:o + 128], ps[:, :].rearrange("p (c t) -> p c t", c=4),
                         [ps], [(dst, (T_key, o // 512))])

        def norm_stats(tmp, xsrc, xkey, n=512):
            sq, rstd = tmp['sq'], tmp['rstd']
            K.op('act', lambda e: e.activation(out=sq[:, :, 0:n], in_=xsrc, func=AF.Square), [xkey], [sq])
            ps = psA()
            for c in range(8):
                mm(ps[:, 0:n], ones[:], sq[:, c, 0:n], c == 0, c == 7, [ones, sq], [ps], c == 7)
            K.op('act', lambda e: e.activation(out=rstd[:, 0:n], in_=ps[:, 0:n], func=AF.Ln, bias=cons[:, 20:21], scale=1.0 / 1024),
                 [ps, cons], [rstd])
            K.op('act', lambda e: e.activation(out=rstd[:, 0:n], in_=rstd[:, 0:n], func=AF.Exp, scale=-0.5), [rstd], [rstd])
            return rstd

        def norm_mod(tmp, xsrc, xkey, hdst, hkey, l, nrm, cv, n=512):
            rstd = norm_stats(tmp, xsrc, xkey, n)
            u = tmp['u']
            for c in range(8):
                uc = u[c % 2]
                K.op('dve', lambda e, c=c, uc=uc: e.scalar_tensor_tensor(
                    out=uc[:, 0:n], in0=xsrc[:, c, :], scalar=MA(l, nrm, cv, c), in1=rstd[:, 0:n],
                    op0=ALU.mult, op1=ALU.mult), [xkey, rstd, Amod], [uc])
                K.op('act', lambda e, c=c, uc=uc: e.activation(out=hdst(c), in_=uc[:, 0:n], func=AF.Identity,
                                                               bias=MB(l, nrm, cv, c), scale=1.0), [uc, modv], [hkey])

        wq_plan, wq_ready = [], []

        def wq_top_up():
            while wq_plan and len(wq_ready) < len(slabs) - 1:
                wq_ready.append(wload([wq_plan.pop(0)]))

        def wq_get():
            wq_top_up()
            r_ = wq_ready.pop(0)
            wq_top_up()
            return r_

        def linear_fm(wparts, hsrc, hkeys, ntile, epi, tile0=0, planned=False):
            ci = 0
            if not planned:
                assert not wq_plan and not wq_ready
                wq_plan.extend(wparts)
            for src in wparts:
                sl, (wv,) = wq_get()
                kcn, n = src.shape[1], src.shape[2]
                for j in range(n // 128):
                    for t in range(tile0, tile0 + ntile):
                        ps = psA()
                        for kc in range(kcn):
                            mm(ps[:, :], wv[:, kc, j * 128:(j + 1) * 128], hsrc(kc, t), kc == 0, kc == kcn - 1,
                               [sl] + hkeys(t), [ps], kc == kcn - 1)
                        epi(ci, t, ps)
                    ci += 1

        def resid_epi(xT, xkey, l, nrm, cv):
            def epi(c, t, ps):
                K.op('dve', lambda e: e.scalar_tensor_tensor(
                    out=xT[:, c, t * 512:(t + 1) * 512], in0=ps[:, :], scalar=MG(l, nrm, cv, c),
                    in1=xT[:, c, t * 512:(t + 1) * 512], op0=ALU.mult, op1=ALU.add), [ps, modv, xkey(t)], [xkey(t)])
            return epi

        def mlp(scope, xT, xkeyf, T, l, cv, tmp, hT, nhalf):
            nt = T // 512
            HC = 32 // nhalf
            ncw = min(512, 4096 // HC)
            assert not wq_plan and not wq_ready
            for half in range(nhalf):
                wq_plan.extend([wview(w1_d[l], half * HC * 128 + s * 512, 512) for s in range(HC // 4)])
                wq_plan.extend([wview(w2_d[l], c * ncw, ncw, k0=half * HC, kc=HC) for c in range(1024 // ncw)])
            wq_top_up()
            for t in range(nt):
                norm_mod(tmp, xT[:, :, t * 512:(t + 1) * 512], xkeyf(t),
                         lambda c, t=t: hT[:, c, t * 512:(t + 1) * 512], (hT, t), l, 1, cv)
            hid = K.tile(scope, f'hid{l}{cv}', [128, HC, T], BF16)
            rl = [K.tile(scope, f'rl{l}{cv}{i}', [128, 512], F32) for i in range(2)]
            for half in range(nhalf):
                def epi1(c, t, ps):
                    r_ = rl[(c + t) % 2]
                    K.op('act', lambda e: e.activation(out=r_[:], in_=ps[:, :], func=AF.Relu), [ps], [r_])
                    K.op('pool', lambda e: e.tensor_tensor(out=hid[:, c, t * 512:(t + 1) * 512], in0=r_[:], in1=r_[:],
                                                           op=ALU.mult), [r_], [(hid, (c, t))])
                linear_fm([wview(w1_d[l], half * HC * 128 + s * 512, 512) for s in range(HC // 4)],
                          lambda kc, t: hT[:, kc, t * 512:(t + 1) * 512], lambda t: [(hT, t)], nt, epi1, planned=True)
                linear_fm([wview(w2_d[l], c * ncw, ncw, k0=half * HC, kc=HC) for c in range(1024 // ncw)],
                          lambda kc, t: hid[:, kc, t * 512:(t + 1) * 512], lambda t: [hid], nt,
                          resid_epi(xT, xkeyf, l, 1, cv), planned=True)

        def final_out(tmp, xT, xkeyf, T, y_d, ostg):
            for t in range(T // 512):
                xs = xT[:, :, t * 512:(t + 1) * 512]
                rstd = norm_stats(tmp, xs, xkeyf(t))
                yn = tmp['yn']
                for c in range(8):
                    K.op('dve', lambda e, c=c: e.scalar_tensor_tensor(
                        out=yn[:, c, :], in0=xs[:, c, :], scalar=SP('gf', c, c + 1), in1=rstd[:, 0:512],
                        op0=ALU.mult, op1=ALU.mult), [xkeyf(t), rstd, spt], [(yn, c)])
                for tt in range(4):
                    og = ostg[tt % 2]
                    for hb in range(2):
                        ps = psA()
                        for j in range(4):
                            c = hb * 4 + j
                            tr(ps[:, j * 128:(j + 1) * 128], yn[:, c, tt * 128:(tt + 1) * 128], [(yn, c)], [ps], j == 3)
                        evac(og[:, hb * 512:(hb + 1) * 512], ps[:, :], [ps], [(og, hb)])
                    r0 = t * 512 + tt * 128
                    K.dma('sp', y_d[r0:r0 + 128, :], og[:], r=[og], is_out=True)

        def attn_A_unit(tmp, qrhs, qkey, kblocks, g, OT, otok0, esink):
            O, D = psB(), psB()
            Pts = tmp['P']
            nb = len(kblocks)
            Ss = [None] * nb

            def qk(i):
                S = psA()
                mm(S[:, :], kblocks[i][0], qrhs, True, True, [kblocks[i][1], qkey], [S], True)
                Ss[i] = S
            qk(0)
            if nb > 1:
                qk(1)
            for i in range(nb):
                if i + 2 < nb:
                    qk(i + 2)
                Pt = Pts[tmp['pi'] % len(Pts)]
                tmp['pi'] += 1
                S = Ss[i]
                K.op('act', lambda e, S=S, Pt=Pt: e.activation(out=Pt[:], in_=S[:, :], func=AF.Exp, scale=SCALE), [S], [Pt])
                mi = kblocks[i][4]
                if mi is not None:
                    K.op('dve', lambda e, Pt=Pt, mi=mi: e.tensor_tensor(out=Pt[:], in0=Pt[:], in1=masks[:, mi, :], op=ALU.mult),
                         [Pt, masks], [Pt])
                mm(O[:, :], kblocks[i][2], Pt[:], i == 0, i == nb - 1, [kblocks[i][3], Pt], [O], i == nb - 1)
                mm(D[:, :], ones[:], Pt[:], i == 0, i == nb - 1, [ones, Pt], [D], i == nb - 1)
            rd = tmp['rd']
            for j in range(4):
                h = g * 4 + j
                K.op('act', lambda e, j=j, h=h: e.activation(out=rd[:, j * 128:(j + 1) * 128], in_=D[:, j * 128:(j + 1) * 128],
                                                             func=AF.Ln, bias=esink[:, h:h + 1], scale=1.0),
                     [D, cons], [(rd, j)])
            K.op('act', lambda e: e.activation(out=rd[:], in_=rd[:], func=AF.Exp, scale=-1.0), [rd], [rd])
            for j in range(4):
                p0 = (j % 2) * 64
                ch = g * 2 + j // 2
                K.op('dve', lambda e, j=j, p0=p0, ch=ch: e.tensor_tensor(
                    out=OT[p0:p0 + 64, ch, otok0:otok0 + 128], in0=O[p0:p0 + 64, j * 128:(j + 1) * 128],
                    in1=rd[p0:p0 + 64, j * 128:(j + 1) * 128], op=ALU.mult), [O, rd], [(OT, (ch, p0, otok0))])

        def attn_B_unit(tmp, q1, q2, qkey, kblocks, OT, och, otok0, neglam, subw):
            O, D = psB(), psB()
            Pts = tmp['P']
            nb = len(kblocks)
            Ss = [None] * nb

            def qk(i):
                S = psA()
                mm(S[:, 0:256], kblocks[i][0], q1, True, True, [kblocks[i][2], qkey], [S], False)
                mm(S[:, 256:512], kblocks[i][1], q2, True, True, [kblocks[i][2], qkey], [S], True)
                Ss[i] = S
            qk(0)
            if nb > 1:
                qk(1)
            for i in range(nb):
                if i + 2 < nb:
                    qk(i + 2)
                Pt = Pts[tmp['pi'] % len(Pts)]
                tmp['pi'] += 1
                S = Ss[i]
                K.op('act', lambda e, S=S, Pt=Pt: e.activation(out=Pt[:], in_=S[:, :], func=AF.Exp, scale=SCALE), [S], [Pt])
                mm(O[:, :], kblocks[i][3], Pt[:], i == 0, i == nb - 1, [kblocks[i][4], Pt], [O], i == nb - 1)
                mm(D[:, :], ones[:], Pt[:], i == 0, i == nb - 1, [ones, Pt], [D], i == nb - 1)
            rd, ob, sqb, rs = tmp['rd'], tmp['ob'], tmp['sqb'], tmp['rs']
            K.op('act', lambda e: e.activation(out=rd[:], in_=D[:, :], func=AF.Ln), [D], [rd])
            K.op('act', lambda e: e.activation(out=rd[:], in_=rd[:], func=AF.Exp, scale=-1.0), [rd], [rd])
            K.op('dve', lambda e: e.tensor_tensor(out=rd[:], in0=O[:, :], in1=rd[:], op=ALU.mult), [O, rd], [rd])
            K.op('dve', lambda e: e.scalar_tensor_tensor(out=ob[:], in0=rd[:, 256:512], scalar=neglam, in1=rd[:, 0:256],
                                                         op0=ALU.mult, op1=ALU.add), [rd, cons], [ob])
            K.op('act', lambda e: e.activation(out=sqb[:], in_=ob[:], func=AF.Square), [ob], [sqb])
            ps = psA()
            mm(ps[:, 0:256], ones[:], sqb[:], True, True, [ones, sqb], [ps], True)
            K.op('act', lambda e: e.activation(out=rs[:], in_=ps[:, 0:256], func=AF.Ln, bias=cons[:, 20:21], scale=1.0 / 128), [ps, cons], [rs])
            K.op('act', lambda e: e.activation(out=rs[:], in_=rs[:], func=AF.Exp, scale=-0.5), [rs], [rs])
            K.op('dve', lambda e: e.scalar_tensor_tensor(out=OT[:, och, otok0:otok0 + 256], in0=ob[:], scalar=subw, in1=rs[:],
                                                         op0=ALU.mult, op1=ALU.mult), [ob, rs, cons], [(OT, (och, 0, otok0))])

        lq = SP('lamqk')
        K.op('dve', lambda e: e.memset(cons[:], 0.0), [], [cons])
        K.op('dve', lambda e: e.memset(cons[:, 20:21], EPS), [cons], [cons])
        prod = K.tile(es, 'lprod', [128, 128], F32)
        K.op('dve', lambda e: e.tensor_tensor(out=prod[:].rearrange("p (a d) -> p a d", a=2),
                                              in0=lq.rearrange("p (a b d) -> p a b d", a=2, b=2)[:, :, 0, :],
                                              in1=lq.rearrange("p (a b d) -> p a b d", a=2, b=2)[:, :, 1, :], op=ALU.mult),
             [spt], [prod])
        K.op('dve', lambda e: e.reduce_sum(out=cons[:, 16:18], in_=prod[:].rearrange("p (a d) -> p a d", a=2),
                                           axis=mybir.AxisListType.X), [prod, cons], [cons])
        K.op('act', lambda e: e.activation(out=cons[:, 18:20], in_=cons[:, 16:18], func=AF.Exp), [cons], [cons])
        K.op('act', lambda e: e.activation(out=cons[:, 0:8], in_=SP('sink'), func=AF.Exp), [spt, cons], [cons])
        K.op('dve', lambda e: e.tensor_tensor(out=cons[:, 8:9], in0=cons[:, 19:20], in1=cons[:, 18:19], op=ALU.subtract), [cons], [cons])
        K.op('dve', lambda e: e.tensor_scalar(out=cons[:, 8:9], in0=cons[:, 8:9], scalar1=-LAM_INIT, scalar2=None, op0=ALU.add), [cons], [cons])
        K.op('dve', lambda e: e.tensor_scalar(out=cons[:, 9:10], in0=SP('subln'), scalar1=1.0 - LAM_INIT, scalar2=None, op0=ALU.mult),
             [spt, cons], [cons])
        esink, neglam, subw = cons, cons[:, 8:9], cons[:, 9:10]

        def exchange(scope, nm, data_ap, data_keys, W, xsrc, xdst, ksrc, kdst, out_ap, out_key):
            cb = K.tile(scope, 'cb' + nm, [128, 128], F32)
            hg = K.tile(scope, 'hg' + nm, [128, 128], F32)
            K.op('dve', lambda e: e.memset(cb[:], 0.0), [], [cb])
            K.op('dve', lambda e: e.tensor_scalar(out=cb[:, 0:W], in0=data_ap, scalar1=SP('sel', 1, 2), scalar2=None, op0=ALU.mult),
                 list(data_keys) + [spt, cb], [cb])
            K.op('dve', lambda e: e.tensor_scalar(out=cb[:, 64:64 + W], in0=data_ap, scalar1=SP('sel', 0, 1), scalar2=None, op0=ALU.mult),
                 list(data_keys) + [spt, cb], [cb])
            K.dma('sp', xsrc, cb[:], r=[cb], w=[ksrc])
            if NO_CC:
                K.dma('pool', xdst, xsrc, r=[ksrc], w=[kdst])
            else:
                K.cc(lambda e: e.collective_compute("AllReduce", ALU.add, replica_groups=PAIRS, ins=[xsrc.opt()], outs=[xdst.opt()]),
                     r=[ksrc], w=[kdst])
            K.dma('sp', hg[:], xdst, r=[kdst], w=[hg])
            K.op('dve', lambda e: e.tensor_scalar(out=out_ap, in0=hg[:, 0:W], scalar1=SP('sel', 0, 1), scalar2=None, op0=ALU.mult),
                 [hg, spt], [out_key])
            K.op('dve', lambda e: e.scalar_tensor_tensor(out=out_ap, in0=hg[:, 64:64 + W], scalar=SP('sel', 1, 2), in1=out_ap,
                                                         op0=ALU.mult, op1=ALU.add), [hg, spt, out_key], [out_key])

        def rec_layer(scope, xT, xkeyf, T, hT, cv, segL, phases, sfx, rwa_d, rwx_d, tmp=None, exch=None, state_out=None):
            nseg = T // segL
            nt = T // 512
            PL = 512
            npc = T // PL
            xrp2 = [K.tile(scope, f'xrp{i}' + sfx, [128, nseg, segL + 4], F32) for i in range(2)]
            xc2 = [K.tile(scope, f'xc{i}' + sfx, [128, 512], F32) for i in range(3)]
            xcb2 = [K.tile(scope, f'xcb{i}' + sfx, [128, 512], BF16) for i in range(3)]
            gb = K.tile(scope, 'gb' + sfx, [128, T], BF16)
            bA2 = [K.tile(scope, f'bA{i}' + sfx, [128, PL], F32) for i in range(2)]
            bR2 = [K.tile(scope, f'bR{i}' + sfx, [128, PL], F32) for i in range(2)]
            bI2 = [K.tile(scope, f'bI{i}' + sfx, [128, PL], F32) for i in range(2)]
            y = K.tile(scope, 'y' + sfx, [128, 10, T], BF16)
            wg2 = [K.tile(scope, f'wg{i}' + sfx, [128, 4, 128], BF16) for i in range(3)]
            carry = K.tile(scope, 'carry' + sfx, [128, 4], F32)
            cl = K.tile(scope, 'cl' + sfx, [128, 2, 2, 10], F32)
            pfx = '_p' if sfx == 'p' else '_s'
            lam = SP('lam' + pfx).rearrange("p (d c) -> p d c", d=2)
            K.op('act', lambda e: e.activation(out=cl[:, :, 0, :], in_=lam, func=AF.Exp, scale=-1.0), [spt], [cl])
            K.op('act', lambda e: e.activation(out=cl[:, :, 0, :], in_=cl[:, :, 0, :], func=AF.Ln, bias=1.0, scale=1.0), [cl], [cl])
            K.op('dve', lambda e: e.tensor_scalar(out=cl[:, :, 1, :], in0=cl[:, :, 0, :], scalar1=-16.0, scalar2=None, op0=ALU.mult), [cl], [cl])
            K.op('dve', lambda e: e.tensor_scalar(out=cl[:, :, 0, :], in0=cl[:, :, 0, :], scalar1=-8.0, scalar2=None, op0=ALU.mult), [cl], [cl])
            for xr_ in xrp2:
                K.op('pool', lambda e, xr_=xr_: e.memset(xr_[:], 0.0), [], [xr_])
            cw = SP('convw' + pfx)
            halo = None
            if exch is not None:
                hl = K.tile(scope, 'hl', [128, 10, 2], F32)
                halo = K.tile(scope, 'halo', [128, 10, 2], F32)
                for n in range(10):
                    sl, (wv,) = wload([wview(rwin_d, 1280 + n * 128, 128)])
                    ps = psA()
                    for kc in range(8):
                        mm(ps[:, 0:2], wv[:, kc, :], hT[:, kc, T - 2:T], kc == 0, kc == 7, [sl, (hT, nt - 1)], [ps], kc == 7)
                    evac(hl[:, n, :], ps[:, 0:2], [ps], [(hl, n)], eng='dve')
                exchange(scope, 'x1', hl[:].rearrange("p a b -> p (a b)"), [hl], 20, xsrc1, xdst1, exch['xsrc1'], exch['xdst1'],
                         halo[:].rearrange("p a b -> p (a b)"), halo)
            gcount = {'i': 0}
            for (dirs, final) in phases:
                if exch is not None and final:
                    exchange(scope, 'x2', exch['stS'][:], [exch['stS']], 10, xsrc2, xdst2, exch['xsrc2'], exch['xdst2'],
                             exch['sdn'][:], exch['sdn'])
                pend = {}

                wq = {}

                def wfetch(n, dirs=dirs, final=final):
                    parts = [wview(rwin_d, 1280 + n * 128, 128)]
                    if final:
                        parts.append(wview(rwin_d, n * 128, 128))
                    sl, wvs = wload(parts)
                    slk = [(sl, j) for j in range(len(parts))] if len(parts) > 1 else [sl]
                    wg = wg2[n % 3]
                    gparts = []
                    for d in dirs:
                        gparts += [rwa_d[d, n].rearrange("c (o d) -> c o d", o=1), rwx_d[d, n].rearrange("c (o d) -> c o d", o=1)]
                    for j, gp in enumerate(gparts):
                        K.dma('pool', wg[:, j:j + 1, :], gp, w=[(wg, j)])
                    wq[n] = (sl, wvs, slk)

                def inproj(n, dirs=dirs, final=final):
                    xrp = xrp2[n % 2]
                    sl, wvs, slk = wq.pop(n)
                    wg = wg2[n % 3]
                    for t in range(nt):
                        ps = psA()
                        for kc in range(8):
                            mm(ps[:, :], wvs[0][:, kc, :], hT[:, kc, t * 512:(t + 1) * 512], kc == 0, kc == 7,
                               [slk[0], (hT, t)], [ps], kc == 7)
                        if segL >= 512:
                            sgi, o = (t * 512) // segL, (t * 512) % segL
                            evac(xrp[:, sgi, 2 + o:2 + o + 512], ps[:, :], [ps], [(xrp, t)], eng='act')
                        else:
                            ns = 512 // segL
                            evac(xrp[:, t * ns:(t + 1) * ns, 2:2 + segL], ps[:, :].rearrange("p (s l) -> p s l", s=ns), [ps], [(xrp, t)], eng='act')
                    if halo is not None:
                        K.op('dve', lambda e, n=n: e.tensor_copy(out=xrp[:, 0, 2 + segL:4 + segL], in_=halo[:, n, :][:, ::-1]), [halo], [(xrp, 'halo')])
                    pend[n] = (sl, wvs, slk)

                def gateproj(n):
                    sl, wvs, slk = pend[n]
                    for t in range(nt):
                        ps2 = psA()
                        for kc in range(8):
                            mm(ps2[:, :], wvs[1][:, kc, :], hT[:, kc, t * 512:(t + 1) * 512], kc == 0, kc == 7,
                               [slk[1], (hT, t)], [ps2], kc == 7)
                        K.op('act', lambda e, t=t, ps2=ps2: e.activation(out=gb[:, t * 512:(t + 1) * 512], in_=ps2[:, :],
                                                                         func=AF.Gelu_apprx_tanh), [ps2], [(gb, t)])

                def stageA(n, pc, k):
                    xrp = xrp2[n % 2]
                    xcp, xbp = xc2[k % 3], xcb2[k % 3]
                    c0 = pc * PL
                    if segL >= PL:
                        segs = [(c0 // segL, c0 % segL, PL, 0)]
                    else:
                        segs = [(c0 // segL + s_, 0, segL, s_ * segL) for s_ in range(PL // segL)]
                    for (sgi, o, L_, bo) in segs:
                        o_ = xcp[:, bo:bo + L_]
                        K.op('dve', lambda e, sgi=sgi, o=o, L_=L_, o_=o_: e.tensor_scalar(
                            out=o_, in0=xrp[:, sgi, o:o + L_], scalar1=cw[:, n:n + 1], scalar2=SP('convb', n, n + 1),
                            op0=ALU.mult, op1=ALU.add), [xrp, spt], [xcp])
                        for k_ in range(1, 5):
                            K.op('dve', lambda e, sgi=sgi, o=o, L_=L_, o_=o_, k_=k_: e.scalar_tensor_tensor(
                                out=o_, in0=xrp[:, sgi, o + k_:o + k_ + L_], scalar=cw[:, k_ * 10 + n:k_ * 10 + n + 1], in1=o_,
                                op0=ALU.mult, op1=ALU.add), [xrp, spt, xcp], [xcp])
                    K.op('pool', lambda e: e.tensor_copy(out=xbp[:], in_=xcp[:]), [xcp], [xbp])

                def stageB(n, pc, k, dirs=dirs, final=final):
                    wg = wg2[n % 3]
                    xcp, xbp = xc2[k % 3], xcb2[k % 3]
                    c0 = pc * PL
                    for di, d in enumerate(dirs):
                        up = (d == 0)
                        gi = gcount['i'] % 2
                        gcount['i'] += 1
                        bA, bR, bI = bA2[gi], bR2[gi], bI2[gi]
                        psr, psi = psA(), psA()
                        mm(psr[:, :], wg[:, 2 * di, :], xbp[:], True, True, [(wg, 2 * di), xbp], [psr], True)
                        mm(psi[:, :], wg[:, 2 * di + 1, :], xbp[:], True, True, [(wg, 2 * di + 1), xbp], [psi], True)
                        K.op('act', lambda e, psr=psr, bR=bR, d=d: e.activation(
                            out=bR[:], in_=psr[:, :], func=AF.Sigmoid, bias=SP('ba' + pfx, d * 10 + n, d * 10 + n + 1), scale=1.0),
                            [psr, spt], [bR])
                        K.op('act', lambda e, psi=psi, bI=bI, d=d: e.activation(
                            out=bI[:], in_=psi[:, :], func=AF.Sigmoid, bias=SP('bx' + pfx, d * 10 + n, d * 10 + n + 1), scale=1.0),
                            [psi, spt], [bI])
                        K.op('act', lambda e, d=d, bA=bA, bR=bR: e.activation(out=bA[:], in_=bR[:], func=AF.Exp, scale=cl[:, d, 0, n:n + 1]), [bR, cl], [bA])
                        K.op('act', lambda e, d=d, bR=bR: e.activation(out=bR[:], in_=bR[:], func=AF.Exp, scale=cl[:, d, 1, n:n + 1]), [bR, cl], [bR])
                        K.op('pool', lambda e, bI=bI: e.tensor_tensor(out=bI[:], in0=bI[:], in1=xcp[:], op=ALU.mult), [bI, xcp], [bI])
                        K.op('act', lambda e, bR=bR: e.activation(out=bR[:], in_=bR[:], func=AF.Sqrt, bias=1.0, scale=-1.0), [bR], [bR])
                        K.op('pool', lambda e, bI=bI, bR=bR: e.tensor_tensor(out=bI[:], in0=bI[:], in1=bR[:], op=ALU.mult), [bI, bR], [bI])
                        nsc = PL // segL if segL < PL else 1
                        L = PL // nsc
                        seg_first = (c0 % segL == 0) if up else ((c0 + PL) % segL == 0)
                        for sc_ in range(nsc):
                            lo = sc_ * L
                            if segL <= PL or (seg_first and exch is None):
                                init, ik = 0.0, []
                            elif seg_first:
                                init = (SP('state_up', n, n + 1) if up else exch['sdn'][:, n:n + 1])
                                ik = [spt if up else exch['sdn']]
                            else:
                                init, ik = carry[:, 0:1], [carry]
                            if up:
                                K.op('dve', lambda e, lo=lo, L=L, init=init, bA=bA, bR=bR, bI=bI: e.tensor_tensor_scan(
                                    out=bR[:, lo:lo + L], data0=bA[:, lo:lo + L], data1=bI[:, lo:lo + L], initial=init,
                                    op0=ALU.mult, op1=ALU.add), [bA, bI] + ik, [bR])
                            else:
                                K.op('dve', lambda e, lo=lo, L=L, init=init, bA=bA, bR=bR, bI=bI: e.tensor_tensor_scan(
                                    out=bR[:, lo:lo + L][:, ::-1], data0=bA[:, lo:lo + L][:, ::-1], data1=bI[:, lo:lo + L][:, ::-1],
                                    initial=init, op0=ALU.mult, op1=ALU.add), [bA, bI] + ik, [bR])
                        last_col = PL - 1 if up else 0
                        if segL > PL:
                            K.op('dve', lambda e, last_col=last_col, bR=bR: e.tensor_copy(out=carry[:, 0:1], in_=bR[:, last_col:last_col + 1]), [bR], [carry])
                            if exch is not None and up and pc == npc - 1:
                                K.op('dve', lambda e, last_col=last_col, bR=bR: e.tensor_copy(out=exch['stS'][:, n:n + 1], in_=bR[:, last_col:last_col + 1]),
                                     [bR], [exch['stS']])
                        elif state_out is not None:
                            lc = segL - 1 if up else 0
                            ns_ = PL // segL
                            s0 = c0 // segL
                            K.op('dve', lambda e, d=d, lc=lc, bR=bR, s0=s0, ns_=ns_: e.tensor_copy(
                                out=state_out[:, d, s0:s0 + ns_, n], in_=bR[:].rearrange("p (s l) -> p s l", l=segL)[:, :, lc]), [bR], [(state_out, (d, n, s0))])
                        yv = y[:, n, c0:c0 + PL]
                        first = (di == 0 and not (final and len(dirs) == 1))
                        lastd = final and di == len(dirs) - 1
                        if first and not lastd:
                            K.op('pool', lambda e, yv=yv, bR=bR: e.tensor_copy(out=yv, in_=bR[:]), [bR], [(y, (n, pc))])
                        else:
                            if not first:
                                K.op('pool', lambda e, yv=yv, bR=bR: e.tensor_tensor(out=bR[:], in0=bR[:], in1=yv, op=ALU.add), [bR, (y, (n, pc))], [bR])
                            if lastd:
                                K.op('dve', lambda e, yv=yv, c0=c0, bR=bR: e.tensor_tensor(out=yv, in0=bR[:], in1=gb[:, c0:c0 + PL], op=ALU.mult),
                                     [bR, gb], [(y, (n, pc))])
                            else:
                                K.op('pool', lambda e, yv=yv, bR=bR: e.tensor_copy(out=yv, in_=bR[:]), [bR], [(y, (n, pc))])

                items = []
                for n in range(10):
                    porder = list(range(npc)) if dirs[0] == 0 else list(range(npc - 1, -1, -1))
                    items += [(n, pc) for pc in porder]
                wfetch(0)
                wfetch(1)
                inproj(0)
                stageA(items[0][0], items[0][1], 0)
                stageA(items[1][0], items[1][1], 1)
                for i, (n, pc) in enumerate(items):
                    if i % npc == 0:
                        if n + 1 < 10:
                            inproj(n + 1)
                        if n + 2 < 10:
                            wfetch(n + 2)
                        if final:
                            gateproj(n)
                    if i + 2 < len(items):
                        stageA(items[i + 2][0], items[i + 2][1], i + 2)
                    stageB(n, pc, i)
            l = 1
            linear_fm([wview(rwout_d, s * 128, 128) for s in range(8)], lambda kc, t: y[:, kc, t * 512:(t + 1) * 512],
                      lambda t: [y], nt, resid_epi(xT, xkeyf, l, 0, cv))

        try:
            stage_check(1)
            with ExitStack() as P:
                xT = K.tile(P, 'xTp', [128, 8, TP], F32)
                hT = K.tile(P, 'hTp', [128, 8, TP], BF16)
                xk = lambda t: (xT, ('x', t))
                with ExitStack() as sc:
                    alloc_slabs(sc, 4)
                    tmp = dict(sq=K.tile(sc, 'sq', [128, 8, 512], BF16), rstd=K.tile(sc, 'rstd', [128, 512], F32),
                               u=[K.tile(sc, f'u{i}', [128, 512], F32) for i in range(2)],
                               P=[K.tile(sc, f'P{i}', [128, 512], BF16) for i in range(4)], pi=0,
                               rd=K.tile(sc, 'rd', [128, 512], F32), ob=K.tile(sc, 'ob', [128, 256], F32),
                               sqb=K.tile(sc, 'sqb', [128, 256], BF16), rs=K.tile(sc, 'rs', [128, 256], F32))
                    stg = dict(stg=[K.tile(sc, f'stg{i}', [128, 1024], F32) for i in range(2)])
                    QK = K.tile(sc, 'QKp', [128, 9, TP], BF16)
                    VAt = K.tile(sc, 'VAp', [128, 8, 2, 128], BF16)
                    VBt = K.tile(sc, 'VBp', [128, 8, 512], BF16)
                    OT = K.tile(sc, 'OTp', [128, 8, TP], BF16)
                    kvs = [K.tile(sc, 'kvs0', [128, 8, 512], F32)] * 2
                    load_xT(stg, xp_d, 0, 8, xT, 0, 'x')
                    stage_check(1.1)
                    for t in range(2):
                        norm_mod(tmp, xT[:, :, t * 512:(t + 1) * 512], xk(t), lambda c, t=t: hT[:, c, t * 512:(t + 1) * 512], (hT, t), 0, 0, 0)
                    QBz = K.tile(sc, 'QBzp', [128, 4, 2, TP], BF16)
                    K.op('pool', lambda e: e.memset(QBz[:], 0.0), [], [QBz])

                    def epi_qk(c, t, ps):
                        if 5 <= c <= 8:
                            for hh in range(2):
                                evac(QBz[hh * 64:(hh + 1) * 64, c - 5, hh, t * 512:(t + 1) * 512], ps[hh * 64:(hh + 1) * 64, :], [ps], [(QBz, (c, t, hh))])
                        else:
                            cq = c if c < 5 else c - 4
                            evac(QK[:, cq, t * 512:(t + 1) * 512], ps[:, :], [ps], [(QK, (cq, t))])
                    stage_check(1.2)
                    linear_fm([wview(winx_d, QA, 512), wview(winx_d, KA, 128), wview(winx_d, QB, 512), wview(winx_d, KB, 512)],
                              lambda kc, t: hT[:, kc, t * 512:(t + 1) * 512], lambda t: [(hT, t)], 2, epi_qk)
                    stage_check(1.3)
                    assert not wq_plan and not wq_ready
                    wq_plan.extend([wview(win_d, c0_, n_) for (c0_, n_) in [(512, 256), (1280, 512), (1792, 512)]])
                    for si, (c0, n, o0) in enumerate([(512, 256, 0), (1280, 512, 256), (1792, 512, 768)]):
                        sl, (wv,) = wq_get()
                        for tt in range(8):
                            ps = psA()
                            for kc in range(8):
                                mm(ps[:, 0:n], hT[:, kc, tt * 128:(tt + 1) * 128], wv[:, kc, :], kc == 0, kc == 7, [sl, (hT, tt // 4)], [ps], kc == 7)
                            kv = kvs[si % 2]
                            evac(kv[:, tt, 0:n], ps[:, 0:n], [ps], [(kv, tt)])
                            if si == 0:
                                for u_ in range(2):
                                    K.op('dve', lambda e, tt=tt, ps=ps, u_=u_: e.tensor_copy(
                                        out=VAt[:, tt].rearrange("p g (u d) -> p g u d", u=2)[:, :, u_, :],
                                        in_=ps[:, 128:256].rearrange("p (g d) -> p g d", g=2)), [ps], [(VAt, (tt, u_))])
                            if si == 2:
                                evac(VBt[:, tt, :], ps[:, 0:512], [ps], [(VBt, tt)])
                        kv = kvs[si % 2]
                        for (d_, lo, w_) in ([(nak_d, 0, 128), (nav_d, 128, 128)] if si == 0 else [((nbk_d, nbv_d)[si - 1], 0, 512)]):
                            if os.environ.get('KNOOUT') == '1':
                                continue
                            if os.environ.get('KNOOUT') == '2':
                                for t8 in range(8):
                                    K.dma('sp', d_[t8 * 128:(t8 + 1) * 128, :], kv[:, t8, lo:lo + w_], r=[kv], is_out=True)
                                continue
                            K.dma('sp', d_.rearrange("(t p) c -> p t c", p=128), kv[:, :, lo:lo + w_], r=[kv], is_out=True)
                    stage_check(1.4)
                    wq_plan.extend([wview(wout_d, s * 512, 512) for s in range(2)])
                    wq_top_up()
                    for sq_ in range(4):
                        for g in range(2):
                            for qb in range(2):
                                q0 = sq_ * 256 + qb * 128
                                kbl = []
                                for kb in range(2):
                                    k0 = sq_ * 256 + kb * 128
                                    kbl.append((QK[g * 64:(g + 1) * 64, 4, k0:k0 + 128], QK, VAt[:, sq_ * 2 + kb, g, :], VAt, None))
                                attn_A_unit(tmp, QK[g * 64:(g + 1) * 64, 0:4, q0:q0 + 128], QK, kbl, g, OT, q0, esink)
                        for hb in range(4 if STAGE >= 1.5 else 0):
                            q0 = sq_ * 256
                            kbl = []
                            for kb in range(2):
                                k0 = sq_ * 256 + kb * 128
                                kbl.append((QK[:, 5 + hb, k0:k0 + 128], QK[:, 5 + hb, k0:k0 + 128], QK,
                                            VBt[:, sq_ * 2 + kb, hb * 128:(hb + 1) * 128], VBt))
                            attn_B_unit(tmp, QBz[:, hb, 0, q0:q0 + 256], QBz[:, hb, 1, q0:q0 + 256], QBz, kbl, OT, 4 + hb, q0, neglam, subw)
                    stage_check(1.6)
                    linear_fm([wview(wout_d, s * 512, 512) for s in range(2)], lambda kc, t: OT[:, kc, t * 512:(t + 1) * 512],
                              lambda t: [OT], 2, resid_epi(xT, xk, 0, 0, 0), planned=True)
                    K.barrier()
                with ExitStack() as sc:
                    alloc_slabs(sc, 4)
                    tmp = dict(sq=K.tile(sc, 'sq2', [128, 8, 512], BF16), rstd=K.tile(sc, 'rstd2', [128, 512], F32),
                               u=[K.tile(sc, f'u2{i}', [128, 512], F32) for i in range(2)])
                    stage_check(2)
                    mlp(sc, xT, xk, TP, 0, 0, tmp, hT, 1)
                    K.barrier()
                with ExitStack() as sc:
                    alloc_slabs(sc, 3)
                    tmp = dict(sq=K.tile(sc, 'sq3', [128, 8, 512], BF16), rstd=K.tile(sc, 'rstd3', [128, 512], F32),
                               u=[K.tile(sc, f'u3{i}', [128, 512], F32) for i in range(2)])
                    for t in range(2):
                        norm_mod(tmp, xT[:, :, t * 512:(t + 1) * 512], xk(t), lambda c, t=t: hT[:, c, t * 512:(t + 1) * 512], (hT, t), 1, 0, 0)
                    stage_check(3)
                    stp = K.tile(sc, 'stp', [128, 2, 4, 10], F32)
                    rec_layer(sc, xT, xk, TP, hT, 0, 256, [([0, 1], True)], 'p', rwa_p_d, rwx_p_d, tmp, state_out=stp)
                    ps = psA()
                    tr(ps[0:80, 0:128], stp[:].rearrange("p d s n -> p (d s n)"), [stp], [ps], True)
                    sto = K.tile(sc, 'sto', [80, 128], F32)
                    evac(sto[:], ps[0:80, 0:128], [ps], [sto], eng='dve')
                    for d, dd in enumerate([nsf_d, nsb_d]):
                        for s_ in range(4):
                            K.dma('sp', dd[s_].rearrange("(n p) -> n p", p=128), sto[d * 40 + s_ * 10:d * 40 + s_ * 10 + 10, :], r=[sto], is_out=True)
                    K.barrier()
                with ExitStack() as sc:
                    alloc_slabs(sc, 4)
                    tmp = dict(sq=K.tile(sc, 'sq4', [128, 8, 512], BF16), rstd=K.tile(sc, 'rstd4', [128, 512], F32),
                               u=[K.tile(sc, f'u4{i}', [128, 512], F32) for i in range(2)], yn=K.tile(sc, 'yn4', [128, 8, 512], F32))
                    stage_check(4)
                    mlp(sc, xT, xk, TP, 1, 0, tmp, hT, 1)
                    ostg = [K.tile(sc, f'ostg{i}', [128, 1024], F32) for i in range(2)]
                    final_out(tmp, xT, xk, TP, yp_d, ostg)
                    K.barrier()

            class TlV(Tl):
                def __init__(s, base_ap, name):
                    s.t = None
                    s.base = base_ap
                    s.name = name
                    s.subs = {}

                def __getitem__(s, idx):
                    return s.base[idx]

            with ExitStack() as S:
                stage_check(5)
                X64 = K.tile(S, 'X64', [128, 8, 4096], BF16)
                hT = X64
                with ExitStack() as S1:
                    OT = K.tile(S1, 'OTs', [128, 8, TS], BF16)
                    with ExitStack() as sc:
                        tmp = dict(sq=K.tile(sc, 'sq5', [128, 8, 512], BF16), rstd=K.tile(sc, 'rstd5', [128, 512], F32),
                                   u=[K.tile(sc, f'u5{i}', [128, 512], F32) for i in range(2)])
                        stg = dict(stg=[K.tile(sc, f'stg5{i}', [128, 1024], F32) for i in range(2)])
                        xt2 = [K.tile(sc, f'xt5{i}', [128, 8, 512], F32) for i in range(2)]
                        for t in range(8):
                            xt = xt2[t % 2]
                            load_xT(stg, xso_d if t < 4 else xst_d, (t % 4) * 512, 4, xt, 0, 'x')
                            norm_mod(tmp, xt[:, :, :], xt, lambda c, t=t: hT[:, c, t * 512:(t + 1) * 512], (hT, t), 0, 0, 1)
                        K.barrier()
                    hk = lambda t: [(hT, t)]
                    hs = lambda kc, t: hT[:, kc, t * 512:(t + 1) * 512]
                    with ExitStack() as sc:
                        rp = dict(cs=[K.tile(sc, f'cs{i}', [128, 2, 512], F32) for i in range(2)],
                                  t1=[K.tile(sc, f't1{i}', [128, 512], F32) for i in range(2)],
                                  t2=[K.tile(sc, f't2{i}', [128, 512], F32) for i in range(2)])
                        tmp = dict(P=[K.tile(sc, f'P5{i}', [128, 512], BF16) for i in range(4)], pi=0,
                                   rd=K.tile(sc, 'rd5', [128, 512], F32), ob=K.tile(sc, 'ob5', [128, 256], F32),
                                   sqb=K.tile(sc, 'sqb5', [128, 256], BF16), rs=K.tile(sc, 'rs5', [128, 256], F32))
                        cstg = K.tile(sc, 'cstg', [128, 2, 512], F32)
                        ri = {'i': 0}

                        def rope_proj(src_w, src_wr, ncol, tiles, dst, pre=None):
                            if pre is None:
                                sl, (wv,) = wload([src_w])
                                slr, (wvr,) = wload([src_wr])
                            else:
                                (sl, (wv,)), (slr, (wvr,)) = pre
                            for j in range(ncol // 128):
                                for t in tiles:
                                    i = ri['i'] % 2
                                    ri['i'] += 1
                                    cs, t1, t2 = rp['cs'][i], rp['t1'][i], rp['t2'][i]
                                    K.dma('sp', cs[:, 0, :], cos_d[:, t * 512:(t + 1) * 512], w=[(cs, 0)])
                                    K.dma('sp', cs[:, 1, :], sin_d[:, t * 512:(t + 1) * 512], w=[(cs, 1)])
                                    ps, psr = psA(), psA()
                                    for kc in range(8):
                                        mm(ps[:, :], wv[:, kc, j * 128:(j + 1) * 128], hs(kc, t), kc == 0, kc == 7, [sl] + hk(t), [ps], kc == 7)
                                    for kc in range(8):
                                        mm(psr[:, :], wvr[:, kc, j * 128:(j + 1) * 128], hs(kc, t), kc == 0, kc == 7, [slr] + hk(t), [psr], kc == 7)
                                    K.op('dve', lambda e, ps=ps, cs=cs, t1=t1: e.tensor_tensor(out=t1[:], in0=ps[:, :], in1=cs[:, 0, :], op=ALU.mult), [ps, (cs, 0)], [t1])
                                    K.op('dve', lambda e, psr=psr, cs=cs, t2=t2: e.tensor_tensor(out=t2[:], in0=psr[:, :], in1=cs[:, 1, :], op=ALU.mult), [psr, (cs, 1)], [t2])
                                    d_ap, d_key = dst(j, t)
                                    if isinstance(d_ap, tuple):
                                        for hh in range(2):
                                            K.op('pool', lambda e, t1=t1, t2=t2, d_ap=d_ap, hh=hh: e.tensor_tensor(
                                                out=d_ap[hh], in0=t1[hh * 64:(hh + 1) * 64, :], in1=t2[hh * 64:(hh + 1) * 64, :], op=ALU.add), [t1, t2], [(d_key[0], (d_key[1], hh))])
                                    else:
                                        K.op('pool', lambda e, t1=t1, t2=t2, d_ap=d_ap: e.tensor_tensor(out=d_ap, in0=t1[:], in1=t2[:], op=ALU.add), [t1, t2], [d_key])

                        def ctx_kT(c_d, col0, dstK, dkey):
                            ps = psA()
                            for b in range(2):
                                K.dma('sp', cstg[:, b, 0:128], c_d[b * 128:(b + 1) * 128, col0:col0 + 128], w=[(cstg, b)])
                                tr(ps[:, b * 128:(b + 1) * 128], cstg[:, b, 0:128], [(cstg, b)], [ps], b == 1)
                            evac(dstK, ps[:, 0:256], [ps], [dkey])

                        with ExitStack() as ga:
                            stage_check(6)
                            alloc_slabs(ga, 3)
                            QAs = K.tile(ga, 'QAs', [128, 4, TS], BF16)
                            KAs = K.tile(ga, 'KAs', [128, 4352], BF16)
                            VAs = K.tile(ga, 'VAs', [128, 34, 2, 128], BF16)
                            wqa, wqar = wload([wview(winx_d, QA, 512)]), wload([wview(winx_d, QAR, 512)])
                            slk_, (wkk_,) = wload([wview(winx_d, KA, 256)])
                            rope_proj(None, None, 512, range(4),
                                      lambda j, t: (QAs[:, j, t * 512:(t + 1) * 512], (QAs, (j, t))), pre=(wqa, wqar))
                            wva = wload([wview(winx_d, VA, 128)])
                            rope_proj(None, None, 128, range(8),
                                      lambda j, t: (KAs[:, 256 + t * 512:256 + (t + 1) * 512], (KAs, t)),
                                      pre=((slk_, (wkk_[:, :, 0:128],)), (slk_, (wkk_[:, :, 128:256],))))
                            ctx_kT(cak_d, 0, KAs[:, 0:256], (KAs, 'c'))
                            sl, (wv,) = wva
                            for blk in range(32):
                                ps = psA()
                                for kc in range(8):
                                    mm(ps[:, 0:128], hT[:, kc, blk * 128:(blk + 1) * 128], wv[:, kc, :], kc == 0, kc == 7, [sl, (hT, blk // 4)], [ps], kc == 7)
                                for u_ in range(2):
                                    K.op('dve', lambda e, blk=blk, ps=ps, u_=u_: e.tensor_copy(
                                        out=VAs[:, 2 + blk].rearrange("p g (u d) -> p g u d", u=2)[:, :, u_, :],
                                        in_=ps[:, 0:128].rearrange("p (g d) -> p g d", g=2)), [ps], [(VAs, (2 + blk, u_))])
                            K.dma('sp', cstg[:, :, 0:128], cav_d.rearrange("(b p) c -> p b c", p=128), w=[cstg])
                            for b in range(2):
                                for u_ in range(2):
                                    K.op('dve', lambda e, b=b, u_=u_: e.tensor_copy(
                                        out=VAs[:, b].rearrange("p g (u d) -> p g u d", u=2)[:, :, u_, :],
                                        in_=cstg[:, b, 0:128].rearrange("p (g d) -> p g d", g=2)), [cstg], [(VAs, (b, u_))])
                            for g in range(2):
                                for i in range(16):
                                    kbl = []
                                    blks = [(0, None), (1, None)] + ([(2 + i - 1, 0)] if i > 0 else []) + [(2 + i, None), (2 + i + 1, 1)]
                                    for (b, mi) in blks:
                                        kbl.append((KAs[g * 64:(g + 1) * 64, b * 128:(b + 1) * 128], KAs, VAs[:, b, g, :], VAs, mi))
                                    attn_A_unit(tmp, QAs[g * 64:(g + 1) * 64, 0:4, i * 128:(i + 1) * 128], QAs, kbl, g, OT, i * 128, esink)
                            K.barrier()
                        with ExitStack() as gb_:
                            stage_check(7)
                            QBs = K.tile(gb_, 'QBs', [128, 2, TS], BF16)
                            K.op('pool', lambda e: e.memset(QBs[:], 0.0), [], [QBs])
                            KBs = K.tile(gb_, 'KBs', [128, 4352], BF16)
                            VBs = K.tile(gb_, 'VBs', [128, 34, 128], BF16)
                            alloc_slabs(gb_, 10, 1024)

                            def load_head(hb_):
                                return [wload([wview(winx_d, off_ + hb_ * 128, 128)]) for off_ in (QB, QBR, KB, KBR, VB)]
                            nxt_w = load_head(0)
                            for hb in range(4):
                                cur_w = nxt_w
                                rope_proj(None, None, 128, range(4),
                                          lambda j, t: ((QBs[0:64, 0, t * 512:(t + 1) * 512], QBs[64:128, 1, t * 512:(t + 1) * 512]), (QBs, t)),
                                          pre=(cur_w[0], cur_w[1]))
                                rope_proj(None, None, 128, range(8),
                                          lambda j, t: (KBs[:, 256 + t * 512:256 + (t + 1) * 512], (KBs, t)), pre=(cur_w[2], cur_w[3]))
                                ctx_kT(cbk_d, hb * 128, KBs[:, 0:256], (KBs, 'c'))
                                sl, (wv,) = cur_w[4]
                                for blk in range(32):
                                    ps = psA()
                                    for kc in range(8):
                                        mm(ps[:, 0:128], hT[:, kc, blk * 128:(blk + 1) * 128], wv[:, kc, :], kc == 0, kc == 7, [sl, (hT, blk // 4)], [ps], kc == 7)
                                    evac(VBs[:, 2 + blk, :], ps[:, 0:128], [ps], [(VBs, 2 + blk)])
                                K.dma('sp', cstg[:, :, 0:128], cbv_d.rearrange("(b p) c -> p b c", p=128)[:, :, hb * 128:(hb + 1) * 128], w=[cstg])
                                K.op('dve', lambda e: e.tensor_copy(out=VBs[:, 0:2, :], in_=cstg[:, :, 0:128]), [cstg], [(VBs, 'c')])
                                if hb + 1 < 4:
                                    nxt_w = load_head(hb + 1)
                                for qt in range(8):
                                    q0 = qt * 256
                                    kbl = [(KBs[:, b * 128:(b + 1) * 128], KBs[:, b * 128:(b + 1) * 128], KBs, VBs[:, b, :], VBs) for b in range(34)]
                                    attn_B_unit(tmp, QBs[:, 0, q0:q0 + 256], QBs[:, 1, q0:q0 + 256], QBs, kbl, OT, 4 + hb, q0, neglam, subw)
                            K.barrier()
                        K.barrier()
                    stage_check(8)
                    xT = TlV(X64.t[:].bitcast(F32), 'xTs')
                    xk = lambda t: (xT, ('x', t))
                    with ExitStack() as sc:
                        alloc_slabs(sc, 4)
                        stg = dict(stg=[K.tile(sc, f'stg6{i}', [128, 1024], F32) for i in range(2)])
                        load_xT(stg, xso_d, 0, 16, xT, 0, 'x')
                        linear_fm([wview(wout_d, s * 512, 512) for s in range(2)], lambda kc, t: OT[:, kc, t * 512:(t + 1) * 512],
                                  lambda t: [OT], 4, resid_epi(xT, xk, 0, 0, 1))
                        K.barrier()
                hT2 = K.tile(S, 'hT2', [128, 8, TS], BF16)
                with ExitStack() as sc:
                    alloc_slabs(sc, 4)
                    tmp = dict(sq=K.tile(sc, 'sq7', [128, 8, 512], BF16), rstd=K.tile(sc, 'rstd7', [128, 512], F32),
                               u=[K.tile(sc, f'u7{i}', [128, 512], F32) for i in range(2)])
                    stage_check(9)
                    mlp(sc, xT, xk, TS, 0, 1, tmp, hT2, 4)
                    for t in range(4):
                        norm_mod(tmp, xT[:, :, t * 512:(t + 1) * 512], xk(t), lambda c, t=t: hT2[:, c, t * 512:(t + 1) * 512], (hT2, t), 1, 0, 1)
                    K.barrier()
                with ExitStack() as sc:
                    alloc_slabs(sc, 3, 2048)
                    stage_check(10)
                    stS = K.tile(sc, 'stS', [128, 10], F32)
                    sdn = K.tile(sc, 'sdn', [128, 10], F32)
                    exch = dict(xsrc1=Tl(None, 'xsrc1'), xdst1=Tl(None, 'xdst1'), xsrc2=Tl(None, 'xsrc2'), xdst2=Tl(None, 'xdst2'), stS=stS, sdn=sdn)
                    rec_layer(sc, xT, xk, TS, hT2, 1, TS, [([0], False), ([1], True)], 's', rwa_s_d, rwx_s_d, None, exch=exch)
                    K.barrier()
                with ExitStack() as sc:
                    alloc_slabs(sc, 4)
                    tmp = dict(sq=K.tile(sc, 'sq8', [128, 8, 512], BF16), rstd=K.tile(sc, 'rstd8', [128, 512], F32),
                               u=[K.tile(sc, f'u8{i}', [128, 512], F32) for i in range(2)])
                    stage_check(11)
                    mlp(sc, xT, xk, TS, 1, 1, tmp, hT2, 4)
                    K.barrier()
                with ExitStack() as sc:
                    tmp = dict(sq=K.tile(sc, 'sq9', [128, 8, 512], BF16), rstd=K.tile(sc, 'rstd9', [128, 512], F32),
                               yn=K.tile(sc, 'yn9', [128, 8, 512], F32))
                    ostg = [K.tile(sc, f'ostg9{i}', [128, 1024], F32) for i in range(2)]
                    final_out(tmp, xT, xk, TS, ys_d, ostg)
                    K.barrier()

        except StopBuild:
            pass
        K.finish()
    return nc


_ROT = np.concatenate([np.arange(16, 32), np.arange(0, 16), np.arange(48, 64), np.arange(32, 48)])


def _host_prepare(inp):
    f = np.float32
    g = lambda k: np.asarray(inp[k], dtype=f)
    SPL, NS = sp_layout()
    w_in = g('att_w_in')[0]
    qa_cols = np.concatenate([np.concatenate([np.arange(c * 64, c * 64 + 64), np.arange((4 + c) * 64, (4 + c) * 64 + 64)]) for c in range(4)])

    def rot_cols(cols):
        cols = np.asarray(cols).reshape(-1, 64)
        return cols[:, _ROT].reshape(-1)
    ka_cols = np.arange(512, 640); va_cols = np.arange(640, 768)
    qb_cols = np.arange(768, 1280); kb_cols = np.arange(1280, 1792); vb_cols = np.arange(1792, 2304)
    ext = np.concatenate([qa_cols, rot_cols(qa_cols), ka_cols, rot_cols(ka_cols), qb_cols, rot_cols(qb_cols),
                          kb_cols, rot_cols(kb_cols), va_cols, vb_cols])
    w_in_x = np.ascontiguousarray(w_in[:, ext])
    shared = dict(w_ada=g('w_ada'), w_mlp1=g('w_mlp1'), w_mlp2=g('w_mlp2'), w_in=np.ascontiguousarray(w_in), w_in_x=w_in_x,
                  w_out=np.ascontiguousarray(g('att_w_out')[0]), rec_w_in=np.ascontiguousarray(g('rec_w_in')[0]),
                  rec_w_out=np.ascontiguousarray(g('rec_w_out')[0]),
                  rwa_p=np.ascontiguousarray(g('rec_w_a')[0]), rwx_p=np.ascontiguousarray(g('rec_w_x')[0]),
                  ident=np.eye(128, dtype=f))
    kk = np.arange(128)[:, None]; qq = np.arange(128)[None, :]
    m = np.stack([np.tile((kk >= qq).astype(f), (1, 4)), np.tile((kk <= qq).astype(f), (1, 4))], axis=1)
    shared['masks'] = np.ascontiguousarray(m)
    p = np.arange(128); d = p % 64
    axis = d // 32; ii = d % 16; half = (d % 32) // 16
    inv = (10000.0 ** (-np.arange(16, dtype=np.float64) / 16))
    cw = g('rec_conv_w')[0]
    cw5_nat = np.concatenate([np.zeros((1, 1280), f), cw], 0)
    cw5_ref = np.concatenate([cw[::-1], np.zeros((1, 1280), f)], 0)
    pm = lambda v: np.ascontiguousarray(np.asarray(v, f).reshape(-1, 128).T)
    xp, xs = g('x_prompt'), g('x_sample')
    maps = []
    for i in range(8):
        b, hf = i // 2, i % 2
        loc = np.arange(4096) if hf == 0 else np.arange(4095, -1, -1)
        mp = dict(shared)
        mp['xp'] = np.ascontiguousarray(xp[4 * i:4 * i + 4].reshape(TP, 1024))
        mp['xs_own'] = np.ascontiguousarray(xs[b, loc[:2048]])
        mp['xs_oth'] = np.ascontiguousarray(xs[b, loc[2048:]])
        pos = np.stack([loc // 64, loc % 64], 0).astype(np.float64)
        ang = pos[axis][:, :] * inv[ii][:, None]
        mp['cos'] = np.cos(ang).astype(f)
        mp['sin'] = (np.sin(ang) * np.where(half == 0, -1.0, 1.0)[:, None]).astype(f)
        mp['cak'] = np.ascontiguousarray(g('cache_a_k')[b, 0].reshape(256, 128))
        mp['cav'] = np.ascontiguousarray(g('cache_a_v')[b, 0].reshape(256, 128))
        mp['cbk'] = np.ascontiguousarray(g('cache_b_k')[b, 0].reshape(256, 512))
        mp['cbv'] = np.ascontiguousarray(g('cache_b_v')[b, 0].reshape(256, 512))
        dsel = [0, 1] if hf == 0 else [1, 0]
        mp['rwa_s'] = np.ascontiguousarray(g('rec_w_a')[0][dsel])
        mp['rwx_s'] = np.ascontiguousarray(g('rec_w_x')[0][dsel])
        sp = np.zeros((128, NS), f)

        def put(name, arr):
            o, w = SPL[name]
            arr = np.asarray(arr, f)
            assert arr.shape == (128, w), (name, arr.shape, w)
            sp[:, o:o + w] = arr
        put('g1', np.concatenate([pm(g('norm1')[l]) for l in range(2)], 1))
        put('g2', np.concatenate([pm(g('norm2')[l]) for l in range(2)], 1))
        put('gf', pm(g('final_norm')))
        put('bada', np.concatenate([pm(g('b_ada')[l]) for l in range(2)], 1))
        put('convw_p', np.concatenate([pm(cw5_nat[k]) for k in range(5)], 1))
        put('convw_s', np.concatenate([pm((cw5_nat if hf == 0 else cw5_ref)[k]) for k in range(5)], 1))
        put('convb', pm(g('rec_conv_b')[0]))
        for nm, key in [('ba', 'rec_b_a'), ('bx', 'rec_b_x'), ('lam', 'rec_lam')]:
            v = g(key)[0]
            put(nm + '_p', np.concatenate([pm(v[0]), pm(v[1])], 1))
            put(nm + '_s', np.concatenate([pm(v[dsel[0]]), pm(v[dsel[1]])], 1))
        put('subln', g('att_subln')[0].reshape(128, 1))
        put('state_up', pm((g('state_fwd') if hf == 0 else g('state_bwd'))[b, 0]))
        put('sink', np.tile(g('att_sink')[0][None, :], (128, 1)))
        put('lamqk', np.tile(g('att_lam_qk')[0].reshape(1, 256), (128, 1)))
        put('sel', np.tile(np.array([[0.0, 1.0]] if hf == 0 else [[1.0, 0.0]], f), (128, 1)))
        cT = np.stack([pm(g('c_ctx')), pm(g('c')[b])], 2).reshape(128, 16)
        put('cT', cT)
        mp['sp'] = sp
        maps.append(mp)
    return maps


_CACHE = {}


def kernel(**inputs):
    if 'nc' not in _CACHE:
        _CACHE['nc'] = build_program()
    nc = _CACHE['nc']
    maps = _host_prepare(inputs)
    res = run_bass_kernel_spmd(nc, maps, core_ids=list(range(8)))
    R = res.results
    f = np.float32
    y_prompt = np.concatenate([R[i]['yp'].reshape(4, 256, 1024) for i in range(8)], 0).astype(f)
    y_sample = np.zeros((4, 4096, 1024), f)
    for i in range(8):
        b, hf = i // 2, i % 2
        ys = R[i]['ys']
        if hf == 0:
            y_sample[b, :2048] = ys
        else:
            y_sample[b, 2048:] = ys[::-1]
    nak = np.concatenate([R[i]['nak'].reshape(4, 1, 256, 2, 64) for i in range(8)], 0).astype(f)
    nav = np.concatenate([R[i]['nav'].reshape(4, 1, 256, 2, 64) for i in range(8)], 0).astype(f)
    nbk = np.concatenate([R[i]['nbk'].reshape(4, 1, 256, 4, 128) for i in range(8)], 0).astype(f)
    nbv = np.concatenate([R[i]['nbv'].reshape(4, 1, 256, 4, 128) for i in range(8)], 0).astype(f)
    nsf = np.concatenate([R[i]['nsf'].reshape(4, 1, 1280) for i in range(8)], 0).astype(f)
    nsb = np.concatenate([R[i]['nsb'].reshape(4, 1, 1280) for i in range(8)], 0).astype(f)
    return (y_prompt, y_sample, nak, nav, nbk, nbv, nsf, nsb)
```

```python
import numpy as np
from contextlib import ExitStack
import concourse.bass as bass
import concourse.mybir as mybir
from concourse.bass_utils import run_bass_kernel_spmd

F32, BF16 = mybir.dt.float32, mybir.dt.bfloat16
AF = mybir.ActivationFunctionType
ALU = mybir.AluOpType
NDS = 6
import os
NO_CC = os.environ.get('KNO_CC') == '1'
STAGE = float(os.environ.get('KSTAGE', '99'))


class StopBuild(Exception):
    pass


DEAD = [False]


def stage_check(k):
    if STAGE < k:
        DEAD[0] = True
EPS = 1e-6
SCALE = 0.125
LAM_INIT = 0.8 - 0.6 * 1.0
TP, TS = 1024, 2048
QA, QAR, KA, KAR, QB, QBR, KB, KBR, VA, VB, WX = 0, 512, 1024, 1152, 1280, 1792, 2304, 2816, 3328, 3456, 3968


def sp_layout():
    ents = [('g1', 16), ('g2', 16), ('gf', 8), ('bada', 96), ('convw_p', 50), ('convw_s', 50), ('convb', 10),
            ('ba_p', 20), ('bx_p', 20), ('lam_p', 20), ('ba_s', 20), ('bx_s', 20), ('lam_s', 20),
            ('subln', 1), ('state_up', 10), ('sink', 8), ('lamqk', 256), ('sel', 2), ('cT', 16)]
    off, d = 0, {}
    for n, w in ents:
        d[n] = (off, w)
        off += w
    return d, off


class Dep:
    __slots__ = ('w', 'r')

    def __init__(s):
        s.w = None
        s.r = {}


class Tl:
    def __init__(s, t, name):
        s.t = t
        s.name = name
        s.subs = {}

    def __getitem__(s, idx):
        return s.t[idx]


class KB_:
    def __init__(s, nc, es):
        s.nc, s.es = nc, es
        s.eng = {}
        for n in ['pe', 'act', 'dve', 'pool', 'sp']:
            s.eng[n] = dict(ops=[], sem=es.enter_context(nc.semaphore('s_' + n)), cnt=0, known={})
        s.dq = {}
        for q in ['sp', 'pool', 'act']:
            s.dq[q] = dict(sems=[es.enter_context(nc.semaphore(f'd_{q}{i}')) for i in range(NDS)], n=0)
        s.out_toks = []
        s.tiles = []
        s.cc_sem = es.enter_context(nc.semaphore('s_cc'))
        s.cc_n = 0
        s.H = {'pe': nc.tensor, 'act': nc.scalar, 'dve': nc.vector, 'pool': nc.gpsimd, 'sp': nc.sync}
        es.enter_context(nc.Block())

    def _emit(s, eng, waits, fn, inc):
        h = s.H[eng]
        for sem, val in waits:
            h.wait_ge(sem, val)
        if fn is not None:
            ins = fn(h)
            if inc is not None:
                ins.then_inc(inc[0], inc[1])

    def tile(s, es, name, shape, dt):
        t = Tl(es.enter_context(s.nc.sbuf_tensor('t_' + name, list(shape), dt)), name)
        return t

    def _conf(s, tl, sub):
        if sub is None:
            return list(tl.subs.values())
        return [tl.subs[k] for k in (sub, None) if k in tl.subs]

    @staticmethod
    def _ks(key):
        return key if isinstance(key, tuple) else (key, None)

    def _deps(s, r, w, tok):
        need = []
        for key in r:
            tl, sub = s._ks(key)
            for d in s._conf(tl, sub):
                if d.w:
                    need.append(d.w)
                if getattr(tl, 'psum', False):
                    need.extend(t for t in d.r.values() if t[0] is not tok[0])
        for key in w:
            tl, sub = s._ks(key)
            for d in s._conf(tl, sub):
                if d.w:
                    need.append(d.w)
                need.extend(d.r.values())
        for key in r:
            tl, sub = s._ks(key)
            d = tl.subs.setdefault(sub, Dep())
            d.r[tok[0].num] = tok
        for key in w:
            tl, sub = s._ks(key)
            if sub is None:
                tl.subs = {}
            d = tl.subs.setdefault(sub, Dep())
            d.w = tok
            d.r = {}
        return need

    def _need(s, eng, toks):
        kn = s.eng[eng]['known']
        waits = []
        for sem, val in toks:
            if kn.get(sem.num, 0) < val:
                kn[sem.num] = val
                waits.append((sem, val))
        return waits

    def op(s, eng, fn, r=(), w=(), inc=True):
        if DEAD[0]:
            return None
        e = s.eng[eng]
        tok = (e['sem'], e['cnt'] + 1)
        need = s._deps(r, w, tok)
        need = [t for t in need if not (t[0] is e['sem'] and (eng == 'pe' or t[1] > e['cnt']))]
        waits = s._need(eng, need)
        s._emit(eng, waits, fn, (e['sem'], 1) if inc else None)
        if inc:
            e['cnt'] += 1
        return tok

    def dma(s, q, out, in_, r=(), w=(), is_out=False, fn=None, **kw):
        if DEAD[0]:
            return None
        dq = s.dq[q]
        n = dq['n']
        sem = dq['sems'][n % NDS]
        val = 16 * (n // NDS + 1)
        dq['n'] += 1
        tok = (sem, val)
        need = s._deps(r, w, tok)
        if n >= NDS:
            need.append((sem, val - 16))
        waits = s._need(q, need)
        if fn is None:
            fn = lambda e: e.dma_start(out=out, in_=in_, **kw)
        s._emit(q, waits, fn, (sem, 16))
        if is_out:
            s.out_toks.append(tok)
        return tok

    def cc(s, fn, r=(), w=()):
        if DEAD[0]:
            return None
        s.cc_n += 1
        tok = (s.cc_sem, s.cc_n)
        need = s._deps(r, w, tok)
        waits = s._need('pool', need)
        h = s.H['pool']
        for sem, val in waits:
            h.wait_ge(sem, val)
        fn(h).then_inc(s.cc_sem)
        return tok

    def all_toks(s):
        toks = [(e['sem'], e['cnt']) for e in s.eng.values() if e['cnt'] > 0]
        if s.cc_n > 0:
            toks.append((s.cc_sem, s.cc_n))
        for dq in s.dq.values():
            n = dq['n']
            for i in range(min(n, NDS)):
                last = ((n - 1 - i) // NDS) * NDS + i if False else None
            for i, sem in enumerate(dq['sems']):
                cnt = (n - i + NDS - 1) // NDS if n > i else 0
                if cnt > 0:
                    toks.append((sem, 16 * cnt))
        return toks

    def barrier(s):
        if DEAD[0]:
            return
        toks = s.all_toks()
        for n in s.eng:
            waits = s._need(n, [t for t in toks if not (t[0] is s.eng[n]['sem'])])
            if waits:
                s._emit(n, waits, None, None)

    def finish(s):
        waits = s._need('sp', s.all_toks())
        s._emit('sp', waits, None, None)

    def emit(s):
        with s.nc.Block() as block:
            def mk(name):
                def f(e):
                    for waits, fn, inc in s.eng[name]['ops']:
                        for sem, val in waits:
                            e.wait_ge(sem, val)
                        if fn is not None:
                            ins = fn(e)
                            if inc is not None:
                                ins.then_inc(inc[0], inc[1])
                return f
            block.tensor(mk('pe'))
            block.scalar(mk('act'))
            block.vector(mk('dve'))
            block.gpsimd(mk('pool'))
            block.sync(mk('sp'))


def build_program():
    DEAD[0] = False
    nc = bass.Bass("TRN2", target_bir_lowering=False)
    SPL, NS = sp_layout()

    def din(name, shape):
        return nc.dram_tensor(name, list(shape), F32, kind="ExternalInput").ap()

    def dout(name, shape):
        return nc.dram_tensor(name, list(shape), F32, kind="ExternalOutput").ap()

    xp_d = din('xp', [TP, 1024]); xso_d = din('xs_own', [TS, 1024]); xst_d = din('xs_oth', [TS, 1024])
    cos_d = din('cos', [128, 4096]); sin_d = din('sin', [128, 4096])
    cak_d = din('cak', [256, 128]); cav_d = din('cav', [256, 128]); cbk_d = din('cbk', [256, 512]); cbv_d = din('cbv', [256, 512])
    sp_d = din('sp', [128, NS]); ident_d = din('ident', [128, 128]); mask_d = din('masks', [128, 2, 512])
    wada_d = din('w_ada', [2, 1024, 6144]); w1_d = din('w_mlp1', [2, 1024, 4096]); w2_d = din('w_mlp2', [2, 4096, 1024])
    win_d = din('w_in', [1024, 2304]); winx_d = din('w_in_x', [1024, WX]); wout_d = din('w_out', [1024, 1024])
    rwin_d = din('rec_w_in', [1024, 2560]); rwout_d = din('rec_w_out', [1280, 1024])
    rwa_p_d = din('rwa_p', [2, 10, 128, 128]); rwx_p_d = din('rwx_p', [2, 10, 128, 128])
    rwa_s_d = din('rwa_s', [2, 10, 128, 128]); rwx_s_d = din('rwx_s', [2, 10, 128, 128])
    yp_d = dout('yp', [TP, 1024]); ys_d = dout('ys', [TS, 1024])
    nak_d = dout('nak', [TP, 128]); nav_d = dout('nav', [TP, 128]); nbk_d = dout('nbk', [TP, 512]); nbv_d = dout('nbv', [TP, 512])
    nsf_d = dout('nsf', [4, 1280]); nsb_d = dout('nsb', [4, 1280])
    xsrc1 = nc.dram_tensor('xsrc1', [128, 128], F32).ap()
    xdst1 = nc.dram_tensor('xdst1', [128, 128], F32).ap()
    xsrc2 = nc.dram_tensor('xsrc2', [128, 128], F32).ap()
    xdst2 = nc.dram_tensor('xdst2', [128, 128], F32).ap()
    PAIRS = [[0, 1], [2, 3], [4, 5], [6, 7]]

    with ExitStack() as es:
        K = KB_(nc, es)
        PS = [Tl(es.enter_context(nc.psum_tensor(f'ps{i}', [128, 512], F32)), f'ps{i}') for i in range(8)]
        for p_ in PS:
            p_.psum = True
        ring = {'a': 0, 'b': 0}

        def psA():
            ring['a'] = (ring['a'] + 1) % 4
            return PS[ring['a']]

        def psB():
            ring['b'] = (ring['b'] + 1) % 4
            return PS[4 + ring['b']]

        ident = K.tile(es, 'ident', [128, 128], F32)
        ones = K.tile(es, 'ones', [128, 128], BF16)
        masks = K.tile(es, 'masks', [128, 2, 512], F32)
        spt = K.tile(es, 'spt', [128, NS], F32)
        modv = K.tile(es, 'modv', [128, 2, 48, 2], F32)
        Amod = K.tile(es, 'Amod', [128, 2, 2, 2, 8], F32)
        cons = K.tile(es, 'cons', [128, 64], F32)
        K.dma('sp', ident[:], ident_d, w=[ident])
        K.dma('sp', masks[:], mask_d, w=[masks])
        K.dma('sp', spt[:], sp_d, w=[spt])
        K.op('dve', lambda e: e.memset(ones[:], 1.0), w=[ones])

        def SP(name, a=0, b=None):
            o, w = SPL[name]
            b = w if b is None else b
            return spt[:, o + a:o + b]

        flip = {'i': 0}

        def evac(out, in_, r, w, scale=1.0, bias=0.0, eng=None):
            if eng is None:
                flip['i'] ^= 1
                eng = 'act' if flip['i'] else 'dve'
            if eng == 'act':
                K.op('act', lambda e: e.activation(out=out, in_=in_, func=AF.Identity, bias=bias, scale=scale), r, w)
            else:
                K.op('dve', lambda e: e.tensor_copy(out=out, in_=in_), r, w)

        def mm(out, lhsT, rhs, start, stop, r, w, last):
            K.op('pe', lambda e: e.matmul(out, lhsT, rhs, start=start, stop=stop), r, w, inc=last)

        def tr(out, in_, r, w, last):
            K.op('pe', lambda e: e.transpose(out, in_, ident[:]), list(r) + [ident], w, inc=last)

        slabs = []
        slab_i = {'i': 0}

        def alloc_slabs(scope, n, size=4096):
            slabs.clear()
            slab_i['size'] = size
            for i in range(n):
                slabs.append(K.tile(scope, f'slab{i}_{len(K.tiles)}', [128, size], BF16))
                K.tiles.append(None)
            slab_i['i'] = 0

        def wload(parts):
            sl = slabs[slab_i['i'] % len(slabs)]
            slab_i['i'] += 1
            views, off = [], 0
            for j, src in enumerate(parts):
                kc, n = src.shape[1], src.shape[2]
                v = sl[:, off:off + kc * n].rearrange("p (k n) -> p k n", k=kc)
                K.dma('pool', v, src, w=[(sl, j)] if len(parts) > 1 else [sl])
                views.append(v)
                off += kc * n
            assert off <= slab_i['size']
            return sl, views

        def wview(w2d, c0, n, k0=0, kc=None):
            v = w2d.rearrange("(k p) n -> p k n", p=128)
            kc = v.shape[1] - k0 if kc is None else kc
            return v[:, k0:k0 + kc, c0:c0 + n]

        with ExitStack() as sc:
            alloc_slabs(sc, 4)
            csil = K.tile(sc, 'csil', [128, 8, 2], BF16)
            K.op('act', lambda e: e.activation(out=csil[:], in_=SP('cT').rearrange("p (k j) -> p k j", j=2), func=AF.Silu),
                 [spt], [csil])
            for l in range(2):
                ps = psA()
                for blk in range(12):
                    sl, (wv,) = wload([wview(wada_d[l], blk * 512, 512)])
                    for j in range(4):
                        ch = blk * 4 + j
                        for kc in range(8):
                            mm(ps[:, ch * 2:ch * 2 + 2], wv[:, kc, j * 128:(j + 1) * 128], csil[:, kc, :],
                               kc == 0, kc == 7, [sl, csil], [ps], kc == 7)
                for cv in range(2):
                    K.op('dve', lambda e, l=l, cv=cv, ps=ps: e.tensor_tensor(
                        out=modv[:, l, :, cv], in0=ps[:, 0:96].rearrange("p (c j) -> p c j", j=2)[:, :, cv],
                        in1=SP('bada', l * 48, l * 48 + 48), op=ALU.add), [ps, spt], [(modv, (l, cv))])
                for nrm in range(2):
                    for cv in range(2):
                        g = SP('g1' if nrm == 0 else 'g2', l * 8, l * 8 + 8)
                        K.op('dve', lambda e, l=l, nrm=nrm, cv=cv, g=g: e.scalar_tensor_tensor(
                            out=Amod[:, l, nrm, cv, :], in0=modv[:, l, nrm * 24 + 8:nrm * 24 + 16, cv], scalar=1.0,
                            in1=g, op0=ALU.add, op1=ALU.mult), [(modv, (l, cv)), spt], [(Amod, (l, nrm, cv))])
            K.barrier()

        def MA(l, nrm, cv, c):
            return Amod[:, l, nrm, cv, c:c + 1]

        def MB(l, nrm, cv, c):
            return modv[:, l, nrm * 24 + c, cv:cv + 1]

        def MG(l, nrm, cv, c):
            return modv[:, l, nrm * 24 + 16 + c, cv:cv + 1]

        def load_xT(scope_tiles, x_d, t0, ntt, dst, dst_t0, T_key):
            stg = scope_tiles['stg']
            for tt in range(ntt):
                st = stg[tt % 2]
                K.dma('sp', st[:], x_d[t0 + tt * 128:t0 + (tt + 1) * 128, :], w=[st])
                for hb in range(2):
                    ps = psA()
                    for j in range(4):
                        c = hb * 4 + j
                        tr(ps[:, j * 128:(j + 1) * 128], st[:, c * 128:(c + 1) * 128], [st], [ps], j == 3)
                    o = dst_t0 + tt * 128
                    evac(dst[:, hb * 4:hb * 4 + 4, o:o + 128], ps[:, :].rearrange("p (c t) -> p c t", c=4),
                         [ps], [(dst, (T_key, o // 512))])

        def norm_stats(tmp, xsrc, xkey, n=512):
            sq, rstd = tmp['sq'], tmp['rstd']
            K.op('act', lambda e: e.activation(out=sq[:, :, 0:n], in_=xsrc, func=AF.Square), [xkey], [sq])
            ps = psA()
            for c in range(8):
                mm(ps[:, 0:n], ones[:], sq[:, c, 0:n], c == 0, c == 7, [ones, sq], [ps], c == 7)
            K.op('act', lambda e: e.activation(out=rstd[:, 0:n], in_=ps[:, 0:n], func=AF.Ln, bias=cons[:, 20:21], scale=1.0 / 1024),
                 [ps, cons], [rstd])
            K.op('act', lambda e: e.activation(out=rstd[:, 0:n], in_=rstd[:, 0:n], func=AF.Exp, scale=-0.5), [rstd], [rstd])
            return rstd

        def norm_mod(tmp, xsrc, xkey, hdst, hkey, l, nrm, cv, n=512):
            rstd = norm_stats(tmp, xsrc, xkey, n)
            u = tmp['u']
            for c in range(8):
                uc = u[c % 2]
                K.op('dve', lambda e, c=c, uc=uc: e.scalar_tensor_tensor(
                    out=uc[:, 0:n], in0=xsrc[:, c, :], scalar=MA(l, nrm, cv, c), in1=rstd[:, 0:n],
                    op0=ALU.mult, op1=ALU.mult), [xkey, rstd, Amod], [uc])
                K.op('act', lambda e, c=c, uc=uc: e.activation(out=hdst(c), in_=uc[:, 0:n], func=AF.Identity,
                                                               bias=MB(l, nrm, cv, c), scale=1.0), [uc, modv], [hkey])

        wq_plan, wq_ready = [], []

        def wq_top_up():
            while wq_plan and len(wq_ready) < len(slabs) - 1:
                wq_ready.append(wload([wq_plan.pop(0)]))

        def wq_get():
            wq_top_up()
            r_ = wq_ready.pop(0)
            wq_top_up()
            return r_

        def linear_fm(wparts, hsrc, hkeys, ntile, epi, tile0=0, planned=False):
            ci = 0
            if not planned:
                assert not wq_plan and not wq_ready
                wq_plan.extend(wparts)
            for src in wparts:
                sl, (wv,) = wq_get()
                kcn, n = src.shape[1], src.shape[2]
                for j in range(n // 128):
                    for t in range(tile0, tile0 + ntile):
                        ps = psA()
                        for kc in range(kcn):
                            mm(ps[:, :], wv[:, kc, j * 128:(j + 1) * 128], hsrc(kc, t), kc == 0, kc == kcn - 1,
                               [sl] + hkeys(t), [ps], kc == kcn - 1)
                        epi(ci, t, ps)
                    ci += 1

        def resid_epi(xT, xkey, l, nrm, cv):
            def epi(c, t, ps):
                K.op('dve', lambda e: e.scalar_tensor_tensor(
                    out=xT[:, c, t * 512:(t + 1) * 512], in0=ps[:, :], scalar=MG(l, nrm, cv, c),
                    in1=xT[:, c, t * 512:(t + 1) * 512], op0=ALU.mult, op1=ALU.add), [ps, modv, xkey(t)], [xkey(t)])
            return epi

        def mlp(scope, xT, xkeyf, T, l, cv, tmp, hT, nhalf):
            nt = T // 512
            HC = 32 // nhalf
            ncw = min(512, 4096 // HC)
            assert not wq_plan and not wq_ready
            for half in range(nhalf):
                wq_plan.extend([wview(w1_d[l], half * HC * 128 + s * 512, 512) for s in range(HC // 4)])
                wq_plan.extend([wview(w2_d[l], c * ncw, ncw, k0=half * HC, kc=HC) for c in range(1024 // ncw)])
            wq_top_up()
            for t in range(nt):
                norm_mod(tmp, xT[:, :, t * 512:(t + 1) * 512], xkeyf(t),
                         lambda c, t=t: hT[:, c, t * 512:(t + 1) * 512], (hT, t), l, 1, cv)
            hid = K.tile(scope, f'hid{l}{cv}', [128, HC, T], BF16)
            rl = [K.tile(scope, f'rl{l}{cv}{i}', [128, 512], F32) for i in range(2)]
            for half in range(nhalf):
                def epi1(c, t, ps):
                    r_ = rl[(c + t) % 2]
                    K.op('act', lambda e: e.activation(out=r_[:], in_=ps[:, :], func=AF.Relu), [ps], [r_])
                    K.op('pool', lambda e: e.tensor_tensor(out=hid[:, c, t * 512:(t + 1) * 512], in0=r_[:], in1=r_[:],
                                                           op=ALU.mult), [r_], [(hid, (c, t))])
                linear_fm([wview(w1_d[l], half * HC * 128 + s * 512, 512) for s in range(HC // 4)],
                          lambda kc, t: hT[:, kc, t * 512:(t + 1) * 512], lambda t: [(hT, t)], nt, epi1, planned=True)
                linear_fm([wview(w2_d[l], c * ncw, ncw, k0=half * HC, kc=HC) for c in range(1024 // ncw)],
                          lambda kc, t: hid[:, kc, t * 512:(t + 1) * 512], lambda t: [hid], nt,
                          resid_epi(xT, xkeyf, l, 1, cv), planned=True)

        def final_out(tmp, xT, xkeyf, T, y_d, ostg):
            for t in range(T // 512):
                xs = xT[:, :, t * 512:(t + 1) * 512]
                rstd = norm_stats(tmp, xs, xkeyf(t))
                yn = tmp['yn']
                for c in range(8):
                    K.op('dve', lambda e, c=c: e.scalar_tensor_tensor(
                        out=yn[:, c, :], in0=xs[:, c, :], scalar=SP('gf', c, c + 1), in1=rstd[:, 0:512],
                        op0=ALU.mult, op1=ALU.mult), [xkeyf(t), rstd, spt], [(yn, c)])
                for tt in range(4):
                    og = ostg[tt % 2]
                    for hb in range(2):
                        ps = psA()
                        for j in range(4):
                            c = hb * 4 + j
                            tr(ps[:, j * 128:(j + 1) * 128], yn[:, c, tt * 128:(tt + 1) * 128], [(yn, c)], [ps], j == 3)
                        evac(og[:, hb * 512:(hb + 1) * 512], ps[:, :], [ps], [(og, hb)])
                    r0 = t * 512 + tt * 128
                    K.dma('sp', y_d[r0:r0 + 128, :], og[:], r=[og], is_out=True)

        def attn_A_unit(tmp, qrhs, qkey, kblocks, g, OT, otok0, esink):
            O, D = psB(), psB()
            Pts = tmp['P']
            nb = len(kblocks)
            Ss = [None] * nb

            def qk(i):
                S = psA()
                mm(S[:, :], kblocks[i][0], qrhs, True, True, [kblocks[i][1], qkey], [S], True)
                Ss[i] = S
            qk(0)
            if nb > 1:
                qk(1)
            for i in range(nb):
                if i + 2 < nb:
                    qk(i + 2)
                Pt = Pts[tmp['pi'] % len(Pts)]
                tmp['pi'] += 1
                S = Ss[i]
                K.op('act', lambda e, S=S, Pt=Pt: e.activation(out=Pt[:], in_=S[:, :], func=AF.Exp, scale=SCALE), [S], [Pt])
                mi = kblocks[i][4]
                if mi is not None:
                    K.op('dve', lambda e, Pt=Pt, mi=mi: e.tensor_tensor(out=Pt[:], in0=Pt[:], in1=masks[:, mi, :], op=ALU.mult),
                         [Pt, masks], [Pt])
                mm(O[:, :], kblocks[i][2], Pt[:], i == 0, i == nb - 1, [kblocks[i][3], Pt], [O], i == nb - 1)
                mm(D[:, :], ones[:], Pt[:], i == 0, i == nb - 1, [ones, Pt], [D], i == nb - 1)
            rd = tmp['rd']
            for j in range(4):
                h = g * 4 + j
                K.op('act', lambda e, j=j, h=h: e.activation(out=rd[:, j * 128:(j + 1) * 128], in_=D[:, j * 128:(j + 1) * 128],
                                                             func=AF.Ln, bias=esink[:, h:h + 1], scale=1.0),
                     [D, cons], [(rd, j)])
            K.op('act', lambda e: e.activation(out=rd[:], in_=rd[:], func=AF.Exp, scale=-1.0), [rd], [rd])
            for j in range(4):
                p0 = (j % 2) * 64
                ch = g * 2 + j // 2
                K.op('dve', lambda e, j=j, p0=p0, ch=ch: e.tensor_tensor(
                    out=OT[p0:p0 + 64, ch, otok0:otok0 + 128], in0=O[p0:p0 + 64, j * 128:(j + 1) * 128],
                    in1=rd[p0:p0 + 64, j * 128:(j + 1) * 128], op=ALU.mult), [O, rd], [(OT, (ch, p0, otok0))])

        def attn_B_unit(tmp, q1, q2, qkey, kblocks, OT, och, otok0, neglam, subw):
            O, D = psB(), psB()
            Pts = tmp['P']
            nb = len(kblocks)
            Ss = [None] * nb

            def qk(i):
                S = psA()
                mm(S[:, 0:256], kblocks[i][0], q1, True, True, [kblocks[i][2], qkey], [S], False)
                mm(S[:, 256:512], kblocks[i][1], q2, True, True, [kblocks[i][2], qkey], [S], True)
                Ss[i] = S
            qk(0)
            if nb > 1:
                qk(1)
            for i in range(nb):
                if i + 2 < nb:
                    qk(i + 2)
                Pt = Pts[tmp['pi'] % len(Pts)]
                tmp['pi'] += 1
                S = Ss[i]
                K.op('act', lambda e, S=S, Pt=Pt: e.activation(out=Pt[:], in_=S[:, :], func=AF.Exp, scale=SCALE), [S], [Pt])
                mm(O[:, :], kblocks[i][3], Pt[:], i == 0, i == nb - 1, [kblocks[i][4], Pt], [O], i == nb - 1)
                mm(D[:, :], ones[:], Pt[:], i == 0, i == nb - 1, [ones, Pt], [D], i == nb - 1)
            rd, ob, sqb, rs = tmp['rd'], tmp['ob'], tmp['sqb'], tmp['rs']
            K.op('act', lambda e: e.activation(out=rd[:], in_=D[:, :], func=AF.Ln), [D], [rd])
            K.op('act', lambda e: e.activation(out=rd[:], in_=rd[:], func=AF.Exp, scale=-1.0), [rd], [rd])
            K.op('dve', lambda e: e.tensor_tensor(out=rd[:], in0=O[:, :], in1=rd[:], op=ALU.mult), [O, rd], [rd])
            K.op('dve', lambda e: e.scalar_tensor_tensor(out=ob[:], in0=rd[:, 256:512], scalar=neglam, in1=rd[:, 0:256],
                                                         op0=ALU.mult, op1=ALU.add), [rd, cons], [ob])
            K.op('act', lambda e: e.activation(out=sqb[:], in_=ob[:], func=AF.Square), [ob], [sqb])
            ps = psA()
            mm(ps[:, 0:256], ones[:], sqb[:], True, True, [ones, sqb], [ps], True)
            K.op('act', lambda e: e.activation(out=rs[:], in_=ps[:, 0:256], func=AF.Ln, bias=cons[:, 20:21], scale=1.0 / 128), [ps, cons], [rs])
            K.op('act', lambda e: e.activation(out=rs[:], in_=rs[:], func=AF.Exp, scale=-0.5), [rs], [rs])
            K.op('dve', lambda e: e.scalar_tensor_tensor(out=OT[:, och, otok0:otok0 + 256], in0=ob[:], scalar=subw, in1=rs[:],
                                                         op0=ALU.mult, op1=ALU.mult), [ob, rs, cons], [(OT, (och, 0, otok0))])

        lq = SP('lamqk')
        K.op('dve', lambda e: e.memset(cons[:], 0.0), [], [cons])
        K.op('dve', lambda e: e.memset(cons[:, 20:21], EPS), [cons], [cons])
        prod = K.tile(es, 'lprod', [128, 128], F32)
        K.op('dve', lambda e: e.tensor_tensor(out=prod[:].rearrange("p (a d) -> p a d", a=2),
                                              in0=lq.rearrange("p (a b d) -> p a b d", a=2, b=2)[:, :, 0, :],
                                              in1=lq.rearrange("p (a b d) -> p a b d", a=2, b=2)[:, :, 1, :], op=ALU.mult),
             [spt], [prod])
        K.op('dve', lambda e: e.reduce_sum(out=cons[:, 16:18], in_=prod[:].rearrange("p (a d) -> p a d", a=2),
                                           axis=mybir.AxisListType.X), [prod, cons], [cons])
        K.op('act', lambda e: e.activation(out=cons[:, 18:20], in_=cons[:, 16:18], func=AF.Exp), [cons], [cons])
        K.op('act', lambda e: e.activation(out=cons[:, 0:8], in_=SP('sink'), func=AF.Exp), [spt, cons], [cons])
        K.op('dve', lambda e: e.tensor_tensor(out=cons[:, 8:9], in0=cons[:, 19:20], in1=cons[:, 18:19], op=ALU.subtract), [cons], [cons])
        K.op('dve', lambda e: e.tensor_scalar(out=cons[:, 8:9], in0=cons[:, 8:9], scalar1=-LAM_INIT, scalar2=None, op0=ALU.add), [cons], [cons])
        K.op('dve', lambda e: e.tensor_scalar(out=cons[:, 9:10], in0=SP('subln'), scalar1=1.0 - LAM_INIT, scalar2=None, op0=ALU.mult),
             [spt, cons], [cons])
        esink, neglam, subw = cons, cons[:, 8:9], cons[:, 9:10]

        def exchange(scope, nm, data_ap, data_keys, W, xsrc, xdst, ksrc, kdst, out_ap, out_key):
            cb = K.tile(scope, 'cb' + nm, [128, 128], F32)
            hg = K.tile(scope, 'hg' + nm, [128, 128], F32)
            K.op('dve', lambda e: e.memset(cb[:], 0.0), [], [cb])
            K.op('dve', lambda e: e.tensor_scalar(out=cb[:, 0:W], in0=data_ap, scalar1=SP('sel', 1, 2), scalar2=None, op0=ALU.mult),
                 list(data_keys) + [spt, cb], [cb])
            K.op('dve', lambda e: e.tensor_scalar(out=cb[:, 64:64 + W], in0=data_ap, scalar1=SP('sel', 0, 1), scalar2=None, op0=ALU.mult),
                 list(data_keys) + [spt, cb], [cb])
            K.dma('sp', xsrc, cb[:], r=[cb], w=[ksrc])
            if NO_CC:
                K.dma('pool', xdst, xsrc, r=[ksrc], w=[kdst])
            else:
                K.cc(lambda e: e.collective_compute("AllReduce", ALU.add, replica_groups=PAIRS, ins=[xsrc.opt()], outs=[xdst.opt()]),
                     r=[ksrc], w=[kdst])
            K.dma('sp', hg[:], xdst, r=[kdst], w=[hg])
            K.op('dve', lambda e: e.tensor_scalar(out=out_ap, in0=hg[:, 0:W], scalar1=SP('sel', 0, 1), scalar2=None, op0=ALU.mult),
                 [hg, spt], [out_key])
            K.op('dve', lambda e: e.scalar_tensor_tensor(out=out_ap, in0=hg[:, 64:64 + W], scalar=SP('sel', 1, 2), in1=out_ap,
                                                         op0=ALU.mult, op1=ALU.add), [hg, spt, out_key], [out_key])

        def rec_layer(scope, xT, xkeyf, T, hT, cv, segL, phases, sfx, rwa_d, rwx_d, tmp=None, exch=None, state_out=None):
            nseg = T // segL
            nt = T // 512
            PL = 512
            npc = T // PL
            xrp2 = [K.tile(scope, f'xrp{i}' + sfx, [128, nseg, segL + 4], F32) for i in range(2)]
            xc2 = [K.tile(scope, f'xc{i}' + sfx, [128, 512], F32) for i in range(3)]
            xcb2 = [K.tile(scope, f'xcb{i}' + sfx, [128, 512], BF16) for i in range(3)]
            gb = K.tile(scope, 'gb' + sfx, [128, T], BF16)
            bA2 = [K.tile(scope, f'bA{i}' + sfx, [128, PL], F32) for i in range(2)]
            bR2 = [K.tile(scope, f'bR{i}' + sfx, [128, PL], F32) for i in range(2)]
            bI2 = [K.tile(scope, f'bI{i}' + sfx, [128, PL], F32) for i in range(2)]
            y = K.tile(scope, 'y' + sfx, [128, 10, T], BF16)
            wg2 = [K.tile(scope, f'wg{i}' + sfx, [128, 4, 128], BF16) for i in range(3)]
            carry = K.tile(scope, 'carry' + sfx, [128, 4], F32)
            cl = K.tile(scope, 'cl' + sfx, [128, 2, 2, 10], F32)
            pfx = '_p' if sfx == 'p' else '_s'
            lam = SP('lam' + pfx).rearrange("p (d c) -> p d c", d=2)
            K.op('act', lambda e: e.activation(out=cl[:, :, 0, :], in_=lam, func=AF.Exp, scale=-1.0), [spt], [cl])
            K.op('act', lambda e: e.activation(out=cl[:, :, 0, :], in_=cl[:, :, 0, :], func=AF.Ln, bias=1.0, scale=1.0), [cl], [cl])
            K.op('dve', lambda e: e.tensor_scalar(out=cl[:, :, 1, :], in0=cl[:, :, 0, :], scalar1=-16.0, scalar2=None, op0=ALU.mult), [cl], [cl])
            K.op('dve', lambda e: e.tensor_scalar(out=cl[:, :, 0, :], in0=cl[:, :, 0, :], scalar1=-8.0, scalar2=None, op0=ALU.mult), [cl], [cl])
            for xr_ in xrp2:
                K.op('pool', lambda e, xr_=xr_: e.memset(xr_[:], 0.0), [], [xr_])
            cw = SP('convw' + pfx)
            halo = None
            if exch is not None:
                hl = K.tile(scope, 'hl', [128, 10, 2], F32)
                halo = K.tile(scope, 'halo', [128, 10, 2], F32)
                for n in range(10):
                    sl, (wv,) = wload([wview(rwin_d, 1280 + n * 128, 128)])
                    ps = psA()
                    for kc in range(8):
                        mm(ps[:, 0:2], wv[:, kc, :], hT[:, kc, T - 2:T], kc == 0, kc == 7, [sl, (hT, nt - 1)], [ps], kc == 7)
                    evac(hl[:, n, :], ps[:, 0:2], [ps], [(hl, n)], eng='dve')
                exchange(scope, 'x1', hl[:].rearrange("p a b -> p (a b)"), [hl], 20, xsrc1, xdst1, exch['xsrc1'], exch['xdst1'],
                         halo[:].rearrange("p a b -> p (a b)"), halo)
            gcount = {'i': 0}
            for (dirs, final) in phases:
                if exch is not None and final:
                    exchange(scope, 'x2', exch['stS'][:], [exch['stS']], 10, xsrc2, xdst2, exch['xsrc2'], exch['xdst2'],
                             exch['sdn'][:], exch['sdn'])
                pend = {}

                wq = {}

                def wfetch(n, dirs=dirs, final=final):
                    parts = [wview(rwin_d, 1280 + n * 128, 128)]
                    if final:
                        parts.append(wview(rwin_d, n * 128, 128))
                    sl, wvs = wload(parts)
                    slk = [(sl, j) for j in range(len(parts))] if len(parts) > 1 else [sl]
                    wg = wg2[n % 3]
                    gparts = []
                    for d in dirs:
                        gparts += [rwa_d[d, n].rearrange("c (o d) -> c o d", o=1), rwx_d[d, n].rearrange("c (o d) -> c o d", o=1)]
                    for j, gp in enumerate(gparts):
                        K.dma('pool', wg[:, j:j + 1, :], gp, w=[(wg, j)])
                    wq[n] = (sl, wvs, slk)

                def inproj(n, dirs=dirs, final=final):
                    xrp = xrp2[n % 2]
                    sl, wvs, slk = wq.pop(n)
                    wg = wg2[n % 3]
                    for t in range(nt):
                        ps = psA()
                        for kc in range(8):
                            mm(ps[:, :], wvs[0][:, kc, :], hT[:, kc, t * 512:(t + 1) * 512], kc == 0, kc == 7,
                               [slk[0], (hT, t)], [ps], kc == 7)
                        if segL >= 512:
                            sgi, o = (t * 512) // segL, (t * 512) % segL
                            evac(xrp[:, sgi, 2 + o:2 + o + 512], ps[:, :], [ps], [(xrp, t)], eng='act')
                        else:
                            ns = 512 // segL
                            evac(xrp[:, t * ns:(t + 1) * ns, 2:2 + segL], ps[:, :].rearrange("p (s l) -> p s l", s=ns), [ps], [(xrp, t)], eng='act')
                    if halo is not None:
                        K.op('dve', lambda e, n=n: e.tensor_copy(out=xrp[:, 0, 2 + segL:4 + segL], in_=halo[:, n, :][:, ::-1]), [halo], [(xrp, 'halo')])
                    pend[n] = (sl, wvs, slk)

                def gateproj(n):
                    sl, wvs, slk = pend[n]
                    for t in range(nt):
                        ps2 = psA()
                        for kc in range(8):
                            mm(ps2[:, :], wvs[1][:, kc, :], hT[:, kc, t * 512:(t + 1) * 512], kc == 0, kc == 7,
                               [slk[1], (hT, t)], [ps2], kc == 7)
                        K.op('act', lambda e, t=t, ps2=ps2: e.activation(out=gb[:, t * 512:(t + 1) * 512], in_=ps2[:, :],
                                                                         func=AF.Gelu_apprx_tanh), [ps2], [(gb, t)])

                def stageA(n, pc, k):
                    xrp = xrp2[n % 2]
                    xcp, xbp = xc2[k % 3], xcb2[k % 3]
                    c0 = pc * PL
                    if segL >= PL:
                        segs = [(c0 // segL, c0 % segL, PL, 0)]
                    else:
                        segs = [(c0 // segL + s_, 0, segL, s_ * segL) for s_ in range(PL // segL)]
                    for (sgi, o, L_, bo) in segs:
                        o_ = xcp[:, bo:bo + L_]
                        K.op('dve', lambda e, sgi=sgi, o=o, L_=L_, o_=o_: e.tensor_scalar(
                            out=o_, in0=xrp[:, sgi, o:o + L_], scalar1=cw[:, n:n + 1], scalar2=SP('convb', n, n + 1),
                            op0=ALU.mult, op1=ALU.add), [xrp, spt], [xcp])
                        for k_ in range(1, 5):
                            K.op('dve', lambda e, sgi=sgi, o=o, L_=L_, o_=o_, k_=k_: e.scalar_tensor_tensor(
                                out=o_, in0=xrp[:, sgi, o + k_:o + k_ + L_], scalar=cw[:, k_ * 10 + n:k_ * 10 + n + 1], in1=o_,
                                op0=ALU.mult, op1=ALU.add), [xrp, spt, xcp], [xcp])
                    K.op('pool', lambda e: e.tensor_copy(out=xbp[:], in_=xcp[:]), [xcp], [xbp])

                def stageB(n, pc, k, dirs=dirs, final=final):
                    wg = wg2[n % 3]
                    xcp, xbp = xc2[k % 3], xcb2[k % 3]
                    c0 = pc * PL
                    for di, d in enumerate(dirs):
                        up = (d == 0)
                        gi = gcount['i'] % 2
                        gcount['i'] += 1
                        bA, bR, bI = bA2[gi], bR2[gi], bI2[gi]
                        psr, psi = psA(), psA()
                        mm(psr[:, :], wg[:, 2 * di, :], xbp[:], True, True, [(wg, 2 * di), xbp], [psr], True)
                        mm(psi[:, :], wg[:, 2 * di + 1, :], xbp[:], True, True, [(wg, 2 * di + 1), xbp], [psi], True)
                        K.op('act', lambda e, psr=psr, bR=bR, d=d: e.activation(
                            out=bR[:], in_=psr[:, :], func=AF.Sigmoid, bias=SP('ba' + pfx, d * 10 + n, d * 10 + n + 1), scale=1.0),
                            [psr, spt], [bR])
                        K.op('act', lambda e, psi=psi, bI=bI, d=d: e.activation(
                            out=bI[:], in_=psi[:, :], func=AF.Sigmoid, bias=SP('bx' + pfx, d * 10 + n, d * 10 + n + 1), scale=1.0),
                            [psi, spt], [bI])
                        K.op('act', lambda e, d=d, bA=bA, bR=bR: e.activation(out=bA[:], in_=bR[:], func=AF.Exp, scale=cl[:, d, 0, n:n + 1]), [bR, cl], [bA])
                        K.op('act', lambda e, d=d, bR=bR: e.activation(out=bR[:], in_=bR[:], func=AF.Exp, scale=cl[:, d, 1, n:n + 1]), [bR, cl], [bR])
                        K.op('pool', lambda e, bI=bI: e.tensor_tensor(out=bI[:], in0=bI[:], in1=xcp[:], op=ALU.mult), [bI, xcp], [bI])
                        K.op('act', lambda e, bR=bR: e.activation(out=bR[:], in_=bR[:], func=AF.Ln, bias=1.0, scale=-1.0), [bR], [bR])
                        K.op('act', lambda e, bR=bR: e.activation(out=bR[:], in_=bR[:], func=AF.Exp, scale=0.5), [bR], [bR])
                        K.op('pool', lambda e, bI=bI, bR=bR: e.tensor_tensor(out=bI[:], in0=bI[:], in1=bR[:], op=ALU.mult), [bI, bR], [bI])
                        nsc = PL // segL if segL < PL else 1
                        L = PL // nsc
                        seg_first = (c0 % segL == 0) if up else ((c0 + PL) % segL == 0)
                        for sc_ in range(nsc):
                            lo = sc_ * L
                            if segL <= PL or (seg_first and exch is None):
                                init, ik = 0.0, []
                            elif seg_first:
                                init = (SP('state_up', n, n + 1) if up else exch['sdn'][:, n:n + 1])
                                ik = [spt if up else exch['sdn']]
                            else:
                                init, ik = carry[:, 0:1], [carry]
                            if up:
                                K.op('dve', lambda e, lo=lo, L=L, init=init, bA=bA, bR=bR, bI=bI: e.tensor_tensor_scan(
                                    out=bR[:, lo:lo + L], data0=bA[:, lo:lo + L], data1=bI[:, lo:lo + L], initial=init,
                                    op0=ALU.mult, op1=ALU.add), [bA, bI] + ik, [bR])
                            else:
                                K.op('dve', lambda e, lo=lo, L=L, init=init, bA=bA, bR=bR, bI=bI: e.tensor_tensor_scan(
                                    out=bR[:, lo:lo + L][:, ::-1], data0=bA[:, lo:lo + L][:, ::-1], data1=bI[:, lo:lo + L][:, ::-1],
                                    initial=init, op0=ALU.mult, op1=ALU.add), [bA, bI] + ik, [bR])
                        last_col = PL - 1 if up else 0
                        if segL > PL:
                            K.op('dve', lambda e, last_col=last_col, bR=bR: e.tensor_copy(out=carry[:, 0:1], in_=bR[:, last_col:last_col + 1]), [bR], [carry])
                            if exch is not None and up and pc == npc - 1:
                                K.op('dve', lambda e, last_col=last_col, bR=bR: e.tensor_copy(out=exch['stS'][:, n:n + 1], in_=bR[:, last_col:last_col + 1]),
                                     [bR], [exch['stS']])
                        elif state_out is not None:
                            lc = segL - 1 if up else 0
                            ns_ = PL // segL
                            s0 = c0 // segL
                            K.op('dve', lambda e, d=d, lc=lc, bR=bR, s0=s0, ns_=ns_: e.tensor_copy(
                                out=state_out[:, d, s0:s0 + ns_, n], in_=bR[:].rearrange("p (s l) -> p s l", l=segL)[:, :, lc]), [bR], [(state_out, (d, n, s0))])
                        yv = y[:, n, c0:c0 + PL]
                        first = (di == 0 and not (final and len(dirs) == 1))
                        lastd = final and di == len(dirs) - 1
                        if first and not lastd:
                            K.op('pool', lambda e, yv=yv, bR=bR: e.tensor_copy(out=yv, in_=bR[:]), [bR], [(y, (n, pc))])
                        else:
                            if not first:
                                K.op('pool', lambda e, yv=yv, bR=bR: e.tensor_tensor(out=bR[:], in0=bR[:], in1=yv, op=ALU.add), [bR, (y, (n, pc))], [bR])
                            if lastd:
                                K.op('dve', lambda e, yv=yv, c0=c0, bR=bR: e.tensor_tensor(out=yv, in0=bR[:], in1=gb[:, c0:c0 + PL], op=ALU.mult),
                                     [bR, gb], [(y, (n, pc))])
                            else:
                                K.op('pool', lambda e, yv=yv, bR=bR: e.tensor_copy(out=yv, in_=bR[:]), [bR], [(y, (n, pc))])

                items = []
                for n in range(10):
                    porder = list(range(npc)) if dirs[0] == 0 else list(range(npc - 1, -1, -1))
                    items += [(n, pc) for pc in porder]
                wfetch(0)
                wfetch(1)
                inproj(0)
                stageA(items[0][0], items[0][1], 0)
                stageA(items[1][0], items[1][1], 1)
                for i, (n, pc) in enumerate(items):
                    if i % npc == 0:
                        if n + 1 < 10:
                            inproj(n + 1)
                        if n + 2 < 10:
                            wfetch(n + 2)
                        if final:
                            gateproj(n)
                    if i + 2 < len(items):
                        stageA(items[i + 2][0], items[i + 2][1], i + 2)
                    stageB(n, pc, i)
            l = 1
            linear_fm([wview(rwout_d, s * 128, 128) for s in range(8)], lambda kc, t: y[:, kc, t * 512:(t + 1) * 512],
                      lambda t: [y], nt, resid_epi(xT, xkeyf, l, 0, cv))

        try:
            stage_check(1)
            with ExitStack() as P:
                xT = K.tile(P, 'xTp', [128, 8, TP], F32)
                hT = K.tile(P, 'hTp', [128, 8, TP], BF16)
                xk = lambda t: (xT, ('x', t))
                with ExitStack() as sc:
                    alloc_slabs(sc, 4)
                    tmp = dict(sq=K.tile(sc, 'sq', [128, 8, 512], BF16), rstd=K.tile(sc, 'rstd', [128, 512], F32),
                               u=[K.tile(sc, f'u{i}', [128, 512], F32) for i in range(2)],
                               P=[K.tile(sc, f'P{i}', [128, 512], BF16) for i in range(4)], pi=0,
                               rd=K.tile(sc, 'rd', [128, 512], F32), ob=K.tile(sc, 'ob', [128, 256], F32),
                               sqb=K.tile(sc, 'sqb', [128, 256], BF16), rs=K.tile(sc, 'rs', [128, 256], F32))
                    stg = dict(stg=[K.tile(sc, f'stg{i}', [128, 1024], F32) for i in range(2)])
                    QK = K.tile(sc, 'QKp', [128, 9, TP], BF16)
                    VAt = K.tile(sc, 'VAp', [128, 8, 2, 128], BF16)
                    VBt = K.tile(sc, 'VBp', [128, 8, 512], BF16)
                    OT = K.tile(sc, 'OTp', [128, 8, TP], BF16)
                    kvs = [K.tile(sc, 'kvs0', [128, 8, 512], F32)] * 2
                    load_xT(stg, xp_d, 0, 8, xT, 0, 'x')
                    stage_check(1.1)
                    for t in range(2):
                        norm_mod(tmp, xT[:, :, t * 512:(t + 1) * 512], xk(t), lambda c, t=t: hT[:, c, t * 512:(t + 1) * 512], (hT, t), 0, 0, 0)
                    QBz = K.tile(sc, 'QBzp', [128, 4, 2, TP], BF16)
                    K.op('pool', lambda e: e.memset(QBz[:], 0.0), [], [QBz])

                    def epi_qk(c, t, ps):
                        if 5 <= c <= 8:
                            for hh in range(2):
                                evac(QBz[hh * 64:(hh + 1) * 64, c - 5, hh, t * 512:(t + 1) * 512], ps[hh * 64:(hh + 1) * 64, :], [ps], [(QBz, (c, t, hh))])
                        else:
                            cq = c if c < 5 else c - 4
                            evac(QK[:, cq, t * 512:(t + 1) * 512], ps[:, :], [ps], [(QK, (cq, t))])
                    stage_check(1.2)
                    linear_fm([wview(winx_d, QA, 512), wview(winx_d, KA, 128), wview(winx_d, QB, 512), wview(winx_d, KB, 512)],
                              lambda kc, t: hT[:, kc, t * 512:(t + 1) * 512], lambda t: [(hT, t)], 2, epi_qk)
                    stage_check(1.3)
                    assert not wq_plan and not wq_ready
                    wq_plan.extend([wview(win_d, c0_, n_) for (c0_, n_) in [(512, 256), (1280, 512), (1792, 512)]])
                    for si, (c0, n, o0) in enumerate([(512, 256, 0), (1280, 512, 256), (1792, 512, 768)]):
                        sl, (wv,) = wq_get()
                        for tt in range(8):
                            ps = psA()
                            for kc in range(8):
                                mm(ps[:, 0:n], hT[:, kc, tt * 128:(tt + 1) * 128], wv[:, kc, :], kc == 0, kc == 7, [sl, (hT, tt // 4)], [ps], kc == 7)
                            kv = kvs[si % 2]
                            evac(kv[:, tt, 0:n], ps[:, 0:n], [ps], [(kv, tt)])
                            if si == 0:
                                for u_ in range(2):
                                    K.op('dve', lambda e, tt=tt, ps=ps, u_=u_: e.tensor_copy(
                                        out=VAt[:, tt].rearrange("p g (u d) -> p g u d", u=2)[:, :, u_, :],
                                        in_=ps[:, 128:256].rearrange("p (g d) -> p g d", g=2)), [ps], [(VAt, (tt, u_))])
                            if si == 2:
                                evac(VBt[:, tt, :], ps[:, 0:512], [ps], [(VBt, tt)])
                        kv = kvs[si % 2]
                        for (d_, lo, w_) in ([(nak_d, 0, 128), (nav_d, 128, 128)] if si == 0 else [((nbk_d, nbv_d)[si - 1], 0, 512)]):
                            if os.environ.get('KNOOUT') == '1':
                                continue
                            if os.environ.get('KNOOUT') == '2':
                                for t8 in range(8):
                                    K.dma('sp', d_[t8 * 128:(t8 + 1) * 128, :], kv[:, t8, lo:lo + w_], r=[kv], is_out=True)
                                continue
                            K.dma('sp', d_.rearrange("(t p) c -> p t c", p=128), kv[:, :, lo:lo + w_], r=[kv], is_out=True)
                    stage_check(1.4)
                    wq_plan.extend([wview(wout_d, s * 512, 512) for s in range(2)])
                    wq_top_up()
                    for sq_ in range(4):
                        for g in range(2):
                            for qb in range(2):
                                q0 = sq_ * 256 + qb * 128
                                kbl = []
                                for kb in range(2):
                                    k0 = sq_ * 256 + kb * 128
                                    kbl.append((QK[g * 64:(g + 1) * 64, 4, k0:k0 + 128], QK, VAt[:, sq_ * 2 + kb, g, :], VAt, None))
                                attn_A_unit(tmp, QK[g * 64:(g + 1) * 64, 0:4, q0:q0 + 128], QK, kbl, g, OT, q0, esink)
                        for hb in range(4 if STAGE >= 1.5 else 0):
                            q0 = sq_ * 256
                            kbl = []
                            for kb in range(2):
                                k0 = sq_ * 256 + kb * 128
                                kbl.append((QK[:, 5 + hb, k0:k0 + 128], QK[:, 5 + hb, k0:k0 + 128], QK,
                                            VBt[:, sq_ * 2 + kb, hb * 128:(hb + 1) * 128], VBt))
                            attn_B_unit(tmp, QBz[:, hb, 0, q0:q0 + 256], QBz[:, hb, 1, q0:q0 + 256], QBz, kbl, OT, 4 + hb, q0, neglam, subw)
                    stage_check(1.6)
                    linear_fm([wview(wout_d, s * 512, 512) for s in range(2)], lambda kc, t: OT[:, kc, t * 512:(t + 1) * 512],
                              lambda t: [OT], 2, resid_epi(xT, xk, 0, 0, 0), planned=True)
                    K.barrier()
                with ExitStack() as sc:
                    alloc_slabs(sc, 4)
                    tmp = dict(sq=K.tile(sc, 'sq2', [128, 8, 512], BF16), rstd=K.tile(sc, 'rstd2', [128, 512], F32),
                               u=[K.tile(sc, f'u2{i}', [128, 512], F32) for i in range(2)])
                    stage_check(2)
                    mlp(sc, xT, xk, TP, 0, 0, tmp, hT, 1)
                    K.barrier()
                with ExitStack() as sc:
                    alloc_slabs(sc, 3)
                    tmp = dict(sq=K.tile(sc, 'sq3', [128, 8, 512], BF16), rstd=K.tile(sc, 'rstd3', [128, 512], F32),
                               u=[K.tile(sc, f'u3{i}', [128, 512], F32) for i in range(2)])
                    for t in range(2):
                        norm_mod(tmp, xT[:, :, t * 512:(t + 1) * 512], xk(t), lambda c, t=t: hT[:, c, t * 512:(t + 1) * 512], (hT, t), 1, 0, 0)
                    stage_check(3)
                    stp = K.tile(sc, 'stp', [128, 2, 4, 10], F32)
                    rec_layer(sc, xT, xk, TP, hT, 0, 256, [([0, 1], True)], 'p', rwa_p_d, rwx_p_d, tmp, state_out=stp)
                    ps = psA()
                    tr(ps[0:80, 0:128], stp[:].rearrange("p d s n -> p (d s n)"), [stp], [ps], True)
                    sto = K.tile(sc, 'sto', [80, 128], F32)
                    evac(sto[:], ps[0:80, 0:128], [ps], [sto], eng='dve')
                    for d, dd in enumerate([nsf_d, nsb_d]):
                        for s_ in range(4):
                            K.dma('sp', dd[s_].rearrange("(n p) -> n p", p=128), sto[d * 40 + s_ * 10:d * 40 + s_ * 10 + 10, :], r=[sto], is_out=True)
                    K.barrier()
                with ExitStack() as sc:
                    alloc_slabs(sc, 4)
                    tmp = dict(sq=K.tile(sc, 'sq4', [128, 8, 512], BF16), rstd=K.tile(sc, 'rstd4', [128, 512], F32),
                               u=[K.tile(sc, f'u4{i}', [128, 512], F32) for i in range(2)], yn=K.tile(sc, 'yn4', [128, 8, 512], F32))
                    stage_check(4)
                    mlp(sc, xT, xk, TP, 1, 0, tmp, hT, 1)
                    ostg = [K.tile(sc, f'ostg{i}', [128, 1024], F32) for i in range(2)]
                    final_out(tmp, xT, xk, TP, yp_d, ostg)
                    K.barrier()

            class TlV(Tl):
                def __init__(s, base_ap, name):
                    s.t = None
                    s.base = base_ap
                    s.name = name
                    s.subs = {}

                def __getitem__(s, idx):
                    return s.base[idx]

            with ExitStack() as S:
                stage_check(5)
                X64 = K.tile(S, 'X64', [128, 8, 4096], BF16)
                hT = X64
                with ExitStack() as S1:
                    OT = K.tile(S1, 'OTs', [128, 8, TS], BF16)
                    with ExitStack() as sc:
                        tmp = dict(sq=K.tile(sc, 'sq5', [128, 8, 512], BF16), rstd=K.tile(sc, 'rstd5', [128, 512], F32),
                                   u=[K.tile(sc, f'u5{i}', [128, 512], F32) for i in range(2)])
                        stg = dict(stg=[K.tile(sc, f'stg5{i}', [128, 1024], F32) for i in range(2)])
                        xt2 = [K.tile(sc, f'xt5{i}', [128, 8, 512], F32) for i in range(2)]
                        for t in range(8):
                            xt = xt2[t % 2]
                            load_xT(stg, xso_d if t < 4 else xst_d, (t % 4) * 512, 4, xt, 0, 'x')
                            norm_mod(tmp, xt[:, :, :], xt, lambda c, t=t: hT[:, c, t * 512:(t + 1) * 512], (hT, t), 0, 0, 1)
                        K.barrier()
                    hk = lambda t: [(hT, t)]
                    hs = lambda kc, t: hT[:, kc, t * 512:(t + 1) * 512]
                    with ExitStack() as sc:
                        rp = dict(cs=[K.tile(sc, f'cs{i}', [128, 2, 512], F32) for i in range(2)],
                                  t1=[K.tile(sc, f't1{i}', [128, 512], F32) for i in range(2)],
                                  t2=[K.tile(sc, f't2{i}', [128, 512], F32) for i in range(2)])
                        tmp = dict(P=[K.tile(sc, f'P5{i}', [128, 512], BF16) for i in range(4)], pi=0,
                                   rd=K.tile(sc, 'rd5', [128, 512], F32), ob=K.tile(sc, 'ob5', [128, 256], F32),
                                   sqb=K.tile(sc, 'sqb5', [128, 256], BF16), rs=K.tile(sc, 'rs5', [128, 256], F32))
                        cstg = K.tile(sc, 'cstg', [128, 2, 512], F32)
                        ri = {'i': 0}

                        def rope_proj(src_w, src_wr, ncol, tiles, dst, pre=None):
                            if pre is None:
                                sl, (wv,) = wload([src_w])
                                slr, (wvr,) = wload([src_wr])
                            else:
                                (sl, (wv,)), (slr, (wvr,)) = pre
                            for j in range(ncol // 128):
                                for t in tiles:
                                    i = ri['i'] % 2
                                    ri['i'] += 1
                                    cs, t1, t2 = rp['cs'][i], rp['t1'][i], rp['t2'][i]
                                    K.dma('sp', cs[:, 0, :], cos_d[:, t * 512:(t + 1) * 512], w=[(cs, 0)])
                                    K.dma('sp', cs[:, 1, :], sin_d[:, t * 512:(t + 1) * 512], w=[(cs, 1)])
                                    ps, psr = psA(), psA()
                                    for kc in range(8):
                                        mm(ps[:, :], wv[:, kc, j * 128:(j + 1) * 128], hs(kc, t), kc == 0, kc == 7, [sl] + hk(t), [ps], kc == 7)
                                    for kc in range(8):
                                        mm(psr[:, :], wvr[:, kc, j * 128:(j + 1) * 128], hs(kc, t), kc == 0, kc == 7, [slr] + hk(t), [psr], kc == 7)
                                    K.op('dve', lambda e, ps=ps, cs=cs, t1=t1: e.tensor_tensor(out=t1[:], in0=ps[:, :], in1=cs[:, 0, :], op=ALU.mult), [ps, (cs, 0)], [t1])
                                    K.op('dve', lambda e, psr=psr, cs=cs, t2=t2: e.tensor_tensor(out=t2[:], in0=psr[:, :], in1=cs[:, 1, :], op=ALU.mult), [psr, (cs, 1)], [t2])
                                    d_ap, d_key = dst(j, t)
                                    if isinstance(d_ap, tuple):
                                        for hh in range(2):
                                            K.op('pool', lambda e, t1=t1, t2=t2, d_ap=d_ap, hh=hh: e.tensor_tensor(
                                                out=d_ap[hh], in0=t1[hh * 64:(hh + 1) * 64, :], in1=t2[hh * 64:(hh + 1) * 64, :], op=ALU.add), [t1, t2], [(d_key[0], (d_key[1], hh))])
                                    else:
                                        K.op('pool', lambda e, t1=t1, t2=t2, d_ap=d_ap: e.tensor_tensor(out=d_ap, in0=t1[:], in1=t2[:], op=ALU.add), [t1, t2], [d_key])

                        def ctx_kT(c_d, col0, dstK, dkey):
                            ps = psA()
                            for b in range(2):
                                K.dma('sp', cstg[:, b, 0:128], c_d[b * 128:(b + 1) * 128, col0:col0 + 128], w=[(cstg, b)])
                                tr(ps[:, b * 128:(b + 1) * 128], cstg[:, b, 0:128], [(cstg, b)], [ps], b == 1)
                            evac(dstK, ps[:, 0:256], [ps], [dkey])

                        with ExitStack() as ga:
                            stage_check(6)
                            alloc_slabs(ga, 3)
                            QAs = K.tile(ga, 'QAs', [128, 4, TS], BF16)
                            KAs = K.tile(ga, 'KAs', [128, 4352], BF16)
                            VAs = K.tile(ga, 'VAs', [128, 34, 2, 128], BF16)
                            wqa, wqar = wload([wview(winx_d, QA, 512)]), wload([wview(winx_d, QAR, 512)])
                            slk_, (wkk_,) = wload([wview(winx_d, KA, 256)])
                            rope_proj(None, None, 512, range(4),
                                      lambda j, t: (QAs[:, j, t * 512:(t + 1) * 512], (QAs, (j, t))), pre=(wqa, wqar))
                            wva = wload([wview(winx_d, VA, 128)])
                            rope_proj(None, None, 128, range(8),
                                      lambda j, t: (KAs[:, 256 + t * 512:256 + (t + 1) * 512], (KAs, t)),
                                      pre=((slk_, (wkk_[:, :, 0:128],)), (slk_, (wkk_[:, :, 128:256],))))
                            ctx_kT(cak_d, 0, KAs[:, 0:256], (KAs, 'c'))
                            sl, (wv,) = wva
                            for blk in range(32):
                                ps = psA()
                                for kc in range(8):
                                    mm(ps[:, 0:128], hT[:, kc, blk * 128:(blk + 1) * 128], wv[:, kc, :], kc == 0, kc == 7, [sl, (hT, blk // 4)], [ps], kc == 7)
                                for u_ in range(2):
                                    K.op('dve', lambda e, blk=blk, ps=ps, u_=u_: e.tensor_copy(
                                        out=VAs[:, 2 + blk].rearrange("p g (u d) -> p g u d", u=2)[:, :, u_, :],
                                        in_=ps[:, 0:128].rearrange("p (g d) -> p g d", g=2)), [ps], [(VAs, (2 + blk, u_))])
                            K.dma('sp', cstg[:, :, 0:128], cav_d.rearrange("(b p) c -> p b c", p=128), w=[cstg])
                            for b in range(2):
                                for u_ in range(2):
                                    K.op('dve', lambda e, b=b, u_=u_: e.tensor_copy(
                                        out=VAs[:, b].rearrange("p g (u d) -> p g u d", u=2)[:, :, u_, :],
                                        in_=cstg[:, b, 0:128].rearrange("p (g d) -> p g d", g=2)), [cstg], [(VAs, (b, u_))])
                            for g in range(2):
                                for i in range(16):
                                    kbl = []
                                    blks = [(0, None), (1, None)] + ([(2 + i - 1, 0)] if i > 0 else []) + [(2 + i, None), (2 + i + 1, 1)]
                                    for (b, mi) in blks:
                                        kbl.append((KAs[g * 64:(g + 1) * 64, b * 128:(b + 1) * 128], KAs, VAs[:, b, g, :], VAs, mi))
                                    attn_A_unit(tmp, QAs[g * 64:(g + 1) * 64, 0:4, i * 128:(i + 1) * 128], QAs, kbl, g, OT, i * 128, esink)
                            K.barrier()
                        with ExitStack() as gb_:
                            stage_check(7)
                            QBs = K.tile(gb_, 'QBs', [128, 2, TS], BF16)
                            K.op('pool', lambda e: e.memset(QBs[:], 0.0), [], [QBs])
                            KBs = K.tile(gb_, 'KBs', [128, 4352], BF16)
                            VBs = K.tile(gb_, 'VBs', [128, 34, 128], BF16)
                            alloc_slabs(gb_, 10, 1024)

                            def load_head(hb_):
                                return [wload([wview(winx_d, off_ + hb_ * 128, 128)]) for off_ in (QB, QBR, KB, KBR, VB)]
                            nxt_w = load_head(0)
                            for hb in range(4):
                                cur_w = nxt_w
                                rope_proj(None, None, 128, range(4),
                                          lambda j, t: ((QBs[0:64, 0, t * 512:(t + 1) * 512], QBs[64:128, 1, t * 512:(t + 1) * 512]), (QBs, t)),
                                          pre=(cur_w[0], cur_w[1]))
                                rope_proj(None, None, 128, range(8),
                                          lambda j, t: (KBs[:, 256 + t * 512:256 + (t + 1) * 512], (KBs, t)), pre=(cur_w[2], cur_w[3]))
                                ctx_kT(cbk_d, hb * 128, KBs[:, 0:256], (KBs, 'c'))
                                sl, (wv,) = cur_w[4]
                                for blk in range(32):
                                    ps = psA()
                                    for kc in range(8):
                                        mm(ps[:, 0:128], hT[:, kc, blk * 128:(blk + 1) * 128], wv[:, kc, :], kc == 0, kc == 7, [sl, (hT, blk // 4)], [ps], kc == 7)
                                    evac(VBs[:, 2 + blk, :], ps[:, 0:128], [ps], [(VBs, 2 + blk)])
                                K.dma('sp', cstg[:, :, 0:128], cbv_d.rearrange("(b p) c -> p b c", p=128)[:, :, hb * 128:(hb + 1) * 128], w=[cstg])
                                K.op('dve', lambda e: e.tensor_copy(out=VBs[:, 0:2, :], in_=cstg[:, :, 0:128]), [cstg], [(VBs, 'c')])
                                if hb + 1 < 4:
                                    nxt_w = load_head(hb + 1)
                                for qt in range(8):
                                    q0 = qt * 256
                                    kbl = [(KBs[:, b * 128:(b + 1) * 128], KBs[:, b * 128:(b + 1) * 128], KBs, VBs[:, b, :], VBs) for b in range(34)]
                                    attn_B_unit(tmp, QBs[:, 0, q0:q0 + 256], QBs[:, 1, q0:q0 + 256], QBs, kbl, OT, 4 + hb, q0, neglam, subw)
                            K.barrier()
                        K.barrier()
                    stage_check(8)
                    xT = TlV(X64.t[:].bitcast(F32), 'xTs')
                    xk = lambda t: (xT, ('x', t))
                    with ExitStack() as sc:
                        alloc_slabs(sc, 4)
                        stg = dict(stg=[K.tile(sc, f'stg6{i}', [128, 1024], F32) for i in range(2)])
                        load_xT(stg, xso_d, 0, 16, xT, 0, 'x')
                        linear_fm([wview(wout_d, s * 512, 512) for s in range(2)], lambda kc, t: OT[:, kc, t * 512:(t + 1) * 512],
                                  lambda t: [OT], 4, resid_epi(xT, xk, 0, 0, 1))
                        K.barrier()
                hT2 = K.tile(S, 'hT2', [128, 8, TS], BF16)
                with ExitStack() as sc:
                    alloc_slabs(sc, 4)
                    tmp = dict(sq=K.tile(sc, 'sq7', [128, 8, 512], BF16), rstd=K.tile(sc, 'rstd7', [128, 512], F32),
                               u=[K.tile(sc, f'u7{i}', [128, 512], F32) for i in range(2)])
                    stage_check(9)
                    mlp(sc, xT, xk, TS, 0, 1, tmp, hT2, 4)
                    for t in range(4):
                        norm_mod(tmp, xT[:, :, t * 512:(t + 1) * 512], xk(t), lambda c, t=t: hT2[:, c, t * 512:(t + 1) * 512], (hT2, t), 1, 0, 1)
                    K.barrier()
                with ExitStack() as sc:
                    alloc_slabs(sc, 3, 2048)
                    stage_check(10)
                    stS = K.tile(sc, 'stS', [128, 10], F32)
                    sdn = K.tile(sc, 'sdn', [128, 10], F32)
                    exch = dict(xsrc1=Tl(None, 'xsrc1'), xdst1=Tl(None, 'xdst1'), xsrc2=Tl(None, 'xsrc2'), xdst2=Tl(None, 'xdst2'), stS=stS, sdn=sdn)
                    rec_layer(sc, xT, xk, TS, hT2, 1, TS, [([0], False), ([1], True)], 's', rwa_s_d, rwx_s_d, None, exch=exch)
                    K.barrier()
                with ExitStack() as sc:
                    alloc_slabs(sc, 4)
                    tmp = dict(sq=K.tile(sc, 'sq8', [128, 8, 512], BF16), rstd=K.tile(sc, 'rstd8', [128, 512], F32),
                               u=[K.tile(sc, f'u8{i}', [128, 512], F32) for i in range(2)])
                    stage_check(11)
                    mlp(sc, xT, xk, TS, 1, 1, tmp, hT2, 4)
                    K.barrier()
                with ExitStack() as sc:
                    tmp = dict(sq=K.tile(sc, 'sq9', [128, 8, 512], BF16), rstd=K.tile(sc, 'rstd9', [128, 512], F32),
                               yn=K.tile(sc, 'yn9', [128, 8, 512], F32))
                    ostg = [K.tile(sc, f'ostg9{i}', [128, 1024], F32) for i in range(2)]
                    final_out(tmp, xT, xk, TS, ys_d, ostg)
                    K.barrier()

        except StopBuild:
            pass
        K.finish()
    return nc


_ROT = np.concatenate([np.arange(16, 32), np.arange(0, 16), np.arange(48, 64), np.arange(32, 48)])


def _host_prepare(inp):
    f = np.float32
    g = lambda k: np.asarray(inp[k], dtype=f)
    SPL, NS = sp_layout()
    w_in = g('att_w_in')[0]
    qa_cols = np.concatenate([np.concatenate([np.arange(c * 64, c * 64 + 64), np.arange((4 + c) * 64, (4 + c) * 64 + 64)]) for c in range(4)])

    def rot_cols(cols):
        cols = np.asarray(cols).reshape(-1, 64)
        return cols[:, _ROT].reshape(-1)
    ka_cols = np.arange(512, 640); va_cols = np.arange(640, 768)
    qb_cols = np.arange(768, 1280); kb_cols = np.arange(1280, 1792); vb_cols = np.arange(1792, 2304)
    ext = np.concatenate([qa_cols, rot_cols(qa_cols), ka_cols, rot_cols(ka_cols), qb_cols, rot_cols(qb_cols),
                          kb_cols, rot_cols(kb_cols), va_cols, vb_cols])
    w_in_x = np.ascontiguousarray(w_in[:, ext])
    shared = dict(w_ada=g('w_ada'), w_mlp1=g('w_mlp1'), w_mlp2=g('w_mlp2'), w_in=np.ascontiguousarray(w_in), w_in_x=w_in_x,
                  w_out=np.ascontiguousarray(g('att_w_out')[0]), rec_w_in=np.ascontiguousarray(g('rec_w_in')[0]),
                  rec_w_out=np.ascontiguousarray(g('rec_w_out')[0]),
                  rwa_p=np.ascontiguousarray(g('rec_w_a')[0]), rwx_p=np.ascontiguousarray(g('rec_w_x')[0]),
                  ident=np.eye(128, dtype=f))
    kk = np.arange(128)[:, None]; qq = np.arange(128)[None, :]
    m = np.stack([np.tile((kk >= qq).astype(f), (1, 4)), np.tile((kk <= qq).astype(f), (1, 4))], axis=1)
    shared['masks'] = np.ascontiguousarray(m)
    p = np.arange(128); d = p % 64
    axis = d // 32; ii = d % 16; half = (d % 32) // 16
    inv = (10000.0 ** (-np.arange(16, dtype=np.float64) / 16))
    cw = g('rec_conv_w')[0]
    cw5_nat = np.concatenate([np.zeros((1, 1280), f), cw], 0)
    cw5_ref = np.concatenate([cw[::-1], np.zeros((1, 1280), f)], 0)
    pm = lambda v: np.ascontiguousarray(np.asarray(v, f).reshape(-1, 128).T)
    xp, xs = g('x_prompt'), g('x_sample')
    maps = []
    for i in range(8):
        b, hf = i // 2, i % 2
        loc = np.arange(4096) if hf == 0 else np.arange(4095, -1, -1)
        mp = dict(shared)
        mp['xp'] = np.ascontiguousarray(xp[4 * i:4 * i + 4].reshape(TP, 1024))
        mp['xs_own'] = np.ascontiguousarray(xs[b, loc[:2048]])
        mp['xs_oth'] = np.ascontiguousarray(xs[b, loc[2048:]])
        pos = np.stack([loc // 64, loc % 64], 0).astype(np.float64)
        ang = pos[axis][:, :] * inv[ii][:, None]
        mp['cos'] = np.cos(ang).astype(f)
        mp['sin'] = (np.sin(ang) * np.where(half == 0, -1.0, 1.0)[:, None]).astype(f)
        mp['cak'] = np.ascontiguousarray(g('cache_a_k')[b, 0].reshape(256, 128))
        mp['cav'] = np.ascontiguousarray(g('cache_a_v')[b, 0].reshape(256, 128))
        mp['cbk'] = np.ascontiguousarray(g('cache_b_k')[b, 0].reshape(256, 512))
        mp['cbv'] = np.ascontiguousarray(g('cache_b_v')[b, 0].reshape(256, 512))
        dsel = [0, 1] if hf == 0 else [1, 0]
        mp['rwa_s'] = np.ascontiguousarray(g('rec_w_a')[0][dsel])
        mp['rwx_s'] = np.ascontiguousarray(g('rec_w_x')[0][dsel])
        sp = np.zeros((128, NS), f)

        def put(name, arr):
            o, w = SPL[name]
            arr = np.asarray(arr, f)
            assert arr.shape == (128, w), (name, arr.shape, w)
            sp[:, o:o + w] = arr
        put('g1', np.concatenate([pm(g('norm1')[l]) for l in range(2)], 1))
        put('g2', np.concatenate([pm(g('norm2')[l]) for l in range(2)], 1))
        put('gf', pm(g('final_norm')))
        put('bada', np.concatenate([pm(g('b_ada')[l]) for l in range(2)], 1))
        put('convw_p', np.concatenate([pm(cw5_nat[k]) for k in range(5)], 1))
        put('convw_s', np.concatenate([pm((cw5_nat if hf == 0 else cw5_ref)[k]) for k in range(5)], 1))
        put('convb', pm(g('rec_conv_b')[0]))
        for nm, key in [('ba', 'rec_b_a'), ('bx', 'rec_b_x'), ('lam', 'rec_lam')]:
            v = g(key)[0]
            put(nm + '_p', np.concatenate([pm(v[0]), pm(v[1])], 1))
            put(nm + '_s', np.concatenate([pm(v[dsel[0]]), pm(v[dsel[1]])], 1))
        put('subln', g('att_subln')[0].reshape(128, 1))
        put('state_up', pm((g('state_fwd') if hf == 0 else g('state_bwd'))[b, 0]))
        put('sink', np.tile(g('att_sink')[0][None, :], (128, 1)))
        put('lamqk', np.tile(g('att_lam_qk')[0].reshape(1, 256), (128, 1)))
        put('sel', np.tile(np.array([[0.0, 1.0]] if hf == 0 else [[1.0, 0.0]], f), (128, 1)))
        cT = np.stack([pm(g('c_ctx')), pm(g('c')[b])], 2).reshape(128, 16)
        put('cT', cT)
        mp['sp'] = sp
        maps.append(mp)
    return maps


_CACHE = {}


def kernel(**inputs):
    if 'nc' not in _CACHE:
        _CACHE['nc'] = build_program()
    nc = _CACHE['nc']
    maps = _host_prepare(inputs)
    res = run_bass_kernel_spmd(nc, maps, core_ids=list(range(8)))
    R = res.results
    f = np.float32
    y_prompt = np.concatenate([R[i]['yp'].reshape(4, 256, 1024) for i in range(8)], 0).astype(f)
    y_sample = np.zeros((4, 4096, 1024), f)
    for i in range(8):
        b, hf = i // 2, i % 2
        ys = R[i]['ys']
        if hf == 0:
            y_sample[b, :2048] = ys
        else:
            y_sample[b, 2048:] = ys[::-1]
    nak = np.concatenate([R[i]['nak'].reshape(4, 1, 256, 2, 64) for i in range(8)], 0).astype(f)
    nav = np.concatenate([R[i]['nav'].reshape(4, 1, 256, 2, 64) for i in range(8)], 0).astype(f)
    nbk = np.concatenate([R[i]['nbk'].reshape(4, 1, 256, 4, 128) for i in range(8)], 0).astype(f)
    nbv = np.concatenate([R[i]['nbv'].reshape(4, 1, 256, 4, 128) for i in range(8)], 0).astype(f)
    nsf = np.concatenate([R[i]['nsf'].reshape(4, 1, 1280) for i in range(8)], 0).astype(f)
    nsb = np.concatenate([R[i]['nsb'].reshape(4, 1, 1280) for i in range(8)], 0).astype(f)
    return (y_prompt, y_sample, nak, nav, nbk, nbv, nsf, nsb)
```
